# Optimizing a Trainium2 kernel written in Bass

```python
import jax, jax.numpy as jnp
from jax import lax
import numpy as np

D_MODEL = 1024
BATCH = 8
SEQ = 4096
DEPTH = 1

GRID_W = 64
CTX_LEN = 256
EPS = 1e-6
HG_HEADS = 4
HG_DK = 128
HG_DV = 128
HG_WIDTH = HG_HEADS * HG_DK
CHUNK = 64
FT_GROUPS = 4
FT_GROUP_W = 128
FT_WIDTH = FT_GROUPS * FT_GROUP_W
IN_COLS = 5 * HG_WIDTH + FT_WIDTH + 2 * D_MODEL
SPLIT_POINTS = (HG_WIDTH, 2 * HG_WIDTH, 3 * HG_WIDTH, 4 * HG_WIDTH, 5 * HG_WIDTH,
                5 * HG_WIDTH + FT_WIDTH, 5 * HG_WIDTH + FT_WIDTH + D_MODEL)
PEER_HEADS = 8
PEER_NKEYS = 128
PEER_EXPERTS = PEER_NKEYS * PEER_NKEYS
PEER_DKEY = 256
PEER_TOPK = 16
PEER_TOKEN_BLOCK = 128

kernel_name = "hgrn2_fnet_peer_prefix_dit"


def _rmsnorm(x, g):
    x32 = x.astype(jnp.float32)
    y = x32 * lax.rsqrt(jnp.mean(x32 * x32, axis=-1, keepdims=True) + EPS)
    return y.astype(x.dtype) * g


def _modulate(x, g, shift, scale):
    return _rmsnorm(x, g) * (1.0 + scale) + shift


def _heads(t, n_heads):
    b, l, _ = t.shape
    return t.reshape(b, l, n_heads, -1).transpose(0, 2, 1, 3)


def _flip(t):
    return jnp.flip(t, axis=2)


def _forget(z, lb):
    return lb + (1.0 - lb) * jax.nn.sigmoid(_heads(z, HG_HEADS).astype(jnp.float32))


def _hgrn2_scan(q, f, v, s0):
    b_, h_, l_, dk = q.shape
    n = l_ // CHUNK
    f32 = jnp.float32
    rs = lambda t: t.reshape(b_, h_, n, CHUNK, t.shape[-1])
    q32 = rs(q.astype(f32) * (dk ** -0.5))
    logf = rs(jnp.log(f))
    k = rs(1.0 - f)
    v32 = rs(v.astype(f32))
    b = jnp.cumsum(logf, axis=3)
    b_mid = b[:, :, :, CHUNK // 2 - 1:CHUNK // 2, :]
    b_last = b[:, :, :, -1:, :]
    a = jnp.einsum('bhnik,bhnjk->bhnij', q32 * jnp.exp(b - b_mid), k * jnp.exp(b_mid - b))
    mask = jnp.tril(jnp.ones((CHUNK, CHUNK), dtype=bool))
    a = jnp.where(mask, a, 0.0)
    o_intra = jnp.einsum('bhnij,bhnjv->bhniv', a, v32)
    q_dec = q32 * jnp.exp(b)
    k_dec = k * jnp.exp(b_last - b)
    decay = jnp.exp(b_last[:, :, :, 0, :])

    def step(s, xs):
        qd, kd, vc, dc = xs
        o = jnp.einsum('bhck,bhkv->bhcv', qd, s)
        s = dc[..., None] * s + jnp.einsum('bhck,bhcv->bhkv', kd, vc)
        return s, o

    xs = (jnp.moveaxis(q_dec, 2, 0), jnp.moveaxis(k_dec, 2, 0),
          jnp.moveaxis(v32, 2, 0), jnp.moveaxis(decay, 2, 0))
    s_final, o_inter = lax.scan(step, s0.astype(f32), xs)
    o = o_intra + jnp.moveaxis(o_inter, 0, 2)
    return o.reshape(b_, h_, l_, -1), s_final


def _hgrn2_final_state(f, v):
    b = jnp.cumsum(jnp.log(f), axis=2)
    k_dec = (1.0 - f) * jnp.exp(b[:, :, -1:, :] - b)
    return jnp.einsum('bhlk,bhlv->bhkv', k_dec, v.astype(jnp.float32))


def _fourier(t):
    b_, l_, _ = t.shape
    t32 = t.astype(jnp.float32).reshape(b_, l_, FT_GROUPS, FT_GROUP_W)
    y = jnp.fft.fftn(t32, axes=(1, 3), norm='ortho').real
    return y.reshape(b_, l_, FT_WIDTH).astype(t.dtype)


def _merge(o_hg, zg, zft, zgh, zgf, hg_norm_g, w_hg_out, w_ft_out, w_out):
    dt = zg.dtype
    o = o_hg * lax.rsqrt(jnp.mean(o_hg * o_hg, axis=-1, keepdims=True) + EPS) * hg_norm_g[:, None, :]
    b_, h_, l_, dv = o.shape
    o = o.transpose(0, 2, 1, 3).reshape(b_, l_, h_ * dv).astype(dt) * jax.nn.silu(zg)
    y_hg = o @ w_hg_out
    y_ft = _fourier(zft) @ w_ft_out
    y = jax.nn.sigmoid(zgh) * y_hg + jax.nn.sigmoid(zgf) * y_ft
    return y @ w_out


def _peer(u, w_q, sub_keys, u_tab, v_tab):
    b_, l_, d = u.shape
    xt = u.reshape(-1, PEER_TOKEN_BLOCK, d)

    def block(xb):
        t = xb.shape[0]
        q = (xb @ w_q).reshape(t, PEER_HEADS, 2, PEER_DKEY // 2)
        s = jnp.einsum('thpk,hpnk->thpn', q, sub_keys).astype(jnp.float32)
        sv, si = lax.top_k(s, PEER_TOPK)
        cand = sv[:, :, 0, :, None] + sv[:, :, 1, None, :]
        cidx = si[:, :, 0, :, None] * PEER_NKEYS + si[:, :, 1, None, :]
        cv, ci = lax.top_k(cand.reshape(t, PEER_HEADS, PEER_TOPK * PEER_TOPK), PEER_TOPK)
        eidx = jnp.take_along_axis(cidx.reshape(t, PEER_HEADS, PEER_TOPK * PEER_TOPK), ci, axis=-1)
        gates = jax.nn.softmax(cv, axis=-1)
        ue = jnp.take(u_tab, eidx, axis=0)
        act = jax.nn.gelu(jnp.einsum('thkd,td->thk', ue, xb), approximate=False)
        ve = jnp.take(v_tab, eidx, axis=0)
        w = (gates * act.astype(jnp.float32)).astype(v_tab.dtype)
        return jnp.einsum('thk,thkd->td', w, ve)

    return lax.map(block, xt).reshape(b_, l_, d)


def setup_inputs(seed: int = 0) -> dict:
    key = jax.random.key(seed)
    ks = jax.random.split(key, 20)
    nrm = lambda k, shape, s: jax.random.normal(k, shape, jnp.float32) * s
    return {
        "x": nrm(ks[0], (BATCH, SEQ, D_MODEL), 1.0),
        "c": nrm(ks[1], (BATCH, D_MODEL), 1.0),
        "ctx": nrm(ks[2], (BATCH, CTX_LEN, D_MODEL), 1.0),
        "c_ctx": nrm(ks[3], (D_MODEL,), 1.0),
        "w_ada": nrm(ks[4], (DEPTH, D_MODEL, 6 * D_MODEL), 0.5 * D_MODEL ** -0.5),
        "b_ada": nrm(ks[5], (DEPTH, 6 * D_MODEL), 0.02),
        "norm_mix_g": 1.0 + nrm(ks[6], (DEPTH, D_MODEL), 0.02),
        "norm_ffn_g": 1.0 + nrm(ks[7], (DEPTH, D_MODEL), 0.02),
        "w_in": nrm(ks[8], (DEPTH, D_MODEL, IN_COLS), D_MODEL ** -0.5),
        "hg_lb_f": nrm(ks[9], (DEPTH + 1, HG_WIDTH), 0.1),
        "hg_lb_b": nrm(ks[10], (DEPTH + 1, HG_WIDTH), 0.1),
        "hg_norm_g": 1.0 + nrm(ks[11], (DEPTH, HG_HEADS, HG_DV), 0.02),
        "w_hg_out": nrm(ks[12], (DEPTH, HG_WIDTH, D_MODEL), HG_WIDTH ** -0.5),
        "w_ft_out": nrm(ks[13], (DEPTH, FT_WIDTH, D_MODEL), FT_WIDTH ** -0.5),
        "w_out": nrm(ks[14], (DEPTH, D_MODEL, D_MODEL), D_MODEL ** -0.5),
        "peer_w_q": nrm(ks[15], (DEPTH, D_MODEL, PEER_HEADS * PEER_DKEY), D_MODEL ** -0.5),
        "peer_sub_keys": nrm(ks[16], (DEPTH, PEER_HEADS, 2, PEER_NKEYS, PEER_DKEY // 2), (PEER_DKEY // 2) ** -0.5),
        "peer_u": nrm(ks[17], (DEPTH, PEER_EXPERTS, D_MODEL), D_MODEL ** -0.5),
        "peer_v": nrm(ks[18], (DEPTH, PEER_EXPERTS, D_MODEL), PEER_HEADS ** -0.5),
        "final_norm_g": 1.0 + nrm(ks[19], (D_MODEL,), 0.02),
    }


def reference(x, c, ctx, c_ctx, w_ada, b_ada, norm_mix_g, norm_ffn_g, w_in, hg_lb_f, hg_lb_b,
              hg_norm_g, w_hg_out, w_ft_out, w_out, peer_w_q, peer_sub_keys, peer_u, peer_v,
              final_norm_g):
    lb_f_all = jnp.cumsum(jax.nn.softmax(hg_lb_f.astype(jnp.float32), axis=0), axis=0)
    lb_b_all = jnp.cumsum(jax.nn.softmax(hg_lb_b.astype(jnp.float32), axis=0), axis=0)
    h, hc = x, ctx
    for l in range(DEPTH):
        lb_f = lb_f_all[l].reshape(HG_HEADS, 1, HG_DK)
        lb_b = lb_b_all[l].reshape(HG_HEADS, 1, HG_DK)
        mod = jax.nn.silu(c) @ w_ada[l] + b_ada[l]
        sh1, sc1, g1, sh2, sc2, g2 = jnp.split(mod[:, None, :], 6, axis=-1)
        modc = jax.nn.silu(c_ctx) @ w_ada[l] + b_ada[l]
        sh1c, sc1c, g1c, sh2c, sc2c, g2c = jnp.split(modc, 6)

        uc = _modulate(hc, norm_mix_g[l], sh1c, sc1c)
        if l < DEPTH - 1:
            qc, ffc, fbc, ic, gc, ftc, ghc, gfc = jnp.split(uc @ w_in[l], SPLIT_POINTS, axis=-1)
            ffc, fbc = _forget(ffc, lb_f), _forget(fbc, lb_b)
            vc, qch = _heads(ic, HG_HEADS), _heads(qc, HG_HEADS)
            zeros = jnp.zeros((hc.shape[0], HG_HEADS, HG_DK, HG_DV), jnp.float32)
            oc_f, s_f = _hgrn2_scan(qch, ffc, vc, zeros)
            oc_b, s_b = _hgrn2_scan(_flip(qch), _flip(fbc), _flip(vc), zeros)
            hc = hc + g1c * _merge(oc_f + _flip(oc_b), gc, ftc, ghc, gfc,
                                   hg_norm_g[l], w_hg_out[l], w_ft_out[l], w_out[l])
            hc = hc + g2c * _peer(_modulate(hc, norm_ffn_g[l], sh2c, sc2c),
                                  peer_w_q[l], peer_sub_keys[l], peer_u[l], peer_v[l])
        else:
            ffc, fbc, ic = jnp.split(uc @ w_in[l][:, HG_WIDTH:4 * HG_WIDTH], 3, axis=-1)
            vc = _heads(ic, HG_HEADS)
            s_f = _hgrn2_final_state(_forget(ffc, lb_f), vc)
            s_b = _hgrn2_final_state(_flip(_forget(fbc, lb_b)), _flip(vc))

        u = _modulate(h, norm_mix_g[l], sh1, sc1)
        q, ff, fb, i, g, ft, gh, gf = jnp.split(u @ w_in[l], SPLIT_POINTS, axis=-1)
        qh, vh = _heads(q, HG_HEADS), _heads(i, HG_HEADS)
        o_f, _ = _hgrn2_scan(qh, _forget(ff, lb_f), vh, s_f)
        o_b, _ = _hgrn2_scan(_flip(qh), _flip(_forget(fb, lb_b)), _flip(vh), s_b)
        h = h + g1 * _merge(o_f + _flip(o_b), g, ft, gh, gf,
                            hg_norm_g[l], w_hg_out[l], w_ft_out[l], w_out[l])
        h = h + g2 * _peer(_modulate(h, norm_ffn_g[l], sh2, sc2),
                           peer_w_q[l], peer_sub_keys[l], peer_u[l], peer_v[l])
    return _rmsnorm(h, final_norm_g)
```

```python
import numpy as np
import ml_dtypes
import concourse.bass as bass
import concourse.mybir as mybir
from concourse.bass_utils import run_bass_kernel_spmd
from contextlib import ExitStack

F32 = mybir.dt.float32
BF16 = mybir.dt.bfloat16
U32 = mybir.dt.uint32
I32 = mybir.dt.int32
AF = mybir.ActivationFunctionType
ALU = mybir.AluOpType
AX = mybir.AxisListType

D = 1024
L = 4096
CTX = 256
NT = L + CTX
NB = NT // 128
NCH = NT // 64
EPS = 1e-6


class Res:
    __slots__ = ("name", "w", "r", "dsem", "dcnt", "bg")

    def __init__(self, name):
        self.name = name
        self.w = None
        self.r = []
        self.dsem = None
        self.dcnt = 0
        self.bg = False


class Sched:
    ENG = ["pe", "dve", "act", "pool", "sp"]

    def __init__(self, nc, es):
        self.nc = nc
        self.es = es
        self.prog = {e: [] for e in self.ENG}
        self.cnt = {e: 0 for e in self.ENG}
        self.sem = {e: es.enter_context(nc.semaphore("sem_" + e)) for e in self.ENG}
        self.seen = {e: {} for e in self.ENG}
        self.all = []
        self.excl = set()
        self.ndsem = 0

    def res(self, name=None):
        r = Res(name or ("r%d" % len(self.all)))
        self.all.append(r)
        return r

    def _semof(self, key):
        return self.sem[key] if isinstance(key, str) else key.dsem

    def _waits(self, e, reads, writes, skip_key=None):
        need = {}

        def add2(ev):
            k, v = ev
            if k == "pe" and e == "pe":
                return
            if need.get(k, 0) < v:
                need[k] = v

        for r in reads:
            if r.w is not None:
                add2(r.w)
        for w in writes:
            if w.w is not None and w.w[0] is not skip_key:
                add2(w.w)
            for ev in w.r:
                add2(ev)
        out = []
        seen = self.seen[e]
        for k, v in need.items():
            if seen.get(k, 0) >= v:
                continue
            seen[k] = v
            out.append((self._semof(k), v))
        return out

    def _update(self, ev, reads, writes):
        for r in reads:
            r.r.append(ev)
        for w in writes:
            w.w = ev
            w.r = []

    def op(self, e, fn, reads=(), writes=()):
        if self.excl:
            ex = [r for r in reads if r in self.excl and r not in writes]
            if ex:
                writes = list(writes) + ex
                reads = [r for r in reads if r not in self.excl]
        waits = self._waits(e, reads, writes)
        self.cnt[e] += 1
        ev = (e, self.cnt[e])
        self.prog[e].append((waits, fn, self.sem[e], 1))
        self._update(ev, reads, writes)

    def dma(self, e, fn, reads=(), writes=(), sem_res=None):
        sr = sem_res or (writes[0] if writes else reads[0])
        if sr.dsem is None:
            sr.dsem = self.es.enter_context(self.nc.semaphore("dsem%d" % self.ndsem))
            self.ndsem += 1
        waits = self._waits(e, reads, writes, skip_key=sr)
        sr.dcnt += 16
        ev = (sr, sr.dcnt)
        self.prog[e].append((waits, fn, sr.dsem, 16))
        self._update(ev, reads, writes)

    def phase_end(self):
        waits = self._waits("sp", [], [r for r in self.all if not r.bg])
        self.prog["sp"].append((waits, None, None, 0))
        nc = self.nc
        engs = {"pe": "tensor", "dve": "vector", "act": "scalar", "pool": "gpsimd", "sp": "sync"}
        with nc.Block() as block:
            for e in self.ENG:
                prog = self.prog[e]

                def body(eng, prog=prog):
                    for waits, fn, sem, inc in prog:
                        for s, v in waits:
                            eng.wait_ge(s, v)
                        if fn is not None:
                            fn(eng).then_inc(sem, inc)

                getattr(block, engs[e])(body)
        self.prog = {e: [] for e in self.ENG}
        for e in self.ENG:
            for k in self.ENG:
                self.seen[e][k] = self.cnt[k]
            for r in self.all:
                if r.dsem is not None and not r.bg:
                    self.seen[e][r] = r.dcnt


def build(debug=False, stop_after=None):
    nc = bass.Bass("TRN2", target_bir_lowering=False)

    def din(name, shape, dt=F32):
        return nc.dram_tensor(name, shape, dt, kind="ExternalInput").ap()

    def dscr(name, shape, dt):
        return nc.dram_tensor(name, shape, dt, kind="ExternalOutput" if debug else "Internal").ap()

    x_d = din("x", [L, D])
    ctx_d = din("ctx", [CTX, D])
    cc_d = din("cc", [128, 8, 2])
    wada_d = din("w_ada", [D, 6 * D])
    bada_d = din("b_ada_p", [128, 48])
    gmix_d = din("gmix_p", [128, 8])
    gffn_d = din("gffn_p", [128, 8])
    win_d = din("w_in", [D, 5120])
    lbf_d = din("lbf", [128, 4, 2])
    lbb_d = din("lbb", [128, 4, 2])
    hgn_d = din("hgn_p", [128, 4])
    whg_d = din("w_hg_out", [512, D])
    wft_d = din("w_ft_out", [512, D])
    wout_d = din("w_out", [D, D])
    wq_d = din("w_q", [D, 2048])
    skT_d = din("skT", [128, 16, 128])
    pu_d = din("peer_u", [16384, D])
    pv_d = din("peer_v", [16384, D])
    fng_d = din("fng", [D])
    csw_d = din("csw", [128, 256], BF16)
    cl_d = din("cl", [32, 128, 32, 128], BF16)
    sl_d = din("sl", [32, 128, 32, 128], BF16)
    identb_d = din("identb", [128, 128], BF16)
    identf_d = din("identf", [128, 128])
    mask_d = din("masks", [128, 2, 128])
    iota_d = din("iota16", [128, 16])
    rmask_d = din("rmask", [128, NT], BF16)
    out_d = nc.dram_tensor("out", [L, D], F32, kind="ExternalOutput").ap()

    rows_d = dscr("rows_s", [32, 128], F32)
    uT_dbg = dscr("uT_s", [128, 8, NT], BF16) if debug else None
    zq_s = dscr("zq_s", [4, 128, NT], BF16)
    zf_s = dscr("zf_s", [8, 128, NT], F32)
    zg_s = dscr("zg_s", [4, 128, NT], BF16)
    zft_s = dscr("zft_s", [4, 128, NT], BF16)
    zgh_s = dscr("zgh_s", [16, 128, NT], BF16)
    v_s = dscr("v_s", [NB, 128, 512], BF16)
    og_s = dscr("og_s", [4, 128, L], BF16)
    of_dbg = dscr("of_s", [4, 128, L], F32) if debug else None
    yT_s = dscr("yT_s", [4, 128, L], BF16)
    h1_s = dscr("h1_s", [L, D], F32)
    u2_s = dscr("u2_s", [L, D], F32)
    uvb_s = nc.dram_tensor("uvb_s", [16384, 2 * D], BF16, kind="Internal").ap()
    eidx_dbg = dscr("eidx_s", [128, 32, 128], I32) if debug else None
    gate_dbg = dscr("gate_s", [128, 32, 128], F32) if debug else None

    with ExitStack() as eg:
        S = Sched(nc, eg)

        def mm(out, lhsT, rhs, start=True, stop=True, r=(), w=(), sgc=False):
            S.op("pe", lambda e: e.matmul(out, lhsT=lhsT, rhs=rhs, start=start, stop=stop,
                                          skip_group_check=sgc), r, w)

        def tr(out, in_, ident, r=(), w=()):
            S.op("pe", lambda e: e.transpose(out, in_, ident), r, w)

        def act(out, in_, func, r=(), w=(), **kw):
            S.op("act", lambda e: e.activation(out=out, in_=in_, func=func, **kw), r, w)

        def tt(out, in0, in1, op, r=(), w=(), eng="dve"):
            S.op(eng, lambda e: e.tensor_tensor(out=out, in0=in0, in1=in1, op=op), r, w)

        def ts(out, in0, s1, op0, s2=None, op1=None, r=(), w=(), eng="dve"):
            if op1 is None:
                S.op(eng, lambda e: e.tensor_scalar(out=out, in0=in0, scalar1=s1, scalar2=None, op0=op0), r, w)
            else:
                S.op(eng, lambda e: e.tensor_scalar(out=out, in0=in0, scalar1=s1, scalar2=s2, op0=op0, op1=op1), r, w)

        def stt(out, in0, scalar, in1, op0, op1, r=(), w=(), accum_out=None):
            S.op("dve", lambda e: e.scalar_tensor_tensor(out=out, in0=in0, scalar=scalar, in1=in1, op0=op0,
                                                         op1=op1, accum_out=accum_out), r, w)

        def cp(out, in_, r=(), w=(), eng="dve"):
            S.op(eng, lambda e: e.tensor_copy(out=out, in_=in_), r, w)

        def dma(out, in_, r=(), w=(), q="sp", sem_res=None):
            S.dma(q, lambda e: e.dma_start(out=out, in_=in_), r, w, sem_res=sem_res)

        def gsb(name, shape, dt):
            return eg.enter_context(nc.sbuf_tensor("s_" + name, shape, dt)), S.res(name)

        PS = []
        for i in range(7):
            PS.append((eg.enter_context(nc.psum_tensor("ps%d" % i, [128, 512], F32)), S.res("ps%d" % i)))
        ptr, r_ptr = eg.enter_context(nc.psum_tensor("ptr", [128, 1024], BF16)), S.res("ptr")
        S.excl = set([p[1] for p in PS] + [r_ptr])

        identb, r_identb = gsb("identb", [128, 128], BF16)
        identf, r_identf = gsb("identf", [128, 128], F32)
        onesf, r_onesf = gsb("onesf", [128, 128], F32)
        modP, r_modP = gsb("modP", [128, 96], F32)
        vecP, r_vecP = gsb("vecP", [128, 64], F32)
        lbT, r_lbT = gsb("lbT", [128, 16], F32)
        hgn, r_hgn = gsb("hgn", [128, 4], F32)
        dma(identb[:], identb_d, w=[r_identb])
        dma(identf[:], identf_d, w=[r_identf])
        dma(hgn[:], hgn_d, w=[r_hgn])
        S.op("pool", lambda e: e.memset(onesf[:], 1.0), (), [r_onesf])
        onec, r_onec = gsb("onec", [128, 1], F32)
        S.op("pool", lambda e: e.memset(onec[:], 1.0), (), [r_onec])
        epsc, r_epsc = gsb("epsc", [128, 1], F32)
        S.op("pool", lambda e: e.memset(epsc[:], EPS), (), [r_epsc])

        with ExitStack() as es:
            def sb(name, shape, dt):
                return es.enter_context(nc.sbuf_tensor("s_" + name, shape, dt)), S.res(name)
            cc, r_cc = sb("cc", [128, 8, 2], F32)
            scc, r_scc = sb("scc", [128, 8, 2], F32)
            bada, r_bada = sb("bada", [128, 48], F32)
            gmix, r_gmix = sb("gmix", [128, 8], F32)
            gffn, r_gffn = sb("gffn", [128, 8], F32)
            lbin, r_lbin = sb("lbin", [128, 2, 4, 2], F32)
            lbd, r_lbd = sb("lbd", [128, 8], F32)
            tmp8, r_tmp8 = sb("tmp8", [128, 8], F32)
            rowsrc, r_rowsrc = sb("rowsrc", [32, 128], F32)
            slabs = [sb("slab%d" % i, [128, 8, 512], F32) for i in range(2)]
            dma(cc[:], cc_d, w=[r_cc])
            dma(bada[:], bada_d, w=[r_bada])
            dma(gmix[:], gmix_d, w=[r_gmix])
            dma(gffn[:], gffn_d, w=[r_gffn])
            dma(lbin[:, 0], lbf_d, w=[r_lbin])
            dma(lbin[:, 1], lbb_d, w=[r_lbin])
            act(scc[:], cc[:], AF.Silu, r=[r_cc], w=[r_scc])
            pmod, r_pmod = PS[0]
            wv = wada_d.rearrange("(kc p) n -> p kc n", p=128)
            for s in range(12):
                slab, r_slab = slabs[s % 2]
                dma(slab[:], wv[:, :, s * 512:(s + 1) * 512], w=[r_slab])
                for jj in range(4):
                    j = s * 4 + jj
                    for kc in range(8):
                        mm(pmod[:, 2 * j:2 * j + 2], slab[:, kc, jj * 128:(jj + 1) * 128], scc[:, kc, :],
                           start=(kc == 0), stop=(kc == 7), r=[r_slab, r_scc], w=[r_pmod])
            pm = pmod[:, 0:96].rearrange("p (j t) -> p j t", t=2)
            tt(modP[:, 0:48], pm[:, :, 0], bada[:], ALU.add, r=[r_pmod, r_bada], w=[r_modP])
            tt(modP[:, 48:96], pm[:, :, 1], bada[:], ALU.add, r=[r_pmod, r_bada], w=[r_modP])
            ts(tmp8[:], modP[:, 8:16], 1.0, ALU.add, r=[r_modP], w=[r_tmp8])
            tt(vecP[:, 0:8], tmp8[:], gmix[:], ALU.mult, r=[r_tmp8, r_gmix], w=[r_vecP])
            cp(vecP[:, 8:16], modP[:, 0:8], r=[r_modP], w=[r_vecP])
            ts(tmp8[:], modP[:, 56:64], 1.0, ALU.add, r=[r_modP], w=[r_tmp8])
            tt(vecP[:, 16:24], tmp8[:], gmix[:], ALU.mult, r=[r_tmp8, r_gmix], w=[r_vecP])
            cp(vecP[:, 24:32], modP[:, 48:56], r=[r_modP], w=[r_vecP])
            cp(vecP[:, 32:40], modP[:, 16:24], r=[r_modP], w=[r_vecP])
            ts(tmp8[:], modP[:, 32:40], 1.0, ALU.add, r=[r_modP], w=[r_tmp8])
            tt(vecP[:, 40:48], tmp8[:], gffn[:], ALU.mult, r=[r_tmp8, r_gffn], w=[r_vecP])
            cp(vecP[:, 48:56], modP[:, 24:32], r=[r_modP], w=[r_vecP])
            cp(vecP[:, 56:64], modP[:, 40:48], r=[r_modP], w=[r_vecP])
            pT, r_pT = PS[1]
            tr(pT[0:32, 0:128], vecP[:, 32:64], identf[:], r=[r_vecP, r_identf], w=[r_pT])
            cp(rowsrc[:], pT[0:32, 0:128], r=[r_pT], w=[r_rowsrc])
            r_rowsd = S.res("rows_d")
            dma(rows_d, rowsrc[:], r=[r_rowsrc], w=[r_rowsd])
            lv = lbin[:].rearrange("p a h t -> p (a h) t")
            tt(lbd[:], lv[:, :, 0], lv[:, :, 1], ALU.subtract, r=[r_lbin], w=[r_lbd])
            lbT4 = lbT[:].rearrange("p (a b h) -> p a b h", a=2, b=2)
            act(lbT4[:, :, 0, :], lbd[:].rearrange("p (a h) -> p a h", a=2), AF.Sigmoid, r=[r_lbd], w=[r_lbT])
            ts(lbT4[:, :, 1, :], lbT4[:, :, 0, :], -1.0, ALU.mult, 1.0, ALU.add, r=[r_lbT], w=[r_lbT])
            S.phase_end()
        if stop_after == "A":
            return nc, S

        r_tbg = [S.res("tbg%d" % i) for i in range(4)]

        def gen_T():
            TR = 1024
            for c in range(16384 // TR):
                rs_ = slice(c * TR, (c + 1) * TR)
                dma(uvb_s[rs_, 0:D], pu_d[rs_, :], q="pool", sem_res=r_tbg[c % 4])
                dma(uvb_s[rs_, D:2 * D], pv_d[rs_, :], q="pool", sem_res=r_tbg[c % 4])
                yield
        gT = gen_T()

        with ExitStack() as es:
            def sb(name, shape, dt):
                return es.enter_context(nc.sbuf_tensor("s_" + name, shape, dt)), S.res(name)
            uT, r_uT = sb("uT", [128, 8, NT], BF16)
            r_uTb = [S.res("uTb%d" % b) for b in range(NB)]
            xts = [sb("xt%d" % i, [128, D], F32) for i in range(2)]
            xns = [sb("xn%d" % i, [128, D], BF16) for i in range(2)]
            junk, r_junk = sb("junk", [128, D], BF16)
            st4, r_st4 = sb("st4", [128, 4], F32)
            tmpu, r_tmpu = sb("tmpu", [128, 8, 128], F32)
            for blk in range(NB):
                xt, r_xt = xts[blk % 2]
                xn, r_xn = xns[blk % 2]
                src = ctx_d[blk * 128:(blk + 1) * 128, :] if blk < 2 else x_d[(blk - 2) * 128:(blk - 1) * 128, :]
                dma(xt[:], src, w=[r_xt])
                act(junk[:], xt[:], AF.Square, r=[r_xt], w=[r_junk, r_st4], accum_out=st4[:, 0:1])
                act(st4[:, 1:2], st4[:, 0:1], AF.Sqrt, r=[r_st4], w=[r_st4], scale=1.0 / D, bias=EPS)
                S.op("dve", lambda e: e.reciprocal(out=st4[:, 2:3], in_=st4[:, 1:2]), [r_st4], [r_st4])
                act(xn[:], xt[:], AF.Copy, r=[r_xt, r_st4], w=[r_xn], scale=st4[:, 2:3])
                for kc in range(8):
                    tr(ptr[:, kc * 128:(kc + 1) * 128], xn[:, kc * 128:(kc + 1) * 128], identb[:],
                       r=[r_xn, r_identb], w=[r_ptr])
                vo = 16 if blk < 2 else 0
                pv = ptr[:].rearrange("p (k t) -> p k t", t=128)
                tt(tmpu[:], pv, vecP[:, vo:vo + 8].unsqueeze(2).broadcast_to([128, 8, 128]), ALU.mult,
                   r=[r_ptr, r_vecP], w=[r_tmpu])
                tt(uT[:, :, blk * 128:(blk + 1) * 128], tmpu[:],
                   vecP[:, vo + 8:vo + 16].unsqueeze(2).broadcast_to([128, 8, 128]), ALU.add,
                   r=[r_tmpu, r_vecP], w=[r_uTb[blk]])
            if debug:
                dma(uT_dbg, uT[:], r=r_uTb, w=[])
            wsts = [sb("wst%d" % i, [128, 8, 128], F32) for i in range(2)]
            wbfs = [sb("wbf%d" % i, [128, 8, 128], BF16) for i in range(2)]
            zf32 = [sb("zf32_%d" % i, [128, NT], F32) for i in range(1)]
            zb16 = [sb("zb16_%d" % i, [128, NT], BF16) for i in range(2)]
            wview = win_d.rearrange("(kc p) n -> p kc n", p=128)
            tiles = [(t * 512, 512) for t in range(8)] + [(4096, 256)]
            plan = []
            for h in range(4):
                plan.append((h, AF.Copy, 128.0 ** -0.5, zq_s[h], False))
            for j in range(8):
                plan.append((4 + j, AF.Sigmoid, 1.0, zf_s[j], True))
            for h in range(4):
                plan.append((16 + h, AF.Silu, 1.0, zg_s[h], False))
            for h in range(4):
                plan.append((20 + h, AF.Copy, 1.0, zft_s[h], False))
            for j in range(16):
                plan.append((24 + j, AF.Sigmoid, 1.0, zgh_s[j], False))
            plan.sort(key=lambda p: {AF.Copy: 0, AF.Sigmoid: 1, AF.Silu: 2}[p[1]])
            nf = nb = 0
            r_zscr = S.res("zscr")
            def load_w(ci):
                wst, r_wst = wsts[ci % 2]
                cb = plan[ci][0]
                dma(wst[:], wview[:, :, cb * 128:(cb + 1) * 128], w=[r_wst])
            load_w(0)
            for ci, (cb, func, scale, dst, isf) in enumerate(plan):
                wst, r_wst = wsts[ci % 2]
                wbf, r_wbf = wbfs[ci % 2]
                cp(wbf[:], wst[:], r=[r_wst], w=[r_wbf])
                if ci + 1 < len(plan):
                    load_w(ci + 1)
                if isf:
                    zst, r_zst = zf32[0]; nf += 1
                else:
                    zst, r_zst = zb16[nb % 2]; nb += 1
                for ti, (t0, tn) in enumerate(tiles):
                    pz, r_pz = PS[ti % 4]
                    for kc in range(8):
                        mm(pz[:, 0:tn], wbf[:, kc, :], uT[:, kc, t0:t0 + tn], start=(kc == 0), stop=(kc == 7),
                           r=[r_wbf] + r_uTb[t0 // 128:(t0 + tn) // 128], w=[r_pz])
                    act(zst[:, t0:t0 + tn], pz[:, 0:tn], func, r=[r_pz], w=[r_zst], scale=scale)
                    if 4 <= cb < 12:
                        jd, jh = (cb - 4) // 4, (cb - 4) % 4
                        ts(zst[:, t0:t0 + tn], zst[:, t0:t0 + tn], lbT[:, jd * 8 + 4 + jh:jd * 8 + 4 + jh + 1], ALU.mult,
                           lbT[:, jd * 8 + jh:jd * 8 + jh + 1], ALU.add, r=[r_zst, r_lbT], w=[r_zst])
                    if ti == 0 and ci % 6 == 2:
                        next(gT, None)
                dma(dst, zst[:], r=[r_zst], w=[], sem_res=r_zst)
            wV32, r_wV32 = sb("wV32", [128, 8, 512], F32)
            wV, r_wV = sb("wV", [128, 8, 512], BF16)
            Vsts = [sb("Vst%d" % i, [128, 512], BF16) for i in range(2)]
            dma(wV32[:], wview[:, :, 1536:2048], w=[r_wV32])
            cp(wV[:], wV32[:], r=[r_wV32], w=[r_wV], eng="pool")
            for blk in range(NB):
                pz, r_pz = PS[4 + blk % 3]
                for kc in range(8):
                    mm(pz[:, :], uT[:, kc, blk * 128:(blk + 1) * 128], wV[:, kc, :], start=(kc == 0), stop=(kc == 7),
                       r=[r_wV, r_uTb[blk]], w=[r_pz])
                Vst, r_Vst = Vsts[blk % 2]
                if blk % 2 == 0:
                    act(Vst[:], pz[:, :], AF.Copy, r=[r_pz], w=[r_Vst])
                else:
                    cp(Vst[:], pz[:, :], r=[r_pz], w=[r_Vst])
                dma(v_s[blk], Vst[:], r=[r_Vst], w=[], sem_res=r_Vst)
            S.phase_end()
        if stop_after == "P":
            return nc, S

        with ExitStack() as es:
            def sb(name, shape, dt):
                return es.enter_context(nc.sbuf_tensor("s_" + name, shape, dt)), S.res(name)
            rmask, r_rmask = sb("rmask", [128, NT], BF16)
            masks, r_masks = sb("masks", [128, 2, 128], F32)
            dma(rmask[:], rmask_d, w=[r_rmask])
            dma(masks[:], mask_d, w=[r_masks])
            Fb, r_F = sb("Fb", [128, NT], F32)
            Lb, r_L = sb("Lb", [128, NT], F32)
            Bb, r_B = sb("Bb", [128, NT], F32)
            Kb, r_K = sb("Kb", [128, NT], BF16)
            qb, r_q = sb("qb", [128, NT], BF16)
            QD, r_QD = sb("QD", [128, NT], BF16)
            QE, r_QE = sb("QE", [128, NT], BF16)
            KD, r_KD = sb("KD", [128, NT], BF16)
            KDt, r_KDt = sb("KDt", [128, NB, 128], BF16)
            Vb, r_V = sb("Vb", [128, NB, 128], BF16)
            sgg, r_sgg = sb("sgg", [128, NT], BF16)
            Sall, r_Sall = sb("Sall", [128, NCH, 128], BF16)
            Sf = [sb("Sf%d" % i, [128, 128], F32) for i in range(2)]
            oT, r_oT = sb("oT", [128, L], F32)
            edge, r_edge = sb("edge", [128, NCH], F32)
            dec, r_dec = sb("dec", [128, NCH], F32)
            Am_all, _ = sb("Am_all", [128, 32, 128], BF16)
            r_Amb = [S.res("Amb%d" % i) for i in range(32)]
            sq, r_sq = sb("sq", [128, 512], F32)
            rs, r_rs = sb("rs", [128, 512], F32)
            t1, r_t1 = sb("t1", [128, 512], F32)
            ogst, r_ogst = QE, r_QE
            r_ogscr = S.res("ogscr")
            r_pUs = [S.res("pU%d" % i) for i in range(8)]
            for h in range(4):
                dma(qb[:], zq_s[h], w=[r_q])
                dma(Vb[:], v_s[:, :, h * 128:(h + 1) * 128].rearrange("b p v -> p b v"), w=[r_V])
                dma(sgg[:], zg_s[h], w=[r_sgg])
                for dr in range(2):
                    lbc = lbT[:, dr * 8 + h:dr * 8 + h + 1]
                    omlc = lbT[:, dr * 8 + 4 + h:dr * 8 + 4 + h + 1]
                    dma(Fb[:], zf_s[dr * 4 + h], w=[r_F])
                    act(Kb[:], Fb[:], AF.Identity, r=[r_F], w=[r_K], scale=-1.0, bias=onec[:, 0:1])
                    act(Lb[:], Fb[:], AF.Ln, r=[r_F], w=[r_L])
                    S.op("dve", lambda e: e.tensor_tensor_scan(out=Bb[:], data0=rmask[:], data1=Lb[:], initial=0.0,
                                                               op0=ALU.mult, op1=ALU.add), [r_rmask, r_L], [r_B])
                    Bv = Bb[:].rearrange("p (c t) -> p c t", t=64)
                    cp(edge[:], Bv[:, :, 63], r=[r_B], w=[r_edge])
                    ebc = edge[:].unsqueeze(2).broadcast_to([128, NCH, 64])
                    if dr == 1:
                        tt(Lb[:], Lb[:], Bb[:], ALU.subtract, r=[r_L, r_B], w=[r_L])
                        tt(Bv, Lb[:].rearrange("p (c t) -> p c t", t=64), ebc, ALU.add, r=[r_L, r_edge], w=[r_B])
                    tt(Lb[:].rearrange("p (c t) -> p c t", t=64), Bv, ebc, ALU.subtract, r=[r_B, r_edge], w=[r_L])
                    act(Fb[:], Bb[:], AF.Exp, r=[r_B], w=[r_F])
                    tt(QD[:], Fb[:], qb[:], ALU.mult, r=[r_F, r_q], w=[r_QD])
                    act(Fb[:], Lb[:], AF.Exp, r=[r_L], w=[r_F])
                    tt(QE[:], Fb[:], qb[:], ALU.mult, r=[r_F, r_q], w=[r_QE])
                    act(Fb[:], Lb[:], AF.Exp, r=[r_L], w=[r_F], scale=-1.0)
                    tt(KD[:], Fb[:], Kb[:], ALU.mult, r=[r_F, r_K], w=[r_KD])
                    act(dec[:], edge[:], AF.Exp, r=[r_edge], w=[r_dec])
                    next(gT, None)
                    for b0 in range(0, NB, 8):
                        nb_ = min(8, NB - b0)
                        for bi in range(nb_):
                            blk = b0 + bi
                            tr(ptr[:, bi * 128:(bi + 1) * 128], KD[:, blk * 128:(blk + 1) * 128], identb[:],
                               r=[r_KD, r_identb], w=[r_ptr])
                        act(KDt[:, b0:b0 + nb_, :], ptr[:, 0:nb_ * 128].rearrange("p (b k) -> p b k", k=128), AF.Copy,
                            r=[r_ptr], w=[r_KDt])
                    order = list(range(NCH)) if dr == 0 else [3, 2, 1, 0] + list(range(NCH - 1, 3, -1))
                    S.op("pool", lambda e: e.memset(Sf[0][0][:], 0.0), (), [Sf[0][1]])
                    S.op("pool", lambda e, n0=order[0]: e.memset(Sall[:, n0, :], 0.0), (), [r_Sall])
                    def a_part(i):
                        blk = i + 2
                        pA, r_pA = PS[2 + i % 2]
                        cs = slice(blk * 128, (blk + 1) * 128)
                        mm(pA[:, 0:128], KD[:, cs], QE[:, cs], r=[r_KD, r_QE], w=[r_pA])
                        tt(Am_all[:, i, :], pA[:, 0:128], masks[:, dr, :], ALU.mult, r=[r_pA, r_masks], w=[r_Amb[i]])
                    n_a = 0
                    for idx in range(len(order) - 1):
                        n = order[idx]
                        blk, half = n // 2, n % 2
                        psl = slice(64 * half, 64 * half + 64)
                        pU, r_pU = PS[(0, 1, 4, 5)[idx % 4]]
                        pUs = pU[:, ((idx // 4) % 4) * 128:((idx // 4) % 4 + 1) * 128]
                        mm(pUs, KDt[psl, blk, :], Vb[psl, blk, :], r=[r_KDt, r_V], w=[r_pU])
                        cur, r_cur = Sf[idx % 2]
                        nxt, r_nxt = Sf[(idx + 1) % 2]
                        stt(nxt[:], cur[:], dec[:, n:n + 1], pUs, ALU.mult, ALU.add, r=[r_cur, r_dec, r_pU], w=[r_nxt])
                        act(Sall[:, order[idx + 1], :], nxt[:], AF.Copy, r=[r_nxt], w=[r_Sall])
                        if idx % 2 == 1 and n_a < 32:
                            a_part(n_a)
                            n_a += 1
                    while n_a < 32:
                        a_part(n_a)
                        n_a += 1
                    for blk in range(2, NB):
                        i = blk - 2
                        pO, r_pO = PS[4 + i % 2]
                        mm(pO[:, 0:128], Vb[:, blk, :], Am_all[:, i, :], start=True, stop=False, r=[r_V, r_Amb[i]], w=[r_pO],
                           sgc=True)
                        mm(pO[:, 0:64], Sall[:, 2 * blk, :], QD[:, blk * 128:blk * 128 + 64], start=False, stop=False,
                           r=[r_Sall, r_QD], w=[r_pO], sgc=True)
                        mm(pO[:, 64:128], Sall[:, 2 * blk + 1, :], QD[:, blk * 128 + 64:blk * 128 + 128], start=False,
                           stop=True, r=[r_Sall, r_QD], w=[r_pO], sgc=True)
                        if dr == 0:
                            act(oT[:, i * 128:(i + 1) * 128], pO[:, 0:128], AF.Copy, r=[r_pO], w=[r_oT])
                        else:
                            tt(oT[:, i * 128:(i + 1) * 128], oT[:, i * 128:(i + 1) * 128], pO[:, 0:128], ALU.add,
                               r=[r_pO, r_oT], w=[r_oT])
                    if debug and dr == 0:
                        r_ofd = S.res("ofd")
                        dma(of_dbg[h], oT[:], r=[r_oT], w=[r_ofd], sem_res=r_oT)
                for t in range(8):
                    tsl = slice(t * 512, (t + 1) * 512)
                    pN, r_pN = PS[6]
                    tt(sq[:], oT[:, tsl], oT[:, tsl], ALU.mult, r=[r_oT], w=[r_sq])
                    mm(pN[:, :], onesf[:], sq[:], r=[r_onesf, r_sq], w=[r_pN])
                    act(rs[:], pN[:, :], AF.Ln, r=[r_pN], w=[r_rs], scale=1.0 / 128, bias=epsc[:, 0:1])
                    act(rs[:], rs[:], AF.Exp, r=[r_rs], w=[r_rs], scale=-0.5)
                    tt(t1[:], oT[:, tsl], rs[:], ALU.mult, r=[r_oT, r_rs], w=[r_t1])
                    stt(ogst[:, tsl], t1[:], hgn[:, h:h + 1], sgg[:, CTX + t * 512:CTX + (t + 1) * 512], ALU.mult, ALU.mult,
                        r=[r_t1, r_hgn, r_sgg], w=[r_ogst])
                dma(og_s[h], ogst[:, 0:L], r=[r_ogst], w=[], sem_res=r_ogst)
            S.phase_end()
        if stop_after == "H":
            return nc, S

        with ExitStack() as es:
            def sb(name, shape, dt):
                return es.enter_context(nc.sbuf_tensor("s_" + name, shape, dt)), S.res(name)
            ftT, r_ftT = sb("ftT", [128, 4, L], BF16)
            csw, r_csw = sb("csw", [128, 256], BF16)
            Pc, r_Pc = sb("Pc", [128, 32, 512], BF16)
            Psn, r_Psn = sb("Psn", [128, 32, 512], BF16)
            CLs = [sb("CLs%d" % i, [128, 32, 128], BF16) for i in range(2)]
            SLs = [sb("SLs%d" % i, [128, 32, 128], BF16) for i in range(2)]
            Ytm = [sb("Ytm%d" % i, [128, 512], BF16) for i in range(2)]
            YT, r_YT = sb("YT", [128, 4, L], BF16)
            dma(csw[:], csw_d, w=[r_csw])
            for g in range(4):
                dma(ftT[:, g, :], zft_s[g][:, CTX:NT], w=[r_ftT])
            for lb in range(32):
                pa, r_pa = PS[(lb % 2) * 2]
                pb, r_pb = PS[(lb % 2) * 2 + 1]
                for g in range(4):
                    p_, r_p = (pa, r_pa) if g < 2 else (pb, r_pb)
                    mm(p_[:, (g % 2) * 256:(g % 2 + 1) * 256], ftT[:, g, lb * 128:(lb + 1) * 128], csw[:],
                       r=[r_ftT, r_csw], w=[r_p])
                for (p_, r_p, g0) in ((pa, r_pa, 0), (pb, r_pb, 2)):
                    pv = p_[:, :].rearrange("p (g t m) -> p g t m", g=2, t=2)
                    act(Pc[:, lb, g0 * 128:(g0 + 2) * 128].rearrange("p (g m) -> p g m", g=2), pv[:, :, 0, :], AF.Copy,
                        r=[r_p], w=[r_Pc])
                    ts(Psn[:, lb, g0 * 128:(g0 + 2) * 128].rearrange("p (g m) -> p g m", g=2), pv[:, :, 1, :], -1.0,
                       ALU.mult, r=[r_p], w=[r_Psn])
            r_ytd = S.res("ytd")
            for kb in range(32):
                cl, r_cl = CLs[kb % 2]
                sl, r_sl = SLs[kb % 2]
                dma(cl[:], cl_d[kb], w=[r_cl])
                dma(sl[:], sl_d[kb], w=[r_sl])
                pY, r_pY = PS[4 + kb % 2]
                for lb in range(32):
                    mm(pY[:, :], cl[:, lb, :], Pc[:, lb, :], start=(lb == 0), stop=False, r=[r_cl, r_Pc], w=[r_pY])
                for lb in range(32):
                    mm(pY[:, :], sl[:, lb, :], Psn[:, lb, :], start=False, stop=(lb == 31), r=[r_sl, r_Psn], w=[r_pY])
                ytm, r_ytm = Ytm[kb % 2]
                act(ytm[:], pY[:, :], AF.Copy, r=[r_pY], w=[r_ytm], scale=float((4096.0 * 128.0) ** -0.5))
                for g in range(4):
                    tr(ptr[:, g * 128:(g + 1) * 128], ytm[:, g * 128:(g + 1) * 128], identb[:], r=[r_ytm, r_identb],
                       w=[r_ptr])
                cp(YT[:, :, kb * 128:(kb + 1) * 128], ptr[:, 0:512].rearrange("p (g t) -> p g t", g=4), r=[r_ptr],
                   w=[r_YT])
            for g in range(4):
                dma(yT_s[g], YT[:, g, :], r=[r_YT], w=[], sem_res=r_YT)
            for _ in gT:
                pass
            for r_ in r_tbg:
                r_.bg = False
            S.phase_end()
        if stop_after == "F":
            return nc, S

        with ExitStack() as es:
            def sb(name, shape, dt):
                return es.enter_context(nc.sbuf_tensor("s_" + name, shape, dt)), S.res(name)
            rows, r_rows = sb("rowsM", [128, 4, D], F32)
            dma(rows[:].rearrange("p a d -> p (a d)"), rows_d.rearrange("a b -> (a b)").partition_broadcast(128), w=[r_rows])
            stg, r_stg = sb("stg", [128, 8, D], F32)
            whg, r_whg = sb("whg", [128, 4, D], BF16)
            wft, r_wft = sb("wft", [128, 4, D], BF16)
            wo, r_wo = sb("wo", [128, 8, D], BF16)
            dma(stg[:, 0:4, :], whg_d.rearrange("(h p) n -> p h n", p=128), w=[r_stg])
            cp(whg[:], stg[:, 0:4, :], r=[r_stg], w=[r_whg], eng="pool")
            dma(stg[:, 0:4, :], wft_d.rearrange("(h p) n -> p h n", p=128), w=[r_stg])
            cp(wft[:], stg[:, 0:4, :], r=[r_stg], w=[r_wft], eng="pool")
            dma(stg[:], wout_d.rearrange("(h p) n -> p h n", p=128), w=[r_stg])
            cp(wo[:], stg[:], r=[r_stg], w=[r_wo], eng="pool")
            ogs = [sb("og%d" % i, [128, 4, 512], BF16) for i in range(2)]
            yts = [sb("yt%d" % i, [128, 4, 512], BF16) for i in range(2)]
            sghs = [sb("sgh%d" % i, [128, 8, 512], BF16) for i in range(2)]
            sgfs = [sb("sgf%d" % i, [128, 8, 512], BF16) for i in range(2)]
            yT2, r_yT2 = sb("yT2", [128, 8, 512], BF16)
            ta, r_ta = sb("ta", [128, 512], F32)
            tb, r_tb = sb("tb", [128, 512], F32)
            xms = [sb("xm%d" % i, [128, D], F32) for i in range(2)]
            h1s = [sb("h1_%d" % i, [128, D], F32) for i in range(2)]
            r_h1d = S.res("h1d")
            def load_m(t):
                dma(ogs[t % 2][0][:], og_s.rearrange("h p t -> p h t")[:, :, t * 512:(t + 1) * 512], w=[ogs[t % 2][1]])
                dma(yts[t % 2][0][:], yT_s.rearrange("h p t -> p h t")[:, :, t * 512:(t + 1) * 512], w=[yts[t % 2][1]])
                dma(sghs[t % 2][0][:], zgh_s[0:8].rearrange("h p t -> p h t")[:, :, CTX + t * 512:CTX + (t + 1) * 512],
                    w=[sghs[t % 2][1]])
                dma(sgfs[t % 2][0][:], zgh_s[8:16].rearrange("h p t -> p h t")[:, :, CTX + t * 512:CTX + (t + 1) * 512],
                    w=[sgfs[t % 2][1]])
            load_m(0)
            for t in range(8):
                og, r_og = ogs[t % 2]
                yt, r_yt = yts[t % 2]
                sgh, r_sgh = sghs[t % 2]
                sgf, r_sgf = sgfs[t % 2]
                if t + 1 < 8:
                    load_m(t + 1)
                for db in range(8):
                    pH, r_pH = PS[db % 2]
                    pF, r_pF = PS[2 + db % 2]
                    for h in range(4):
                        mm(pH[:, :], whg[:, h, db * 128:(db + 1) * 128], og[:, h, :], start=(h == 0), stop=(h == 3),
                           r=[r_whg, r_og], w=[r_pH])
                    for g in range(4):
                        mm(pF[:, :], wft[:, g, db * 128:(db + 1) * 128], yt[:, g, :], start=(g == 0), stop=(g == 3),
                           r=[r_wft, r_yt], w=[r_pF])
                    tt(ta[:], pH[:, :], sgh[:, db, :], ALU.mult, r=[r_pH, r_sgh], w=[r_ta])
                    tt(tb[:], pF[:, :], sgf[:, db, :], ALU.mult, r=[r_pF, r_sgf], w=[r_tb])
                    tt(yT2[:, db, :], ta[:], tb[:], ALU.add, r=[r_ta, r_tb], w=[r_yT2], eng="pool")
                for sub in range(4):
                    tb_ = t * 4 + sub
                    xm, r_xm = xms[tb_ % 2]
                    h1, r_h1 = h1s[tb_ % 2]
                    if tb_ == 0:
                        dma(xm[:], x_d[0:128, :], w=[r_xm])
                    if tb_ + 1 < 32:
                        dma(xms[(tb_ + 1) % 2][0][:], x_d[(tb_ + 1) * 128:(tb_ + 2) * 128, :], w=[xms[(tb_ + 1) % 2][1]])
                    for half in range(2):
                        pM, r_pM = PS[4 + half]
                        for db in range(8):
                            mm(pM[:, :], yT2[:, db, sub * 128:(sub + 1) * 128], wo[:, db, half * 512:(half + 1) * 512],
                               start=(db == 0), stop=(db == 7), r=[r_yT2, r_wo], w=[r_pM])
                        hs = slice(half * 512, (half + 1) * 512)
                        tt(ta[:], pM[:, :], rows[:, 0, hs], ALU.mult, r=[r_pM, r_rows], w=[r_ta])
                        tt(h1[:, hs], ta[:], xm[:, hs], ALU.add, r=[r_ta, r_xm], w=[r_h1])
                    dma(h1_s[tb_ * 128:(tb_ + 1) * 128, :], h1[:], r=[r_h1], w=[], sem_res=r_h1)
            S.phase_end()
        if stop_after == "M":
            return nc, S

        eq = eg.enter_context(ExitStack())
        wq, r_wq = eq.enter_context(nc.sbuf_tensor("s_wq", [128, 8, 2048], BF16)), S.res("wq")
        skb, r_skb = eq.enter_context(nc.sbuf_tensor("s_skb", [128, 16, 128], BF16)), S.res("skb")
        with ExitStack() as es:
            def sb(name, shape, dt):
                return es.enter_context(nc.sbuf_tensor("s_" + name, shape, dt)), S.res(name)
            stgs = [sb("stgq%d" % i, [128, 8, 512], F32) for i in range(2)]
            wqv = wq_d.rearrange("(kc p) n -> p kc n", p=128)
            for c4 in range(4):
                stg, r_stg = stgs[c4 % 2]
                dma(stg[:], wqv[:, :, c4 * 512:(c4 + 1) * 512], w=[r_stg])
                cp(wq[:, :, c4 * 512:(c4 + 1) * 512], stg[:], r=[r_stg], w=[r_wq], eng=("pool" if c4 % 2 else "dve"))
            stg, r_stg = stgs[0]
            dma(stg[:, 0:4, :].rearrange("p a (b n) -> p (a b) n", n=128), skT_d, w=[r_stg])
            cp(skb[:], stg[:, 0:4, :].rearrange("p a (b n) -> p (a b) n", n=128), r=[r_stg], w=[r_skb])
            S.phase_end()

        with ExitStack() as es:
            def sb(name, shape, dt):
                return es.enter_context(nc.sbuf_tensor("s_" + name, shape, dt)), S.res(name)
            rows, r_rows = sb("rowsQ", [128, 4, D], F32)
            dma(rows[:].rearrange("p a d -> p (a d)"), rows_d.rearrange("a b -> (a b)").partition_broadcast(128), w=[r_rows])
            fngr, r_fngr = sb("fngr", [128, D], F32)
            dma(fngr[:], fng_d.partition_broadcast(128), w=[r_fngr])
            io16, r_io16 = sb("io16", [128, 16], F32)
            dma(io16[:], iota_d, w=[r_io16])
            hbs = [sb("hb%d" % i, [128, D], F32) for i in range(2)]
            u2f, r_u2f = sb("u2f", [128, D], F32)
            u2bs = [sb("u2b%d" % i, [128, D], BF16) for i in range(2)]
            eidxs = [sb("eidx%d" % i, [128, 128], I32) for i in range(2)]
            gates = [sb("gate%d" % i, [128, 128], F32) for i in range(2)]
            junkq, r_junkq = sb("junkq", [128, D], BF16)
            st4q, r_st4q = sb("st4q", [128, 4], F32)
            u2T, r_u2T = sb("u2T", [128, 8, 128], BF16)
            qT, r_qT = sb("qT", [128, 16, 128], BF16)
            ssb, r_ssb = sb("ssb", [128, 16, 128], F32)
            swk, _ = sb("swk", [128, 8, 128], F32)
            sv, _ = sb("sv", [128, 16, 16], F32)
            si, _ = sb("si", [128, 16, 16], U32)
            sif, r_sif = sb("sif", [128, 16, 16], F32)
            cand, _ = sb("cand", [128, 8, 256], F32)
            cwk, _ = sb("cwk", [128, 4, 256], F32)
            cv, _ = sb("cv", [128, 8, 16], F32)
            ci, _ = sb("ci", [128, 8, 16], U32)
            cih, r_cih = sb("cih", [128, 8, 16], U32)
            cil, r_cil = sb("cil", [128, 8, 16], U32)
            cif, r_cif = sb("cif", [128, 2, 128], F32)
            oh, r_oh = sb("oh", [128, 64, 16], F32)
            ef, r_ef = sb("ef", [128, 2, 128], F32)
            ex, r_ex = sb("ex", [128, 8, 16], F32)
            sm, r_sm = sb("sm", [128, 8], F32)
            r_svh = [S.res("svh%d" % i) for i in range(16)]
            r_sih = [S.res("sih%d" % i) for i in range(16)]
            r_swkh = [S.res("swkh%d" % i) for i in range(8)]
            r_candh = [S.res("candh%d" % i) for i in range(8)]
            r_cvh = [S.res("cvh%d" % i) for i in range(8)]
            r_cih_ = [S.res("cih_%d" % i) for i in range(8)]
            r_cwkh = [S.res("cwkh%d" % i) for i in range(4)]
            NR = 18
            GS = 4
            NG = 128 // GS
            uvg = [sb("uvg%d" % i, [128, 2 * D], BF16) for i in range(NR)]
            diag = [sb("diag%d" % i, [128, GS, 128], BF16) for i in range(2)]
            junkg, _ = sb("junkg", [128, D], BF16)
            apre = [sb("apre%d" % i, [128, 128], F32) for i in range(2)]
            wts = [sb("wts%d" % i, [128, 128], F32) for i in range(2)]
            r_aps = [[S.res("aps%d_%d" % (i, j)) for j in range(128)] for i in range(2)]
            r_wtg = [[S.res("wtg%d_%d" % (i, g)) for g in range(NG)] for i in range(2)]
            h2, r_h2 = sb("h2", [128, D], F32)
            st4, r_st4 = sb("st4g", [128, 4], F32)
            outs = [sb("outs%d" % i, [128, D], F32) for i in range(1)]
            gstate = {"gcount": 0}

            def vop(fn, r, w):
                S.op("dve", fn, r, w)

            def gen_Q(tb_):
                hb, r_hb = hbs[tb_ % 2]
                u2b, r_u2b = u2bs[tb_ % 2]
                eidx, r_eidx = eidxs[tb_ % 2]
                gate, r_gate = gates[tb_ % 2]
                dma(hb[:], h1_s[tb_ * 128:(tb_ + 1) * 128, :], w=[r_hb])
                yield
                act(junkq[:], hb[:], AF.Square, r=[r_hb], w=[r_st4q], accum_out=st4q[:, 0:1])
                act(st4q[:, 1:2], st4q[:, 0:1], AF.Sqrt, r=[r_st4q], w=[r_st4q], scale=1.0 / D, bias=EPS)
                yield
                vop(lambda e: e.reciprocal(out=st4q[:, 2:3], in_=st4q[:, 1:2]), [r_st4q], [r_st4q])
                stt(u2f[:], hb[:], st4q[:, 2:3], rows[:, 1, :], ALU.mult, ALU.mult, r=[r_hb, r_st4q, r_rows], w=[r_u2f])
                tt(u2b[:], u2f[:], rows[:, 2, :], ALU.add, r=[r_u2f, r_rows], w=[r_u2b])
                yield
                for kc in range(8):
                    tr(ptr[:, kc * 128:(kc + 1) * 128], u2b[:, kc * 128:(kc + 1) * 128], identb[:],
                       r=[r_u2b, r_identb], w=[r_ptr])
                yield
                act(u2T[:], ptr[:].rearrange("p (k t) -> p k t", t=128), AF.Copy, r=[r_ptr], w=[r_u2T])
                yield

                def qmm(q4):
                    pQ, r_pQ = PS[2 + q4 % 2]
                    for k4 in range(4):
                        hp = q4 * 4 + k4
                        for kc in range(8):
                            mm(pQ[:, k4 * 128:(k4 + 1) * 128], wq[:, kc, hp * 128:(hp + 1) * 128], u2T[:, kc, :],
                               start=(kc == 0), stop=(kc == 7), r=[r_wq, r_u2T], w=[r_pQ])

                def qev(q4):
                    pQ, r_pQ = PS[2 + q4 % 2]
                    act(qT[:, q4 * 4:(q4 + 1) * 4, :], pQ[:, :].rearrange("p (a t) -> p a t", t=128), AF.Copy,
                        r=[r_pQ], w=[r_qT])

                def smm(b4):
                    pS, r_pS = PS[2 + b4 % 2]
                    for k4 in range(4):
                        hp = b4 * 4 + k4
                        mm(pS[:, k4 * 128:(k4 + 1) * 128], qT[:, hp, :], skb[:, hp, :], r=[r_qT, r_skb], w=[r_pS])

                def sev(b4):
                    pS, r_pS = PS[2 + b4 % 2]
                    act(ssb[:, b4 * 4:(b4 + 1) * 4, :], pS[:, :].rearrange("p (a n) -> p a n", n=128), AF.Copy,
                        r=[r_pS], w=[r_ssb])

                qmm(0)
                yield
                qmm(1)
                qev(0)
                yield
                qmm(2)
                qev(1)
                yield
                qmm(3)
                qev(2)
                yield
                qev(3)
                yield
                smm(0)
                yield
                smm(1)
                sev(0)
                yield
                smm(2)
                sev(1)
                yield
                smm(3)
                sev(2)
                yield
                sev(3)
                yield
                for hh in range(2):
                    L8 = list(range(hh * 8, hh * 8 + 8))
                    for hp in L8:
                        vop(lambda e, hp=hp: e.max(out=sv[:, hp, 0:8], in_=ssb[:, hp, :]), [r_ssb], [r_svh[hp]])
                    for hp in L8:
                        vop(lambda e, hp=hp: e.max_index(out=si[:, hp, 0:8], in_max=sv[:, hp, 0:8], in_values=ssb[:, hp, :]),
                            [r_ssb, r_svh[hp]], [r_sih[hp]])
                    for hp in L8:
                        vop(lambda e, hp=hp: e.match_replace(out=swk[:, hp % 8, :], in_to_replace=sv[:, hp, 0:8],
                                                             in_values=ssb[:, hp, :], imm_value=-1e30),
                            [r_ssb, r_svh[hp]], [r_swkh[hp % 8]])
                    yield
                    for hp in L8:
                        vop(lambda e, hp=hp: e.max(out=sv[:, hp, 8:16], in_=swk[:, hp % 8, :]), [r_swkh[hp % 8]], [r_svh[hp]])
                    for hp in L8:
                        vop(lambda e, hp=hp: e.max_index(out=si[:, hp, 8:16], in_max=sv[:, hp, 8:16], in_values=swk[:, hp % 8, :]),
                            [r_swkh[hp % 8], r_svh[hp]], [r_sih[hp]])
                    yield
                cp(sif[:], si[:], r=r_sih, w=[r_sif])
                sv4 = sv[:].rearrange("p (h a) k -> p h a k", a=2)
                for h in range(8):
                    tt(cand[:, h, :].rearrange("p (i j) -> p i j", j=16),
                       sv4[:, h, 0, :].unsqueeze(2).broadcast_to([128, 16, 16]),
                       sv4[:, h, 1, :].unsqueeze(1).broadcast_to([128, 16, 16]), ALU.add,
                       r=[r_svh[2 * h], r_svh[2 * h + 1]], w=[r_candh[h]])
                yield
                for hh in range(2):
                    H4 = list(range(hh * 4, hh * 4 + 4))
                    for h in H4:
                        vop(lambda e, h=h: e.max(out=cv[:, h, 0:8], in_=cand[:, h, :]), [r_candh[h]], [r_cvh[h]])
                    for h in H4:
                        vop(lambda e, h=h: e.max_index(out=ci[:, h, 0:8], in_max=cv[:, h, 0:8], in_values=cand[:, h, :]),
                            [r_candh[h], r_cvh[h]], [r_cih_[h]])
                    for h in H4:
                        vop(lambda e, h=h: e.match_replace(out=cwk[:, h % 4, :], in_to_replace=cv[:, h, 0:8],
                                                           in_values=cand[:, h, :], imm_value=-1e30),
                            [r_candh[h], r_cvh[h]], [r_cwkh[h % 4]])
                    for h in H4:
                        vop(lambda e, h=h: e.max(out=cv[:, h, 8:16], in_=cwk[:, h % 4, :]), [r_cwkh[h % 4]], [r_cvh[h]])
                    for h in H4:
                        vop(lambda e, h=h: e.max_index(out=ci[:, h, 8:16], in_max=cv[:, h, 8:16], in_values=cwk[:, h % 4, :]),
                            [r_cwkh[h % 4], r_cvh[h]], [r_cih_[h]])
                    yield
                vop(lambda e: e.tensor_single_scalar(out=cih[:], in_=ci[:], scalar=4, op=ALU.logical_shift_right),
                    r_cih_, [r_cih])
                vop(lambda e: e.tensor_single_scalar(out=cil[:], in_=ci[:], scalar=15, op=ALU.bitwise_and),
                    r_cih_, [r_cil])
                cp(cif[:, 0, :], cih[:].rearrange("p h k -> p (h k)"), r=[r_cih], w=[r_cif])
                cp(cif[:, 1, :], cil[:].rearrange("p h k -> p (h k)"), r=[r_cil], w=[r_cif])
                tt(ex[:], cv[:], cv[:, :, 0:1].broadcast_to([128, 8, 16]), ALU.subtract, r=r_cvh, w=[r_ex])
                act(ex[:], ex[:], AF.Exp, r=[r_ex], w=[r_ex])
                yield
                sif4 = sif[:].rearrange("p (h a) k -> p h a k", a=2)
                for a in range(2):
                    for hh in range(2):
                        hs_ = slice(hh * 64, hh * 64 + 64)
                        tt(oh[:], cif[:, a, hs_].unsqueeze(2).broadcast_to([128, 64, 16]),
                           io16[:].unsqueeze(1).broadcast_to([128, 64, 16]), ALU.is_equal,
                           r=[r_cif, r_io16], w=[r_oh])
                        tt(oh[:].rearrange("p (h k) i -> p h k i", h=4), oh[:].rearrange("p (h k) i -> p h k i", h=4),
                           sif4[:, hh * 4:hh * 4 + 4, a, :].unsqueeze(2).broadcast_to([128, 4, 16, 16]), ALU.mult,
                           r=[r_oh, r_sif], w=[r_oh])
                        vop(lambda e, a=a, hs_=hs_: e.tensor_reduce(out=ef[:, a, hs_], in_=oh[:], axis=AX.X, op=ALU.add),
                            [r_oh], [r_ef])
                        yield
                stt(ef[:, 0, :], ef[:, 0, :], 128.0, ef[:, 1, :], ALU.mult, ALU.add, r=[r_ef], w=[r_ef])
                cp(eidx[:], ef[:, 0, :], r=[r_ef], w=[r_eidx])
                vop(lambda e: e.tensor_reduce(out=sm[:], in_=ex[:], axis=AX.X, op=ALU.add), [r_ex], [r_sm])
                vop(lambda e: e.reciprocal(out=sm[:], in_=sm[:]), [r_sm], [r_sm])
                tt(gate[:].rearrange("p (h k) -> p h k", h=8), ex[:],
                   sm[:].unsqueeze(2).broadcast_to([128, 8, 16]), ALU.mult, r=[r_ex, r_sm], w=[r_gate])
                yield

            def gen_G(tb_):
                hb, r_hb = hbs[tb_ % 2]
                u2b, r_u2b = u2bs[tb_ % 2]
                eidx, r_eidx = eidxs[tb_ % 2]
                gate, r_gate = gates[tb_ % 2]
                ap_ = apre[tb_ % 2][0]
                wt = wts[tb_ % 2][0]
                pa, r_pa = PS[0 if tb_ % 2 == 0 else 4]
                pb, r_pb = PS[1 if tb_ % 2 == 0 else 5]
                def stage_a(g):
                    bufs = []
                    for jj in range(GS):
                        j = g * GS + jj
                        gc = gstate["gcount"]
                        gstate["gcount"] += 1
                        ug, r_ug = uvg[gc % NR]
                        bufs.append((ug, r_ug))
                        S.dma("pool", lambda e, ug=ug, eidx=eidx, j=j: e.indirect_dma_start(
                            out=ug[:], out_offset=None, in_=uvb_s,
                            in_offset=bass.IndirectOffsetOnAxis(ap=eidx[:, j:j + 1], axis=0)),
                            [r_eidx], [r_ug])
                        stt(junkg[:], ug[:, 0:D], 1.0, u2b[:], ALU.mult, ALU.mult, r=[r_ug, r_u2b],
                            w=[r_aps[tb_ % 2][j]], accum_out=ap_[:, j:j + 1])
                    return bufs

                def stage_gelu(g):
                    gsl = slice(g * GS, (g + 1) * GS)
                    act(wt[:, gsl], ap_[:, gsl], AF.Gelu, r=r_aps[tb_ % 2][g * GS:(g + 1) * GS], w=[r_wtg[tb_ % 2][g]])

                def stage_mult(g):
                    r_wt = r_wtg[tb_ % 2][g]
                    for jj in range(GS):
                        j = g * GS + jj
                        act(wt[:, j:j + 1], wt[:, j:j + 1], AF.Copy, r=[r_wt, r_gate], w=[r_wt], scale=gate[:, j:j + 1])

                def stage_c(g, bufs):
                    r_wt = r_wtg[tb_ % 2][g]
                    dg, r_dg = diag[g % 2]
                    for jj in range(GS):
                        j = g * GS + jj
                        act(dg[:, jj, :], identb[:], AF.Copy, r=[r_identb, r_wt], w=[r_dg], scale=wt[:, j:j + 1])
                    for jj in range(GS):
                        j = g * GS + jj
                        ug, r_ug = bufs[jj]
                        mm(pa[:, :], dg[:, jj, :], ug[:, D:D + 512], start=(j == 0), stop=(j == 127), r=[r_dg, r_ug], w=[r_pa])
                        mm(pb[:, :], dg[:, jj, :], ug[:, D + 512:2 * D], start=(j == 0), stop=(j == 127), r=[r_dg, r_ug], w=[r_pb])

                allb = {}
                prev_tb = pend["tb"]
                for g in range(NG + 1):
                    if prev_tb is not None and g == 2:
                        block_end_1(prev_tb)
                    if prev_tb is not None and g == 4:
                        block_end_2(prev_tb)
                    if g < NG:
                        allb[g] = stage_a(g)
                    if g >= 1:
                        stage_mult(g - 1)
                        stage_c(g - 1, allb.pop(g - 1))
                    if g < NG:
                        stage_gelu(g)
                    yield
                pend["tb"] = tb_

            def block_end_1(tb_):
                hb, r_hb = hbs[tb_ % 2]
                pa, r_pa = PS[0 if tb_ % 2 == 0 else 4]
                pb, r_pb = PS[1 if tb_ % 2 == 0 else 5]
                tt(h2[:, 0:512], pa[:, :], rows[:, 3, 0:512], ALU.mult, r=[r_pa, r_rows], w=[r_h2])
                tt(h2[:, 512:1024], pb[:, :], rows[:, 3, 512:1024], ALU.mult, r=[r_pb, r_rows], w=[r_h2])
                tt(h2[:], h2[:], hb[:], ALU.add, r=[r_h2, r_hb], w=[r_h2])
                act(junkq[:], h2[:], AF.Square, r=[r_h2], w=[r_st4], accum_out=st4[:, 0:1])
                act(st4[:, 1:2], st4[:, 0:1], AF.Sqrt, r=[r_st4], w=[r_st4], scale=1.0 / D, bias=EPS)

            def block_end_2(tb_):
                vop(lambda e: e.reciprocal(out=st4[:, 2:3], in_=st4[:, 1:2]), [r_st4], [r_st4])
                ot, r_ot = outs[0]
                stt(ot[:], h2[:], st4[:, 2:3], fngr[:], ALU.mult, ALU.mult, r=[r_h2, r_st4, r_fngr], w=[r_ot])
                dma(out_d[tb_ * 128:(tb_ + 1) * 128, :], ot[:], r=[r_ot], w=[], sem_res=r_ot)

            pend = {"tb": None}
            for _ in gen_Q(0):
                pass
            for tb_ in range(32):
                gq = gen_Q(tb_ + 1) if tb_ < 31 else None
                for gi, _ in enumerate(gen_G(tb_)):
                    if gq is not None and gi >= 2:
                        next(gq, None)
                        if gi < 5:
                            next(gq, None)
                if gq is not None:
                    for _ in gq:
                        pass
            block_end_1(pend["tb"])
            block_end_2(pend["tb"])
            S.phase_end()
    return nc, S


_CONST = {}


def _consts():
    if _CONST:
        return _CONST
    bf = ml_dtypes.bfloat16
    n = np.arange(128)
    ang = 2.0 * np.pi * ((n[:, None] * n[None, :]) % 128) / 128.0
    _CONST["csw"] = np.concatenate([np.cos(ang), np.sin(ang)], axis=1).astype(bf)
    l = np.arange(4096, dtype=np.int64)
    angL = 2.0 * np.pi * ((l[:, None] * l[None, :]) % 4096) / 4096.0
    cl = np.cos(angL).astype(np.float32)
    sl = np.sin(angL).astype(np.float32)
    lay = lambda m: np.ascontiguousarray(m.reshape(32, 128, 32, 128).transpose(2, 1, 0, 3)).astype(bf)
    _CONST["cl"] = lay(cl)
    _CONST["sl"] = lay(sl)
    _CONST["identb"] = np.eye(128, dtype=np.float32).astype(bf)
    _CONST["identf"] = np.eye(128, dtype=np.float32)
    j = n[:, None]
    i = n[None, :]
    same = (j // 64) == (i // 64)
    mf = (same & (j <= i)).astype(np.float32)
    mb = (same & (j >= i)).astype(np.float32)
    _CONST["masks"] = np.ascontiguousarray(np.stack([mf, mb], axis=1))
    _CONST["iota16"] = np.tile(np.arange(16, dtype=np.float32)[None, :], (128, 1))
    rm = np.ones((128, NT), np.float32)
    rm[:, 0::64] = 0.0
    _CONST["rmask"] = rm.astype(bf)
    return _CONST


def make_in_maps(x, c, ctx, c_ctx, w_ada, b_ada, norm_mix_g, norm_ffn_g, w_in, hg_lb_f, hg_lb_b, hg_norm_g,
                 w_hg_out, w_ft_out, w_out, peer_w_q, peer_sub_keys, peer_u, peer_v, final_norm_g):
    f = lambda a: np.ascontiguousarray(np.asarray(a, dtype=np.float32))
    C = _consts()
    pl = lambda v, k: f(np.asarray(v).reshape(k, 128).T)
    shared = {
        "w_ada": f(w_ada[0]), "b_ada_p": pl(b_ada[0], 48), "gmix_p": pl(norm_mix_g[0], 8),
        "gffn_p": pl(norm_ffn_g[0], 8), "w_in": f(w_in[0]),
        "lbf": f(np.asarray(hg_lb_f).reshape(2, 4, 128).transpose(2, 1, 0)),
        "lbb": f(np.asarray(hg_lb_b).reshape(2, 4, 128).transpose(2, 1, 0)),
        "hgn_p": f(np.asarray(hg_norm_g[0]).T), "w_hg_out": f(w_hg_out[0]), "w_ft_out": f(w_ft_out[0]),
        "w_out": f(w_out[0]), "w_q": f(peer_w_q[0]),
        "skT": f(np.asarray(peer_sub_keys[0]).reshape(16, 128, 128).transpose(2, 0, 1)),
        "peer_u": f(peer_u[0]), "peer_v": f(peer_v[0]), "fng": f(final_norm_g),
    }
    shared.update(C)
    maps = []
    c = np.asarray(c)
    c_ctx = np.asarray(c_ctx)
    for b in range(8):
        m = dict(shared)
        m["x"] = f(x[b])
        m["ctx"] = f(ctx[b])
        m["cc"] = f(np.stack([c[b].reshape(8, 128).T, c_ctx.reshape(8, 128).T], axis=2))
        maps.append(m)
    return maps


_NC = None


def kernel(**inputs):
    global _NC
    if _NC is None:
        _NC = build()[0]
    maps = make_in_maps(**inputs)
    res = run_bass_kernel_spmd(_NC, maps, core_ids=list(range(8)))
    return np.stack([np.asarray(r["out"], dtype=np.float32) for r in res.results], axis=0)
```

```python
import numpy as np
import ml_dtypes
import concourse.bass as bass
import concourse.mybir as mybir
from concourse.bass_utils import run_bass_kernel_spmd
from contextlib import ExitStack

F32 = mybir.dt.float32
BF16 = mybir.dt.bfloat16
U32 = mybir.dt.uint32
I32 = mybir.dt.int32
AF = mybir.ActivationFunctionType
ALU = mybir.AluOpType
AX = mybir.AxisListType

D = 1024
L = 4096
CTX = 256
NT = L + CTX
NB = NT // 128
NCH = NT // 64
EPS = 1e-6


class Res:
    __slots__ = ("name", "w", "r", "dsem", "dcnt", "bg")

    def __init__(self, name):
        self.name = name
        self.w = None
        self.r = []
        self.dsem = None
        self.dcnt = 0
        self.bg = False


class Sched:
    ENG = ["pe", "dve", "act", "pool", "sp"]

    def __init__(self, nc, es):
        self.nc = nc
        self.es = es
        self.prog = {e: [] for e in self.ENG}
        self.cnt = {e: 0 for e in self.ENG}
        self.sem = {e: es.enter_context(nc.semaphore("sem_" + e)) for e in self.ENG}
        self.seen = {e: {} for e in self.ENG}
        self.all = []
        self.excl = set()
        self.ndsem = 0

    def res(self, name=None):
        r = Res(name or ("r%d" % len(self.all)))
        self.all.append(r)
        return r

    def _semof(self, key):
        return self.sem[key] if isinstance(key, str) else key.dsem

    def _waits(self, e, reads, writes, skip_key=None):
        need = {}

        def add2(ev):
            k, v = ev
            if k == "pe" and e == "pe":
                return
            if need.get(k, 0) < v:
                need[k] = v

        for r in reads:
            if r.w is not None:
                add2(r.w)
        for w in writes:
            if w.w is not None and w.w[0] is not skip_key:
                add2(w.w)
            for ev in w.r:
                add2(ev)
        out = []
        seen = self.seen[e]
        for k, v in need.items():
            if seen.get(k, 0) >= v:
                continue
            seen[k] = v
            out.append((self._semof(k), v))
        return out

    def _update(self, ev, reads, writes):
        for r in reads:
            r.r.append(ev)
        for w in writes:
            w.w = ev
            w.r = []

    def op(self, e, fn, reads=(), writes=()):
        if self.excl:
            ex = [r for r in reads if r in self.excl and r not in writes]
            if ex:
                writes = list(writes) + ex
                reads = [r for r in reads if r not in self.excl]
        waits = self._waits(e, reads, writes)
        self.cnt[e] += 1
        ev = (e, self.cnt[e])
        self.prog[e].append((waits, fn, self.sem[e], 1))
        self._update(ev, reads, writes)

    def dma(self, e, fn, reads=(), writes=(), sem_res=None):
        sr = sem_res or (writes[0] if writes else reads[0])
        if sr.dsem is None:
            sr.dsem = self.es.enter_context(self.nc.semaphore("dsem%d" % self.ndsem))
            self.ndsem += 1
        waits = self._waits(e, reads, writes, skip_key=sr)
        sr.dcnt += 16
        ev = (sr, sr.dcnt)
        self.prog[e].append((waits, fn, sr.dsem, 16))
        self._update(ev, reads, writes)

    def phase_end(self):
        waits = self._waits("sp", [], [r for r in self.all if not r.bg])
        self.prog["sp"].append((waits, None, None, 0))
        nc = self.nc
        engs = {"pe": "tensor", "dve": "vector", "act": "scalar", "pool": "gpsimd", "sp": "sync"}
        with nc.Block() as block:
            for e in self.ENG:
                prog = self.prog[e]

                def body(eng, prog=prog):
                    for waits, fn, sem, inc in prog:
                        for s, v in waits:
                            eng.wait_ge(s, v)
                        if fn is not None:
                            fn(eng).then_inc(sem, inc)

                getattr(block, engs[e])(body)
        self.prog = {e: [] for e in self.ENG}
        for e in self.ENG:
            for k in self.ENG:
                self.seen[e][k] = self.cnt[k]
            for r in self.all:
                if r.dsem is not None and not r.bg:
                    self.seen[e][r] = r.dcnt


def build(debug=False, stop_after=None):
    nc = bass.Bass("TRN2", target_bir_lowering=False)

    def din(name, shape, dt=F32):
        return nc.dram_tensor(name, shape, dt, kind="ExternalInput").ap()

    def dscr(name, shape, dt):
        return nc.dram_tensor(name, shape, dt, kind="ExternalOutput" if debug else "Internal").ap()

    x_d = din("x", [L, D])
    ctx_d = din("ctx", [CTX, D])
    cc_d = din("cc", [128, 8, 2])
    wada_d = din("w_ada", [D, 6 * D])
    bada_d = din("b_ada_p", [128, 48])
    gmix_d = din("gmix_p", [128, 8])
    gffn_d = din("gffn_p", [128, 8])
    win_d = din("w_in", [D, 5120])
    lbf_d = din("lbf", [128, 4, 2])
    lbb_d = din("lbb", [128, 4, 2])
    hgn_d = din("hgn_p", [128, 4])
    whg_d = din("w_hg_out", [512, D])
    wft_d = din("w_ft_out", [512, D])
    wout_d = din("w_out", [D, D])
    wq_d = din("w_q", [D, 2048])
    skT_d = din("skT", [128, 16, 128])
    pu_d = din("peer_u", [16384, D])
    pv_d = din("peer_v", [16384, D])
    fng_d = din("fng", [D])
    csw_d = din("csw", [128, 256], BF16)
    cl_d = din("cl", [32, 128, 32, 128], BF16)
    sl_d = din("sl", [32, 128, 32, 128], BF16)
    identb_d = din("identb", [128, 128], BF16)
    identf_d = din("identf", [128, 128])
    mask_d = din("masks", [128, 2, 128])
    iota_d = din("iota16", [128, 16])
    rmask_d = din("rmask", [128, NT], BF16)
    out_d = nc.dram_tensor("out", [L, D], F32, kind="ExternalOutput").ap()

    rows_d = dscr("rows_s", [32, 128], F32)
    uT_dbg = dscr("uT_s", [128, 8, NT], BF16) if debug else None
    zq_s = dscr("zq_s", [4, 128, NT], BF16)
    zf_s = dscr("zf_s", [8, 128, NT], F32)
    zg_s = dscr("zg_s", [4, 128, NT], BF16)
    zft_s = dscr("zft_s", [4, 128, NT], BF16)
    zgh_s = dscr("zgh_s", [16, 128, NT], BF16)
    v_s = dscr("v_s", [NB, 128, 512], BF16)
    og_s = dscr("og_s", [4, 128, L], BF16)
    of_dbg = dscr("of_s", [4, 128, L], F32) if debug else None
    yT_s = dscr("yT_s", [4, 128, L], BF16)
    h1_s = dscr("h1_s", [L, D], F32)
    u2_s = dscr("u2_s", [L, D], F32)
    uvb_s = nc.dram_tensor("uvb_s", [16384, 2 * D], BF16, kind="Internal").ap()
    eidx_dbg = dscr("eidx_s", [128, 32, 128], I32) if debug else None
    gate_dbg = dscr("gate_s", [128, 32, 128], F32) if debug else None

    with ExitStack() as eg:
        S = Sched(nc, eg)

        def mm(out, lhsT, rhs, start=True, stop=True, r=(), w=(), sgc=False):
            S.op("pe", lambda e: e.matmul(out, lhsT=lhsT, rhs=rhs, start=start, stop=stop,
                                          skip_group_check=sgc), r, w)

        def tr(out, in_, ident, r=(), w=()):
            S.op("pe", lambda e: e.transpose(out, in_, ident), r, w)

        def act(out, in_, func, r=(), w=(), **kw):
            S.op("act", lambda e: e.activation(out=out, in_=in_, func=func, **kw), r, w)

        def tt(out, in0, in1, op, r=(), w=(), eng="dve"):
            S.op(eng, lambda e: e.tensor_tensor(out=out, in0=in0, in1=in1, op=op), r, w)

        def ts(out, in0, s1, op0, s2=None, op1=None, r=(), w=(), eng="dve"):
            if op1 is None:
                S.op(eng, lambda e: e.tensor_scalar(out=out, in0=in0, scalar1=s1, scalar2=None, op0=op0), r, w)
            else:
                S.op(eng, lambda e: e.tensor_scalar(out=out, in0=in0, scalar1=s1, scalar2=s2, op0=op0, op1=op1), r, w)

        def stt(out, in0, scalar, in1, op0, op1, r=(), w=(), accum_out=None):
            S.op("dve", lambda e: e.scalar_tensor_tensor(out=out, in0=in0, scalar=scalar, in1=in1, op0=op0,
                                                         op1=op1, accum_out=accum_out), r, w)

        def cp(out, in_, r=(), w=(), eng="dve"):
            S.op(eng, lambda e: e.tensor_copy(out=out, in_=in_), r, w)

        def dma(out, in_, r=(), w=(), q="sp", sem_res=None):
            S.dma(q, lambda e: e.dma_start(out=out, in_=in_), r, w, sem_res=sem_res)

        def gsb(name, shape, dt):
            return eg.enter_context(nc.sbuf_tensor("s_" + name, shape, dt)), S.res(name)

        PS = []
        for i in range(7):
            PS.append((eg.enter_context(nc.psum_tensor("ps%d" % i, [128, 512], F32)), S.res("ps%d" % i)))
        ptr, r_ptr = eg.enter_context(nc.psum_tensor("ptr", [128, 1024], BF16)), S.res("ptr")
        S.excl = set([p[1] for p in PS] + [r_ptr])

        identb, r_identb = gsb("identb", [128, 128], BF16)
        identf, r_identf = gsb("identf", [128, 128], F32)
        onesf, r_onesf = gsb("onesf", [128, 128], F32)
        modP, r_modP = gsb("modP", [128, 96], F32)
        vecP, r_vecP = gsb("vecP", [128, 64], F32)
        lbT, r_lbT = gsb("lbT", [128, 16], F32)
        hgn, r_hgn = gsb("hgn", [128, 4], F32)
        dma(identb[:], identb_d, w=[r_identb])
        dma(identf[:], identf_d, w=[r_identf])
        dma(hgn[:], hgn_d, w=[r_hgn])
        S.op("pool", lambda e: e.memset(onesf[:], 1.0), (), [r_onesf])
        onec, r_onec = gsb("onec", [128, 1], F32)
        S.op("pool", lambda e: e.memset(onec[:], 1.0), (), [r_onec])
        epsc, r_epsc = gsb("epsc", [128, 1], F32)
        S.op("pool", lambda e: e.memset(epsc[:], EPS), (), [r_epsc])

        with ExitStack() as es:
            def sb(name, shape, dt):
                return es.enter_context(nc.sbuf_tensor("s_" + name, shape, dt)), S.res(name)
            cc, r_cc = sb("cc", [128, 8, 2], F32)
            scc, r_scc = sb("scc", [128, 8, 2], F32)
            bada, r_bada = sb("bada", [128, 48], F32)
            gmix, r_gmix = sb("gmix", [128, 8], F32)
            gffn, r_gffn = sb("gffn", [128, 8], F32)
            lbin, r_lbin = sb("lbin", [128, 2, 4, 2], F32)
            lbd, r_lbd = sb("lbd", [128, 8], F32)
            tmp8, r_tmp8 = sb("tmp8", [128, 8], F32)
            rowsrc, r_rowsrc = sb("rowsrc", [32, 128], F32)
            slabs = [sb("slab%d" % i, [128, 8, 512], F32) for i in range(2)]
            dma(cc[:], cc_d, w=[r_cc])
            dma(bada[:], bada_d, w=[r_bada])
            dma(gmix[:], gmix_d, w=[r_gmix])
            dma(gffn[:], gffn_d, w=[r_gffn])
            dma(lbin[:, 0], lbf_d, w=[r_lbin])
            dma(lbin[:, 1], lbb_d, w=[r_lbin])
            act(scc[:], cc[:], AF.Silu, r=[r_cc], w=[r_scc])
            pmod, r_pmod = PS[0]
            wv = wada_d.rearrange("(kc p) n -> p kc n", p=128)
            for s in range(12):
                slab, r_slab = slabs[s % 2]
                dma(slab[:], wv[:, :, s * 512:(s + 1) * 512], w=[r_slab])
                for jj in range(4):
                    j = s * 4 + jj
                    for kc in range(8):
                        mm(pmod[:, 2 * j:2 * j + 2], slab[:, kc, jj * 128:(jj + 1) * 128], scc[:, kc, :],
                           start=(kc == 0), stop=(kc == 7), r=[r_slab, r_scc], w=[r_pmod])
            pm = pmod[:, 0:96].rearrange("p (j t) -> p j t", t=2)
            tt(modP[:, 0:48], pm[:, :, 0], bada[:], ALU.add, r=[r_pmod, r_bada], w=[r_modP])
            tt(modP[:, 48:96], pm[:, :, 1], bada[:], ALU.add, r=[r_pmod, r_bada], w=[r_modP])
            ts(tmp8[:], modP[:, 8:16], 1.0, ALU.add, r=[r_modP], w=[r_tmp8])
            tt(vecP[:, 0:8], tmp8[:], gmix[:], ALU.mult, r=[r_tmp8, r_gmix], w=[r_vecP])
            cp(vecP[:, 8:16], modP[:, 0:8], r=[r_modP], w=[r_vecP])
            ts(tmp8[:], modP[:, 56:64], 1.0, ALU.add, r=[r_modP], w=[r_tmp8])
            tt(vecP[:, 16:24], tmp8[:], gmix[:], ALU.mult, r=[r_tmp8, r_gmix], w=[r_vecP])
            cp(vecP[:, 24:32], modP[:, 48:56], r=[r_modP], w=[r_vecP])
            cp(vecP[:, 32:40], modP[:, 16:24], r=[r_modP], w=[r_vecP])
            ts(tmp8[:], modP[:, 32:40], 1.0, ALU.add, r=[r_modP], w=[r_tmp8])
            tt(vecP[:, 40:48], tmp8[:], gffn[:], ALU.mult, r=[r_tmp8, r_gffn], w=[r_vecP])
            cp(vecP[:, 48:56], modP[:, 24:32], r=[r_modP], w=[r_vecP])
            cp(vecP[:, 56:64], modP[:, 40:48], r=[r_modP], w=[r_vecP])
            pT, r_pT = PS[1]
            tr(pT[0:32, 0:128], vecP[:, 32:64], identf[:], r=[r_vecP, r_identf], w=[r_pT])
            cp(rowsrc[:], pT[0:32, 0:128], r=[r_pT], w=[r_rowsrc])
            r_rowsd = S.res("rows_d")
            dma(rows_d, rowsrc[:], r=[r_rowsrc], w=[r_rowsd])
            lv = lbin[:].rearrange("p a h t -> p (a h) t")
            tt(lbd[:], lv[:, :, 0], lv[:, :, 1], ALU.subtract, r=[r_lbin], w=[r_lbd])
            lbT4 = lbT[:].rearrange("p (a b h) -> p a b h", a=2, b=2)
            act(lbT4[:, :, 0, :], lbd[:].rearrange("p (a h) -> p a h", a=2), AF.Sigmoid, r=[r_lbd], w=[r_lbT])
            ts(lbT4[:, :, 1, :], lbT4[:, :, 0, :], -1.0, ALU.mult, 1.0, ALU.add, r=[r_lbT], w=[r_lbT])
            S.phase_end()
        if stop_after == "A":
            return nc, S

        r_tbg = [S.res("tbg%d" % i) for i in range(4)]

        def gen_T():
            TR = 1024
            for c in range(16384 // TR):
                rs_ = slice(c * TR, (c + 1) * TR)
                dma(uvb_s[rs_, 0:D], pu_d[rs_, :], q="pool", sem_res=r_tbg[c % 4])
                dma(uvb_s[rs_, D:2 * D], pv_d[rs_, :], q="pool", sem_res=r_tbg[c % 4])
                yield
        gT = gen_T()

        with ExitStack() as es:
            def sb(name, shape, dt):
                return es.enter_context(nc.sbuf_tensor("s_" + name, shape, dt)), S.res(name)
            uT, r_uT = sb("uT", [128, 8, NT], BF16)
            r_uTb = [S.res("uTb%d" % b) for b in range(NB)]
            xts = [sb("xt%d" % i, [128, D], F32) for i in range(2)]
            xns = [sb("xn%d" % i, [128, D], BF16) for i in range(2)]
            junk, r_junk = sb("junk", [128, D], BF16)
            st4, r_st4 = sb("st4", [128, 4], F32)
            tmpu, r_tmpu = sb("tmpu", [128, 8, 128], F32)
            for blk in range(NB):
                xt, r_xt = xts[blk % 2]
                xn, r_xn = xns[blk % 2]
                src = ctx_d[blk * 128:(blk + 1) * 128, :] if blk < 2 else x_d[(blk - 2) * 128:(blk - 1) * 128, :]
                dma(xt[:], src, w=[r_xt])
                act(junk[:], xt[:], AF.Square, r=[r_xt], w=[r_junk, r_st4], accum_out=st4[:, 0:1])
                act(st4[:, 1:2], st4[:, 0:1], AF.Sqrt, r=[r_st4], w=[r_st4], scale=1.0 / D, bias=EPS)
                S.op("dve", lambda e: e.reciprocal(out=st4[:, 2:3], in_=st4[:, 1:2]), [r_st4], [r_st4])
                act(xn[:], xt[:], AF.Copy, r=[r_xt, r_st4], w=[r_xn], scale=st4[:, 2:3])
                for kc in range(8):
                    tr(ptr[:, kc * 128:(kc + 1) * 128], xn[:, kc * 128:(kc + 1) * 128], identb[:],
                       r=[r_xn, r_identb], w=[r_ptr])
                vo = 16 if blk < 2 else 0
                pv = ptr[:].rearrange("p (k t) -> p k t", t=128)
                tt(tmpu[:], pv, vecP[:, vo:vo + 8].unsqueeze(2).broadcast_to([128, 8, 128]), ALU.mult,
                   r=[r_ptr, r_vecP], w=[r_tmpu])
                tt(uT[:, :, blk * 128:(blk + 1) * 128], tmpu[:],
                   vecP[:, vo + 8:vo + 16].unsqueeze(2).broadcast_to([128, 8, 128]), ALU.add,
                   r=[r_tmpu, r_vecP], w=[r_uTb[blk]])
            if debug:
                dma(uT_dbg, uT[:], r=r_uTb, w=[])
            wsts = [sb("wst%d" % i, [128, 8, 128], F32) for i in range(2)]
            wbfs = [sb("wbf%d" % i, [128, 8, 128], BF16) for i in range(2)]
            zf32 = [sb("zf32_%d" % i, [128, NT], F32) for i in range(2)]
            zb16 = [sb("zb16_%d" % i, [128, NT], BF16) for i in range(2)]
            wview = win_d.rearrange("(kc p) n -> p kc n", p=128)
            tiles = [(t * 512, 512) for t in range(8)] + [(4096, 256)]
            plan = []
            for h in range(4):
                plan.append((h, AF.Copy, 128.0 ** -0.5, zq_s[h], False))
            for j in range(8):
                plan.append((4 + j, AF.Sigmoid, 1.0, zf_s[j], True))
            for h in range(4):
                plan.append((16 + h, AF.Silu, 1.0, zg_s[h], False))
            for h in range(4):
                plan.append((20 + h, AF.Copy, 1.0, zft_s[h], False))
            for j in range(16):
                plan.append((24 + j, AF.Sigmoid, 1.0, zgh_s[j], False))
            plan.sort(key=lambda p: {AF.Copy: 0, AF.Sigmoid: 1, AF.Silu: 2}[p[1]])
            nf = nb = 0
            r_zscr = S.res("zscr")
            def load_w(ci):
                wst, r_wst = wsts[ci % 2]
                cb = plan[ci][0]
                dma(wst[:], wview[:, :, cb * 128:(cb + 1) * 128], w=[r_wst])
            load_w(0)
            for ci, (cb, func, scale, dst, isf) in enumerate(plan):
                wst, r_wst = wsts[ci % 2]
                wbf, r_wbf = wbfs[ci % 2]
                cp(wbf[:], wst[:], r=[r_wst], w=[r_wbf])
                if ci + 1 < len(plan):
                    load_w(ci + 1)
                if isf:
                    zst, r_zst = zf32[nf % 2]; nf += 1
                else:
                    zst, r_zst = zb16[nb % 2]; nb += 1
                for ti, (t0, tn) in enumerate(tiles):
                    pz, r_pz = PS[ti % 4]
                    for kc in range(8):
                        mm(pz[:, 0:tn], wbf[:, kc, :], uT[:, kc, t0:t0 + tn], start=(kc == 0), stop=(kc == 7),
                           r=[r_wbf] + r_uTb[t0 // 128:(t0 + tn) // 128], w=[r_pz])
                    act(zst[:, t0:t0 + tn], pz[:, 0:tn], func, r=[r_pz], w=[r_zst], scale=scale)
                    if 4 <= cb < 12:
                        jd, jh = (cb - 4) // 4, (cb - 4) % 4
                        ts(zst[:, t0:t0 + tn], zst[:, t0:t0 + tn], lbT[:, jd * 8 + 4 + jh:jd * 8 + 4 + jh + 1], ALU.mult,
                           lbT[:, jd * 8 + jh:jd * 8 + jh + 1], ALU.add, r=[r_zst, r_lbT], w=[r_zst])
                    if ti == 0 and ci % 6 == 2:
                        next(gT, None)
                dma(dst, zst[:], r=[r_zst], w=[], sem_res=r_zst)
            wV32, r_wV32 = sb("wV32", [128, 8, 512], F32)
            wV, r_wV = sb("wV", [128, 8, 512], BF16)
            Vsts = [sb("Vst%d" % i, [128, 512], BF16) for i in range(2)]
            dma(wV32[:], wview[:, :, 1536:2048], w=[r_wV32])
            cp(wV[:], wV32[:], r=[r_wV32], w=[r_wV], eng="pool")
            for blk in range(NB):
                pz, r_pz = PS[4 + blk % 3]
                for kc in range(8):
                    mm(pz[:, :], uT[:, kc, blk * 128:(blk + 1) * 128], wV[:, kc, :], start=(kc == 0), stop=(kc == 7),
                       r=[r_wV, r_uTb[blk]], w=[r_pz])
                Vst, r_Vst = Vsts[blk % 2]
                if blk % 2 == 0:
                    act(Vst[:], pz[:, :], AF.Copy, r=[r_pz], w=[r_Vst])
                else:
                    cp(Vst[:], pz[:, :], r=[r_pz], w=[r_Vst])
                dma(v_s[blk], Vst[:], r=[r_Vst], w=[], sem_res=r_Vst)
            S.phase_end()
        if stop_after == "P":
            return nc, S

        with ExitStack() as es:
            def sb(name, shape, dt):
                return es.enter_context(nc.sbuf_tensor("s_" + name, shape, dt)), S.res(name)
            rmask, r_rmask = sb("rmask", [128, NT], BF16)
            masks, r_masks = sb("masks", [128, 2, 128], F32)
            dma(rmask[:], rmask_d, w=[r_rmask])
            dma(masks[:], mask_d, w=[r_masks])
            Fb, r_F = sb("Fb", [128, NT], F32)
            Lb, r_L = sb("Lb", [128, NT], F32)
            Bb, r_B = sb("Bb", [128, NT], F32)
            Kb, r_K = sb("Kb", [128, NT], BF16)
            qb, r_q = sb("qb", [128, NT], BF16)
            QD, r_QD = sb("QD", [128, NT], BF16)
            QE, r_QE = sb("QE", [128, NT], BF16)
            KD, r_KD = sb("KD", [128, NT], BF16)
            KDt, r_KDt = sb("KDt", [128, NB, 128], BF16)
            Vb, r_V = sb("Vb", [128, NB, 128], BF16)
            sgg, r_sgg = sb("sgg", [128, NT], BF16)
            Sall, r_Sall = sb("Sall", [128, NCH, 128], BF16)
            Sf = [sb("Sf%d" % i, [128, 128], F32) for i in range(2)]
            oT, r_oT = sb("oT", [128, L], F32)
            edge, r_edge = sb("edge", [128, NCH], F32)
            dec, r_dec = sb("dec", [128, NCH], F32)
            Am_all, _ = sb("Am_all", [128, 32, 128], BF16)
            r_Amb = [S.res("Amb%d" % i) for i in range(32)]
            sq, r_sq = sb("sq", [128, 512], F32)
            rs, r_rs = sb("rs", [128, 512], F32)
            t1, r_t1 = sb("t1", [128, 512], F32)
            ogst, r_ogst = QE, r_QE
            r_ogscr = S.res("ogscr")
            r_pUs = [S.res("pU%d" % i) for i in range(8)]
            for h in range(4):
                dma(qb[:], zq_s[h], w=[r_q])
                dma(Vb[:], v_s[:, :, h * 128:(h + 1) * 128].rearrange("b p v -> p b v"), w=[r_V])
                dma(sgg[:], zg_s[h], w=[r_sgg])
                for dr in range(2):
                    lbc = lbT[:, dr * 8 + h:dr * 8 + h + 1]
                    omlc = lbT[:, dr * 8 + 4 + h:dr * 8 + 4 + h + 1]
                    dma(Fb[:], zf_s[dr * 4 + h], w=[r_F])
                    act(Kb[:], Fb[:], AF.Identity, r=[r_F], w=[r_K], scale=-1.0, bias=onec[:, 0:1])
                    act(Lb[:], Fb[:], AF.Ln, r=[r_F], w=[r_L])
                    S.op("dve", lambda e: e.tensor_tensor_scan(out=Bb[:], data0=rmask[:], data1=Lb[:], initial=0.0,
                                                               op0=ALU.mult, op1=ALU.add), [r_rmask, r_L], [r_B])
                    Bv = Bb[:].rearrange("p (c t) -> p c t", t=64)
                    cp(edge[:], Bv[:, :, 63], r=[r_B], w=[r_edge])
                    ebc = edge[:].unsqueeze(2).broadcast_to([128, NCH, 64])
                    if dr == 1:
                        tt(Lb[:], Lb[:], Bb[:], ALU.subtract, r=[r_L, r_B], w=[r_L])
                        tt(Bv, Lb[:].rearrange("p (c t) -> p c t", t=64), ebc, ALU.add, r=[r_L, r_edge], w=[r_B])
                    tt(Lb[:].rearrange("p (c t) -> p c t", t=64), Bv, ebc, ALU.subtract, r=[r_B, r_edge], w=[r_L])
                    act(Fb[:], Bb[:], AF.Exp, r=[r_B], w=[r_F])
                    tt(QD[:], Fb[:], qb[:], ALU.mult, r=[r_F, r_q], w=[r_QD])
                    act(Bb[:], Lb[:], AF.Exp, r=[r_L], w=[r_B])
                    tt(QE[:], Bb[:], qb[:], ALU.mult, r=[r_B, r_q], w=[r_QE])
                    act(Fb[:], Lb[:], AF.Exp, r=[r_L], w=[r_F], scale=-1.0)
                    tt(KD[:], Fb[:], Kb[:], ALU.mult, r=[r_F, r_K], w=[r_KD])
                    act(dec[:], edge[:], AF.Exp, r=[r_edge], w=[r_dec])
                    next(gT, None)
                    for b0 in range(0, NB, 8):
                        nb_ = min(8, NB - b0)
                        for bi in range(nb_):
                            blk = b0 + bi
                            tr(ptr[:, bi * 128:(bi + 1) * 128], KD[:, blk * 128:(blk + 1) * 128], identb[:],
                               r=[r_KD, r_identb], w=[r_ptr])
                        act(KDt[:, b0:b0 + nb_, :], ptr[:, 0:nb_ * 128].rearrange("p (b k) -> p b k", k=128), AF.Copy,
                            r=[r_ptr], w=[r_KDt])
                    order = list(range(NCH)) if dr == 0 else [3, 2, 1, 0] + list(range(NCH - 1, 3, -1))
                    S.op("pool", lambda e: e.memset(Sf[0][0][:], 0.0), (), [Sf[0][1]])
                    S.op("pool", lambda e, n0=order[0]: e.memset(Sall[:, n0, :], 0.0), (), [r_Sall])
                    def a_part(i):
                        blk = i + 2
                        pA, r_pA = PS[2 + i % 2]
                        cs = slice(blk * 128, (blk + 1) * 128)
                        mm(pA[:, 0:128], KD[:, cs], QE[:, cs], r=[r_KD, r_QE], w=[r_pA])
                        tt(Am_all[:, i, :], pA[:, 0:128], masks[:, dr, :], ALU.mult, r=[r_pA, r_masks], w=[r_Amb[i]])
                    n_a = 0
                    for idx in range(len(order) - 1):
                        n = order[idx]
                        blk, half = n // 2, n % 2
                        psl = slice(64 * half, 64 * half + 64)
                        pU, r_pU = PS[(0, 1, 4, 5)[idx % 4]]
                        pUs = pU[:, ((idx // 4) % 4) * 128:((idx // 4) % 4 + 1) * 128]
                        mm(pUs, KDt[psl, blk, :], Vb[psl, blk, :], r=[r_KDt, r_V], w=[r_pU])
                        cur, r_cur = Sf[idx % 2]
                        nxt, r_nxt = Sf[(idx + 1) % 2]
                        stt(nxt[:], cur[:], dec[:, n:n + 1], pUs, ALU.mult, ALU.add, r=[r_cur, r_dec, r_pU], w=[r_nxt])
                        act(Sall[:, order[idx + 1], :], nxt[:], AF.Copy, r=[r_nxt], w=[r_Sall])
                        if idx % 2 == 1 and n_a < 32:
                            a_part(n_a)
                            n_a += 1
                    while n_a < 32:
                        a_part(n_a)
                        n_a += 1
                    for blk in range(2, NB):
                        i = blk - 2
                        pO, r_pO = PS[4 + i % 2]
                        mm(pO[:, 0:128], Vb[:, blk, :], Am_all[:, i, :], start=True, stop=False, r=[r_V, r_Amb[i]], w=[r_pO],
                           sgc=True)
                        mm(pO[:, 0:64], Sall[:, 2 * blk, :], QD[:, blk * 128:blk * 128 + 64], start=False, stop=False,
                           r=[r_Sall, r_QD], w=[r_pO], sgc=True)
                        mm(pO[:, 64:128], Sall[:, 2 * blk + 1, :], QD[:, blk * 128 + 64:blk * 128 + 128], start=False,
                           stop=True, r=[r_Sall, r_QD], w=[r_pO], sgc=True)
                        if dr == 0:
                            act(oT[:, i * 128:(i + 1) * 128], pO[:, 0:128], AF.Copy, r=[r_pO], w=[r_oT])
                        else:
                            tt(oT[:, i * 128:(i + 1) * 128], oT[:, i * 128:(i + 1) * 128], pO[:, 0:128], ALU.add,
                               r=[r_pO, r_oT], w=[r_oT])
                    if debug and dr == 0:
                        r_ofd = S.res("ofd")
                        dma(of_dbg[h], oT[:], r=[r_oT], w=[r_ofd], sem_res=r_oT)
                for t in range(8):
                    tsl = slice(t * 512, (t + 1) * 512)
                    pN, r_pN = PS[6]
                    tt(sq[:], oT[:, tsl], oT[:, tsl], ALU.mult, r=[r_oT], w=[r_sq])
                    mm(pN[:, :], onesf[:], sq[:], r=[r_onesf, r_sq], w=[r_pN])
                    act(rs[:], pN[:, :], AF.Ln, r=[r_pN], w=[r_rs], scale=1.0 / 128, bias=epsc[:, 0:1])
                    act(rs[:], rs[:], AF.Exp, r=[r_rs], w=[r_rs], scale=-0.5)
                    tt(t1[:], oT[:, tsl], rs[:], ALU.mult, r=[r_oT, r_rs], w=[r_t1])
                    stt(ogst[:, tsl], t1[:], hgn[:, h:h + 1], sgg[:, CTX + t * 512:CTX + (t + 1) * 512], ALU.mult, ALU.mult,
                        r=[r_t1, r_hgn, r_sgg], w=[r_ogst])
                dma(og_s[h], ogst[:, 0:L], r=[r_ogst], w=[], sem_res=r_ogst)
            S.phase_end()
        if stop_after == "H":
            return nc, S

        with ExitStack() as es:
            def sb(name, shape, dt):
                return es.enter_context(nc.sbuf_tensor("s_" + name, shape, dt)), S.res(name)
            ftT, r_ftT = sb("ftT", [128, 4, L], BF16)
            csw, r_csw = sb("csw", [128, 256], BF16)
            Pc, r_Pc = sb("Pc", [128, 32, 512], BF16)
            Psn, r_Psn = sb("Psn", [128, 32, 512], BF16)
            CLs = [sb("CLs%d" % i, [128, 32, 128], BF16) for i in range(2)]
            SLs = [sb("SLs%d" % i, [128, 32, 128], BF16) for i in range(2)]
            Ytm = [sb("Ytm%d" % i, [128, 512], BF16) for i in range(2)]
            YT, r_YT = sb("YT", [128, 4, L], BF16)
            dma(csw[:], csw_d, w=[r_csw])
            for g in range(4):
                dma(ftT[:, g, :], zft_s[g][:, CTX:NT], w=[r_ftT])
            for lb in range(32):
                pa, r_pa = PS[(lb % 2) * 2]
                pb, r_pb = PS[(lb % 2) * 2 + 1]
                for g in range(4):
                    p_, r_p = (pa, r_pa) if g < 2 else (pb, r_pb)
                    mm(p_[:, (g % 2) * 256:(g % 2 + 1) * 256], ftT[:, g, lb * 128:(lb + 1) * 128], csw[:],
                       r=[r_ftT, r_csw], w=[r_p])
                for (p_, r_p, g0) in ((pa, r_pa, 0), (pb, r_pb, 2)):
                    pv = p_[:, :].rearrange("p (g t m) -> p g t m", g=2, t=2)
                    act(Pc[:, lb, g0 * 128:(g0 + 2) * 128].rearrange("p (g m) -> p g m", g=2), pv[:, :, 0, :], AF.Copy,
                        r=[r_p], w=[r_Pc])
                    ts(Psn[:, lb, g0 * 128:(g0 + 2) * 128].rearrange("p (g m) -> p g m", g=2), pv[:, :, 1, :], -1.0,
                       ALU.mult, r=[r_p], w=[r_Psn])
            r_ytd = S.res("ytd")
            for kb in range(32):
                cl, r_cl = CLs[kb % 2]
                sl, r_sl = SLs[kb % 2]
                dma(cl[:], cl_d[kb], w=[r_cl])
                dma(sl[:], sl_d[kb], w=[r_sl])
                pY, r_pY = PS[4 + kb % 2]
                for lb in range(32):
                    mm(pY[:, :], cl[:, lb, :], Pc[:, lb, :], start=(lb == 0), stop=False, r=[r_cl, r_Pc], w=[r_pY])
                for lb in range(32):
                    mm(pY[:, :], sl[:, lb, :], Psn[:, lb, :], start=False, stop=(lb == 31), r=[r_sl, r_Psn], w=[r_pY])
                ytm, r_ytm = Ytm[kb % 2]
                act(ytm[:], pY[:, :], AF.Copy, r=[r_pY], w=[r_ytm], scale=float((4096.0 * 128.0) ** -0.5))
                for g in range(4):
                    tr(ptr[:, g * 128:(g + 1) * 128], ytm[:, g * 128:(g + 1) * 128], identb[:], r=[r_ytm, r_identb],
                       w=[r_ptr])
                cp(YT[:, :, kb * 128:(kb + 1) * 128], ptr[:, 0:512].rearrange("p (g t) -> p g t", g=4), r=[r_ptr],
                   w=[r_YT])
            for g in range(4):
                dma(yT_s[g], YT[:, g, :], r=[r_YT], w=[], sem_res=r_YT)
            for _ in gT:
                pass
            for r_ in r_tbg:
                r_.bg = False
            S.phase_end()
        if stop_after == "F":
            return nc, S

        with ExitStack() as es:
            def sb(name, shape, dt):
                return es.enter_context(nc.sbuf_tensor("s_" + name, shape, dt)), S.res(name)
            rows, r_rows = sb("rowsM", [128, 4, D], F32)
            dma(rows[:].rearrange("p a d -> p (a d)"), rows_d.rearrange("a b -> (a b)").partition_broadcast(128), w=[r_rows])
            stg, r_stg = sb("stg", [128, 8, D], F32)
            whg, r_whg = sb("whg", [128, 4, D], BF16)
            wft, r_wft = sb("wft", [128, 4, D], BF16)
            wo, r_wo = sb("wo", [128, 8, D], BF16)
            dma(stg[:, 0:4, :], whg_d.rearrange("(h p) n -> p h n", p=128), w=[r_stg])
            cp(whg[:], stg[:, 0:4, :], r=[r_stg], w=[r_whg], eng="pool")
            dma(stg[:, 0:4, :], wft_d.rearrange("(h p) n -> p h n", p=128), w=[r_stg])
            cp(wft[:], stg[:, 0:4, :], r=[r_stg], w=[r_wft], eng="pool")
            dma(stg[:], wout_d.rearrange("(h p) n -> p h n", p=128), w=[r_stg])
            cp(wo[:], stg[:], r=[r_stg], w=[r_wo], eng="pool")
            ogs = [sb("og%d" % i, [128, 4, 512], BF16) for i in range(2)]
            yts = [sb("yt%d" % i, [128, 4, 512], BF16) for i in range(2)]
            sghs = [sb("sgh%d" % i, [128, 8, 512], BF16) for i in range(2)]
            sgfs = [sb("sgf%d" % i, [128, 8, 512], BF16) for i in range(2)]
            yT2, r_yT2 = sb("yT2", [128, 8, 512], BF16)
            ta, r_ta = sb("ta", [128, 512], F32)
            tb, r_tb = sb("tb", [128, 512], F32)
            xms = [sb("xm%d" % i, [128, D], F32) for i in range(2)]
            h1s = [sb("h1_%d" % i, [128, D], F32) for i in range(2)]
            r_h1d = S.res("h1d")
            def load_m(t):
                dma(ogs[t % 2][0][:], og_s.rearrange("h p t -> p h t")[:, :, t * 512:(t + 1) * 512], w=[ogs[t % 2][1]])
                dma(yts[t % 2][0][:], yT_s.rearrange("h p t -> p h t")[:, :, t * 512:(t + 1) * 512], w=[yts[t % 2][1]])
                dma(sghs[t % 2][0][:], zgh_s[0:8].rearrange("h p t -> p h t")[:, :, CTX + t * 512:CTX + (t + 1) * 512],
                    w=[sghs[t % 2][1]])
                dma(sgfs[t % 2][0][:], zgh_s[8:16].rearrange("h p t -> p h t")[:, :, CTX + t * 512:CTX + (t + 1) * 512],
                    w=[sgfs[t % 2][1]])
            load_m(0)
            for t in range(8):
                og, r_og = ogs[t % 2]
                yt, r_yt = yts[t % 2]
                sgh, r_sgh = sghs[t % 2]
                sgf, r_sgf = sgfs[t % 2]
                if t + 1 < 8:
                    load_m(t + 1)
                for db in range(8):
                    pH, r_pH = PS[db % 2]
                    pF, r_pF = PS[2 + db % 2]
                    for h in range(4):
                        mm(pH[:, :], whg[:, h, db * 128:(db + 1) * 128], og[:, h, :], start=(h == 0), stop=(h == 3),
                           r=[r_whg, r_og], w=[r_pH])
                    for g in range(4):
                        mm(pF[:, :], wft[:, g, db * 128:(db + 1) * 128], yt[:, g, :], start=(g == 0), stop=(g == 3),
                           r=[r_wft, r_yt], w=[r_pF])
                    tt(ta[:], pH[:, :], sgh[:, db, :], ALU.mult, r=[r_pH, r_sgh], w=[r_ta])
                    tt(tb[:], pF[:, :], sgf[:, db, :], ALU.mult, r=[r_pF, r_sgf], w=[r_tb])
                    tt(yT2[:, db, :], ta[:], tb[:], ALU.add, r=[r_ta, r_tb], w=[r_yT2], eng="pool")
                for sub in range(4):
                    tb_ = t * 4 + sub
                    xm, r_xm = xms[tb_ % 2]
                    h1, r_h1 = h1s[tb_ % 2]
                    if tb_ == 0:
                        dma(xm[:], x_d[0:128, :], w=[r_xm])
                    if tb_ + 1 < 32:
                        dma(xms[(tb_ + 1) % 2][0][:], x_d[(tb_ + 1) * 128:(tb_ + 2) * 128, :], w=[xms[(tb_ + 1) % 2][1]])
                    for half in range(2):
                        pM, r_pM = PS[4 + half]
                        for db in range(8):
                            mm(pM[:, :], yT2[:, db, sub * 128:(sub + 1) * 128], wo[:, db, half * 512:(half + 1) * 512],
                               start=(db == 0), stop=(db == 7), r=[r_yT2, r_wo], w=[r_pM])
                        hs = slice(half * 512, (half + 1) * 512)
                        tt(ta[:], pM[:, :], rows[:, 0, hs], ALU.mult, r=[r_pM, r_rows], w=[r_ta])
                        tt(h1[:, hs], ta[:], xm[:, hs], ALU.add, r=[r_ta, r_xm], w=[r_h1])
                    dma(h1_s[tb_ * 128:(tb_ + 1) * 128, :], h1[:], r=[r_h1], w=[], sem_res=r_h1)
            S.phase_end()
        if stop_after == "M":
            return nc, S

        eq = eg.enter_context(ExitStack())
        wq, r_wq = eq.enter_context(nc.sbuf_tensor("s_wq", [128, 8, 2048], BF16)), S.res("wq")
        skb, r_skb = eq.enter_context(nc.sbuf_tensor("s_skb", [128, 16, 128], BF16)), S.res("skb")
        with ExitStack() as es:
            def sb(name, shape, dt):
                return es.enter_context(nc.sbuf_tensor("s_" + name, shape, dt)), S.res(name)
            stgs = [sb("stgq%d" % i, [128, 8, 512], F32) for i in range(2)]
            wqv = wq_d.rearrange("(kc p) n -> p kc n", p=128)
            for c4 in range(4):
                stg, r_stg = stgs[c4 % 2]
                dma(stg[:], wqv[:, :, c4 * 512:(c4 + 1) * 512], w=[r_stg])
                cp(wq[:, :, c4 * 512:(c4 + 1) * 512], stg[:], r=[r_stg], w=[r_wq], eng=("pool" if c4 % 2 else "dve"))
            stg, r_stg = stgs[0]
            dma(stg[:, 0:4, :].rearrange("p a (b n) -> p (a b) n", n=128), skT_d, w=[r_stg])
            cp(skb[:], stg[:, 0:4, :].rearrange("p a (b n) -> p (a b) n", n=128), r=[r_stg], w=[r_skb])
            S.phase_end()

        with ExitStack() as es:
            def sb(name, shape, dt):
                return es.enter_context(nc.sbuf_tensor("s_" + name, shape, dt)), S.res(name)
            rows, r_rows = sb("rowsQ", [128, 4, D], F32)
            dma(rows[:].rearrange("p a d -> p (a d)"), rows_d.rearrange("a b -> (a b)").partition_broadcast(128), w=[r_rows])
            fngr, r_fngr = sb("fngr", [128, D], F32)
            dma(fngr[:], fng_d.partition_broadcast(128), w=[r_fngr])
            io16, r_io16 = sb("io16", [128, 16], F32)
            dma(io16[:], iota_d, w=[r_io16])
            hbs = [sb("hb%d" % i, [128, D], F32) for i in range(2)]
            u2f, r_u2f = sb("u2f", [128, D], F32)
            u2bs = [sb("u2b%d" % i, [128, D], BF16) for i in range(2)]
            eidxs = [sb("eidx%d" % i, [128, 128], I32) for i in range(2)]
            gates = [sb("gate%d" % i, [128, 128], F32) for i in range(2)]
            junkq, r_junkq = sb("junkq", [128, D], BF16)
            st4q, r_st4q = sb("st4q", [128, 4], F32)
            u2T, r_u2T = sb("u2T", [128, 8, 128], BF16)
            qT, r_qT = sb("qT", [128, 16, 128], BF16)
            ssb, r_ssb = sb("ssb", [128, 16, 128], F32)
            swk, _ = sb("swk", [128, 8, 128], F32)
            sv, _ = sb("sv", [128, 16, 16], F32)
            si, _ = sb("si", [128, 16, 16], U32)
            sif, r_sif = sb("sif", [128, 16, 16], F32)
            cand, _ = sb("cand", [128, 8, 256], F32)
            cwk, _ = sb("cwk", [128, 4, 256], F32)
            cv, _ = sb("cv", [128, 8, 16], F32)
            ci, _ = sb("ci", [128, 8, 16], U32)
            cih, r_cih = sb("cih", [128, 8, 16], U32)
            cil, r_cil = sb("cil", [128, 8, 16], U32)
            cif, r_cif = sb("cif", [128, 2, 128], F32)
            oh, r_oh = sb("oh", [128, 64, 16], F32)
            ef, r_ef = sb("ef", [128, 2, 128], F32)
            ex, r_ex = sb("ex", [128, 8, 16], F32)
            sm, r_sm = sb("sm", [128, 8], F32)
            r_svh = [S.res("svh%d" % i) for i in range(16)]
            r_sih = [S.res("sih%d" % i) for i in range(16)]
            r_swkh = [S.res("swkh%d" % i) for i in range(8)]
            r_candh = [S.res("candh%d" % i) for i in range(8)]
            r_cvh = [S.res("cvh%d" % i) for i in range(8)]
            r_cih_ = [S.res("cih_%d" % i) for i in range(8)]
            r_cwkh = [S.res("cwkh%d" % i) for i in range(4)]
            NR = 18
            GS = 4
            NG = 128 // GS
            uvg = [sb("uvg%d" % i, [128, 2 * D], BF16) for i in range(NR)]
            diag = [sb("diag%d" % i, [128, GS, 128], BF16) for i in range(2)]
            junkg, _ = sb("junkg", [128, D], BF16)
            apre = [sb("apre%d" % i, [128, 128], F32) for i in range(2)]
            wts = [sb("wts%d" % i, [128, 128], F32) for i in range(2)]
            r_aps = [[S.res("aps%d_%d" % (i, j)) for j in range(128)] for i in range(2)]
            r_wtg = [[S.res("wtg%d_%d" % (i, g)) for g in range(NG)] for i in range(2)]
            h2, r_h2 = sb("h2", [128, D], F32)
            st4, r_st4 = sb("st4g", [128, 4], F32)
            outs = [sb("outs%d" % i, [128, D], F32) for i in range(1)]
            gstate = {"gcount": 0}

            def vop(fn, r, w):
                S.op("dve", fn, r, w)

            def gen_Q(tb_):
                hb, r_hb = hbs[tb_ % 2]
                u2b, r_u2b = u2bs[tb_ % 2]
                eidx, r_eidx = eidxs[tb_ % 2]
                gate, r_gate = gates[tb_ % 2]
                dma(hb[:], h1_s[tb_ * 128:(tb_ + 1) * 128, :], w=[r_hb])
                yield
                act(junkq[:], hb[:], AF.Square, r=[r_hb], w=[r_st4q], accum_out=st4q[:, 0:1])
                act(st4q[:, 1:2], st4q[:, 0:1], AF.Sqrt, r=[r_st4q], w=[r_st4q], scale=1.0 / D, bias=EPS)
                yield
                vop(lambda e: e.reciprocal(out=st4q[:, 2:3], in_=st4q[:, 1:2]), [r_st4q], [r_st4q])
                stt(u2f[:], hb[:], st4q[:, 2:3], rows[:, 1, :], ALU.mult, ALU.mult, r=[r_hb, r_st4q, r_rows], w=[r_u2f])
                tt(u2b[:], u2f[:], rows[:, 2, :], ALU.add, r=[r_u2f, r_rows], w=[r_u2b])
                yield
                for kc in range(8):
                    tr(ptr[:, kc * 128:(kc + 1) * 128], u2b[:, kc * 128:(kc + 1) * 128], identb[:],
                       r=[r_u2b, r_identb], w=[r_ptr])
                yield
                act(u2T[:], ptr[:].rearrange("p (k t) -> p k t", t=128), AF.Copy, r=[r_ptr], w=[r_u2T])
                yield

                def qmm(q4):
                    pQ, r_pQ = PS[2 + q4 % 2]
                    for k4 in range(4):
                        hp = q4 * 4 + k4
                        for kc in range(8):
                            mm(pQ[:, k4 * 128:(k4 + 1) * 128], wq[:, kc, hp * 128:(hp + 1) * 128], u2T[:, kc, :],
                               start=(kc == 0), stop=(kc == 7), r=[r_wq, r_u2T], w=[r_pQ])

                def qev(q4):
                    pQ, r_pQ = PS[2 + q4 % 2]
                    act(qT[:, q4 * 4:(q4 + 1) * 4, :], pQ[:, :].rearrange("p (a t) -> p a t", t=128), AF.Copy,
                        r=[r_pQ], w=[r_qT])

                def smm(b4):
                    pS, r_pS = PS[2 + b4 % 2]
                    for k4 in range(4):
                        hp = b4 * 4 + k4
                        mm(pS[:, k4 * 128:(k4 + 1) * 128], qT[:, hp, :], skb[:, hp, :], r=[r_qT, r_skb], w=[r_pS])

                def sev(b4):
                    pS, r_pS = PS[2 + b4 % 2]
                    act(ssb[:, b4 * 4:(b4 + 1) * 4, :], pS[:, :].rearrange("p (a n) -> p a n", n=128), AF.Copy,
                        r=[r_pS], w=[r_ssb])

                qmm(0)
                yield
                qmm(1)
                qev(0)
                yield
                qmm(2)
                qev(1)
                yield
                qmm(3)
                qev(2)
                yield
                qev(3)
                yield
                smm(0)
                yield
                smm(1)
                sev(0)
                yield
                smm(2)
                sev(1)
                yield
                smm(3)
                sev(2)
                yield
                sev(3)
                yield
                for hh in range(2):
                    L8 = list(range(hh * 8, hh * 8 + 8))
                    for hp in L8:
                        vop(lambda e, hp=hp: e.max(out=sv[:, hp, 0:8], in_=ssb[:, hp, :]), [r_ssb], [r_svh[hp]])
                    for hp in L8:
                        vop(lambda e, hp=hp: e.max_index(out=si[:, hp, 0:8], in_max=sv[:, hp, 0:8], in_values=ssb[:, hp, :]),
                            [r_ssb, r_svh[hp]], [r_sih[hp]])
                    for hp in L8:
                        vop(lambda e, hp=hp: e.match_replace(out=swk[:, hp % 8, :], in_to_replace=sv[:, hp, 0:8],
                                                             in_values=ssb[:, hp, :], imm_value=-1e30),
                            [r_ssb, r_svh[hp]], [r_swkh[hp % 8]])
                    yield
                    for hp in L8:
                        vop(lambda e, hp=hp: e.max(out=sv[:, hp, 8:16], in_=swk[:, hp % 8, :]), [r_swkh[hp % 8]], [r_svh[hp]])
                    for hp in L8:
                        vop(lambda e, hp=hp: e.max_index(out=si[:, hp, 8:16], in_max=sv[:, hp, 8:16], in_values=swk[:, hp % 8, :]),
                            [r_swkh[hp % 8], r_svh[hp]], [r_sih[hp]])
                    yield
                cp(sif[:], si[:], r=r_sih, w=[r_sif])
                sv4 = sv[:].rearrange("p (h a) k -> p h a k", a=2)
                for h in range(8):
                    tt(cand[:, h, :].rearrange("p (i j) -> p i j", j=16),
                       sv4[:, h, 0, :].unsqueeze(2).broadcast_to([128, 16, 16]),
                       sv4[:, h, 1, :].unsqueeze(1).broadcast_to([128, 16, 16]), ALU.add,
                       r=[r_svh[2 * h], r_svh[2 * h + 1]], w=[r_candh[h]])
                yield
                for hh in range(2):
                    H4 = list(range(hh * 4, hh * 4 + 4))
                    for h in H4:
                        vop(lambda e, h=h: e.max(out=cv[:, h, 0:8], in_=cand[:, h, :]), [r_candh[h]], [r_cvh[h]])
                    for h in H4:
                        vop(lambda e, h=h: e.max_index(out=ci[:, h, 0:8], in_max=cv[:, h, 0:8], in_values=cand[:, h, :]),
                            [r_candh[h], r_cvh[h]], [r_cih_[h]])
                    for h in H4:
                        vop(lambda e, h=h: e.match_replace(out=cwk[:, h % 4, :], in_to_replace=cv[:, h, 0:8],
                                                           in_values=cand[:, h, :], imm_value=-1e30),
                            [r_candh[h], r_cvh[h]], [r_cwkh[h % 4]])
                    for h in H4:
                        vop(lambda e, h=h: e.max(out=cv[:, h, 8:16], in_=cwk[:, h % 4, :]), [r_cwkh[h % 4]], [r_cvh[h]])
                    for h in H4:
                        vop(lambda e, h=h: e.max_index(out=ci[:, h, 8:16], in_max=cv[:, h, 8:16], in_values=cwk[:, h % 4, :]),
                            [r_cwkh[h % 4], r_cvh[h]], [r_cih_[h]])
                    yield
                vop(lambda e: e.tensor_single_scalar(out=cih[:], in_=ci[:], scalar=4, op=ALU.logical_shift_right),
                    r_cih_, [r_cih])
                vop(lambda e: e.tensor_single_scalar(out=cil[:], in_=ci[:], scalar=15, op=ALU.bitwise_and),
                    r_cih_, [r_cil])
                cp(cif[:, 0, :], cih[:].rearrange("p h k -> p (h k)"), r=[r_cih], w=[r_cif])
                cp(cif[:, 1, :], cil[:].rearrange("p h k -> p (h k)"), r=[r_cil], w=[r_cif])
                tt(ex[:], cv[:], cv[:, :, 0:1].broadcast_to([128, 8, 16]), ALU.subtract, r=r_cvh, w=[r_ex])
                act(ex[:], ex[:], AF.Exp, r=[r_ex], w=[r_ex])
                yield
                sif4 = sif[:].rearrange("p (h a) k -> p h a k", a=2)
                for a in range(2):
                    for hh in range(2):
                        hs_ = slice(hh * 64, hh * 64 + 64)
                        tt(oh[:], cif[:, a, hs_].unsqueeze(2).broadcast_to([128, 64, 16]),
                           io16[:].unsqueeze(1).broadcast_to([128, 64, 16]), ALU.is_equal,
                           r=[r_cif, r_io16], w=[r_oh])
                        tt(oh[:].rearrange("p (h k) i -> p h k i", h=4), oh[:].rearrange("p (h k) i -> p h k i", h=4),
                           sif4[:, hh * 4:hh * 4 + 4, a, :].unsqueeze(2).broadcast_to([128, 4, 16, 16]), ALU.mult,
                           r=[r_oh, r_sif], w=[r_oh])
                        vop(lambda e, a=a, hs_=hs_: e.tensor_reduce(out=ef[:, a, hs_], in_=oh[:], axis=AX.X, op=ALU.add),
                            [r_oh], [r_ef])
                        yield
                stt(ef[:, 0, :], ef[:, 0, :], 128.0, ef[:, 1, :], ALU.mult, ALU.add, r=[r_ef], w=[r_ef])
                cp(eidx[:], ef[:, 0, :], r=[r_ef], w=[r_eidx])
                vop(lambda e: e.tensor_reduce(out=sm[:], in_=ex[:], axis=AX.X, op=ALU.add), [r_ex], [r_sm])
                vop(lambda e: e.reciprocal(out=sm[:], in_=sm[:]), [r_sm], [r_sm])
                tt(gate[:].rearrange("p (h k) -> p h k", h=8), ex[:],
                   sm[:].unsqueeze(2).broadcast_to([128, 8, 16]), ALU.mult, r=[r_ex, r_sm], w=[r_gate])
                yield

            def gen_G(tb_):
                hb, r_hb = hbs[tb_ % 2]
                u2b, r_u2b = u2bs[tb_ % 2]
                eidx, r_eidx = eidxs[tb_ % 2]
                gate, r_gate = gates[tb_ % 2]
                ap_ = apre[tb_ % 2][0]
                wt = wts[tb_ % 2][0]
                pa, r_pa = PS[0 if tb_ % 2 == 0 else 4]
                pb, r_pb = PS[1 if tb_ % 2 == 0 else 5]
                def stage_a(g):
                    bufs = []
                    for jj in range(GS):
                        j = g * GS + jj
                        gc = gstate["gcount"]
                        gstate["gcount"] += 1
                        ug, r_ug = uvg[gc % NR]
                        bufs.append((ug, r_ug))
                        S.dma("pool", lambda e, ug=ug, eidx=eidx, j=j: e.indirect_dma_start(
                            out=ug[:], out_offset=None, in_=uvb_s,
                            in_offset=bass.IndirectOffsetOnAxis(ap=eidx[:, j:j + 1], axis=0)),
                            [r_eidx], [r_ug])
                        stt(junkg[:], ug[:, 0:D], 1.0, u2b[:], ALU.mult, ALU.mult, r=[r_ug, r_u2b],
                            w=[r_aps[tb_ % 2][j]], accum_out=ap_[:, j:j + 1])
                    return bufs

                def stage_gelu(g):
                    gsl = slice(g * GS, (g + 1) * GS)
                    act(wt[:, gsl], ap_[:, gsl], AF.Gelu, r=r_aps[tb_ % 2][g * GS:(g + 1) * GS], w=[r_wtg[tb_ % 2][g]])

                def stage_mult(g):
                    r_wt = r_wtg[tb_ % 2][g]
                    for jj in range(GS):
                        j = g * GS + jj
                        act(wt[:, j:j + 1], wt[:, j:j + 1], AF.Copy, r=[r_wt, r_gate], w=[r_wt], scale=gate[:, j:j + 1])

                def stage_c(g, bufs):
                    r_wt = r_wtg[tb_ % 2][g]
                    dg, r_dg = diag[g % 2]
                    for jj in range(GS):
                        j = g * GS + jj
                        act(dg[:, jj, :], identb[:], AF.Copy, r=[r_identb, r_wt], w=[r_dg], scale=wt[:, j:j + 1])
                    for jj in range(GS):
                        j = g * GS + jj
                        ug, r_ug = bufs[jj]
                        mm(pa[:, :], dg[:, jj, :], ug[:, D:D + 512], start=(j == 0), stop=(j == 127), r=[r_dg, r_ug], w=[r_pa])
                        mm(pb[:, :], dg[:, jj, :], ug[:, D + 512:2 * D], start=(j == 0), stop=(j == 127), r=[r_dg, r_ug], w=[r_pb])

                allb = {}
                prev_tb = pend["tb"]
                for g in range(NG + 1):
                    if prev_tb is not None and g == 2:
                        block_end_1(prev_tb)
                    if prev_tb is not None and g == 4:
                        block_end_2(prev_tb)
                    if g < NG:
                        allb[g] = stage_a(g)
                    if g >= 1:
                        stage_mult(g - 1)
                        stage_c(g - 1, allb.pop(g - 1))
                    if g < NG:
                        stage_gelu(g)
                    yield
                pend["tb"] = tb_

            def block_end_1(tb_):
                hb, r_hb = hbs[tb_ % 2]
                pa, r_pa = PS[0 if tb_ % 2 == 0 else 4]
                pb, r_pb = PS[1 if tb_ % 2 == 0 else 5]
                tt(h2[:, 0:512], pa[:, :], rows[:, 3, 0:512], ALU.mult, r=[r_pa, r_rows], w=[r_h2])
                tt(h2[:, 512:1024], pb[:, :], rows[:, 3, 512:1024], ALU.mult, r=[r_pb, r_rows], w=[r_h2])
                tt(h2[:], h2[:], hb[:], ALU.add, r=[r_h2, r_hb], w=[r_h2])
                act(junkq[:], h2[:], AF.Square, r=[r_h2], w=[r_st4], accum_out=st4[:, 0:1])
                act(st4[:, 1:2], st4[:, 0:1], AF.Sqrt, r=[r_st4], w=[r_st4], scale=1.0 / D, bias=EPS)

            def block_end_2(tb_):
                vop(lambda e: e.reciprocal(out=st4[:, 2:3], in_=st4[:, 1:2]), [r_st4], [r_st4])
                ot, r_ot = outs[0]
                stt(ot[:], h2[:], st4[:, 2:3], fngr[:], ALU.mult, ALU.mult, r=[r_h2, r_st4, r_fngr], w=[r_ot])
                dma(out_d[tb_ * 128:(tb_ + 1) * 128, :], ot[:], r=[r_ot], w=[], sem_res=r_ot)

            pend = {"tb": None}
            for _ in gen_Q(0):
                pass
            for tb_ in range(32):
                gq = gen_Q(tb_ + 1) if tb_ < 31 else None
                for gi, _ in enumerate(gen_G(tb_)):
                    if gq is not None and gi >= 2:
                        next(gq, None)
                        if gi < 5:
                            next(gq, None)
                if gq is not None:
                    for _ in gq:
                        pass
            block_end_1(pend["tb"])
            block_end_2(pend["tb"])
            S.phase_end()
    return nc, S


_CONST = {}


def _consts():
    if _CONST:
        return _CONST
    bf = ml_dtypes.bfloat16
    n = np.arange(128)
    ang = 2.0 * np.pi * ((n[:, None] * n[None, :]) % 128) / 128.0
    _CONST["csw"] = np.concatenate([np.cos(ang), np.sin(ang)], axis=1).astype(bf)
    l = np.arange(4096, dtype=np.int64)
    angL = 2.0 * np.pi * ((l[:, None] * l[None, :]) % 4096) / 4096.0
    cl = np.cos(angL).astype(np.float32)
    sl = np.sin(angL).astype(np.float32)
    lay = lambda m: np.ascontiguousarray(m.reshape(32, 128, 32, 128).transpose(2, 1, 0, 3)).astype(bf)
    _CONST["cl"] = lay(cl)
    _CONST["sl"] = lay(sl)
    _CONST["identb"] = np.eye(128, dtype=np.float32).astype(bf)
    _CONST["identf"] = np.eye(128, dtype=np.float32)
    j = n[:, None]
    i = n[None, :]
    same = (j // 64) == (i // 64)
    mf = (same & (j <= i)).astype(np.float32)
    mb = (same & (j >= i)).astype(np.float32)
    _CONST["masks"] = np.ascontiguousarray(np.stack([mf, mb], axis=1))
    _CONST["iota16"] = np.tile(np.arange(16, dtype=np.float32)[None, :], (128, 1))
    rm = np.ones((128, NT), np.float32)
    rm[:, 0::64] = 0.0
    _CONST["rmask"] = rm.astype(bf)
    return _CONST


def make_in_maps(x, c, ctx, c_ctx, w_ada, b_ada, norm_mix_g, norm_ffn_g, w_in, hg_lb_f, hg_lb_b, hg_norm_g,
                 w_hg_out, w_ft_out, w_out, peer_w_q, peer_sub_keys, peer_u, peer_v, final_norm_g):
    f = lambda a: np.ascontiguousarray(np.asarray(a, dtype=np.float32))
    C = _consts()
    pl = lambda v, k: f(np.asarray(v).reshape(k, 128).T)
    shared = {
        "w_ada": f(w_ada[0]), "b_ada_p": pl(b_ada[0], 48), "gmix_p": pl(norm_mix_g[0], 8),
        "gffn_p": pl(norm_ffn_g[0], 8), "w_in": f(w_in[0]),
        "lbf": f(np.asarray(hg_lb_f).reshape(2, 4, 128).transpose(2, 1, 0)),
        "lbb": f(np.asarray(hg_lb_b).reshape(2, 4, 128).transpose(2, 1, 0)),
        "hgn_p": f(np.asarray(hg_norm_g[0]).T), "w_hg_out": f(w_hg_out[0]), "w_ft_out": f(w_ft_out[0]),
        "w_out": f(w_out[0]), "w_q": f(peer_w_q[0]),
        "skT": f(np.asarray(peer_sub_keys[0]).reshape(16, 128, 128).transpose(2, 0, 1)),
        "peer_u": f(peer_u[0]), "peer_v": f(peer_v[0]), "fng": f(final_norm_g),
    }
    shared.update(C)
    maps = []
    c = np.asarray(c)
    c_ctx = np.asarray(c_ctx)
    for b in range(8):
        m = dict(shared)
        m["x"] = f(x[b])
        m["ctx"] = f(ctx[b])
        m["cc"] = f(np.stack([c[b].reshape(8, 128).T, c_ctx.reshape(8, 128).T], axis=2))
        maps.append(m)
    return maps


_NC = None


def kernel(**inputs):
    global _NC
    if _NC is None:
        _NC = build()[0]
    maps = make_in_maps(**inputs)
    res = run_bass_kernel_spmd(_NC, maps, core_ids=list(range(8)))
    return np.stack([np.asarray(r["out"], dtype=np.float32) for r in res.results], axis=0)
```

```python
import numpy as np
import ml_dtypes
import concourse.bass as bass
import concourse.mybir as mybir
from concourse.bass_utils import run_bass_kernel_spmd
from contextlib import ExitStack

F32 = mybir.dt.float32
BF16 = mybir.dt.bfloat16
U32 = mybir.dt.uint32
I32 = mybir.dt.int32
AF = mybir.ActivationFunctionType
ALU = mybir.AluOpType
AX = mybir.AxisListType

D = 1024
L = 4096
CTX = 256
NT = L + CTX
NB = NT // 128
NCH = NT // 64
EPS = 1e-6


class Res:
    __slots__ = ("name", "w", "r", "dsem", "dcnt", "bg")

    def __init__(self, name):
        self.name = name
        self.w = None
        self.r = []
        self.dsem = None
        self.dcnt = 0
        self.bg = False


class Sched:
    ENG = ["pe", "dve", "act", "pool", "sp"]

    def __init__(self, nc, es):
        self.nc = nc
        self.es = es
        self.prog = {e: [] for e in self.ENG}
        self.cnt = {e: 0 for e in self.ENG}
        self.sem = {e: es.enter_context(nc.semaphore("sem_" + e)) for e in self.ENG}
        self.seen = {e: {} for e in self.ENG}
        self.all = []
        self.excl = set()
        self.ndsem = 0

    def res(self, name=None):
        r = Res(name or ("r%d" % len(self.all)))
        self.all.append(r)
        return r

    def _semof(self, key):
        return self.sem[key] if isinstance(key, str) else key.dsem

    def _waits(self, e, reads, writes, skip_key=None):
        need = {}

        def add2(ev):
            k, v = ev
            if k == "pe" and e == "pe":
                return
            if need.get(k, 0) < v:
                need[k] = v

        for r in reads:
            if r.w is not None:
                add2(r.w)
        for w in writes:
            if w.w is not None and w.w[0] is not skip_key:
                add2(w.w)
            for ev in w.r:
                add2(ev)
        out = []
        seen = self.seen[e]
        for k, v in need.items():
            if seen.get(k, 0) >= v:
                continue
            seen[k] = v
            out.append((self._semof(k), v))
        return out

    def _update(self, ev, reads, writes):
        for r in reads:
            r.r.append(ev)
        for w in writes:
            w.w = ev
            w.r = []

    def op(self, e, fn, reads=(), writes=()):
        if self.excl:
            ex = [r for r in reads if r in self.excl and r not in writes]
            if ex:
                writes = list(writes) + ex
                reads = [r for r in reads if r not in self.excl]
        waits = self._waits(e, reads, writes)
        self.cnt[e] += 1
        ev = (e, self.cnt[e])
        self.prog[e].append((waits, fn, self.sem[e], 1))
        self._update(ev, reads, writes)

    def dma(self, e, fn, reads=(), writes=(), sem_res=None):
        sr = sem_res or (writes[0] if writes else reads[0])
        if sr.dsem is None:
            sr.dsem = self.es.enter_context(self.nc.semaphore("dsem%d" % self.ndsem))
            self.ndsem += 1
        waits = self._waits(e, reads, writes, skip_key=sr)
        sr.dcnt += 16
        ev = (sr, sr.dcnt)
        self.prog[e].append((waits, fn, sr.dsem, 16))
        self._update(ev, reads, writes)

    def phase_end(self):
        waits = self._waits("sp", [], [r for r in self.all if not r.bg])
        self.prog["sp"].append((waits, None, None, 0))
        nc = self.nc
        engs = {"pe": "tensor", "dve": "vector", "act": "scalar", "pool": "gpsimd", "sp": "sync"}
        with nc.Block() as block:
            for e in self.ENG:
                prog = self.prog[e]

                def body(eng, prog=prog):
                    for waits, fn, sem, inc in prog:
                        for s, v in waits:
                            eng.wait_ge(s, v)
                        if fn is not None:
                            fn(eng).then_inc(sem, inc)

                getattr(block, engs[e])(body)
        self.prog = {e: [] for e in self.ENG}
        for e in self.ENG:
            for k in self.ENG:
                self.seen[e][k] = self.cnt[k]
            for r in self.all:
                if r.dsem is not None and not r.bg:
                    self.seen[e][r] = r.dcnt


def build(debug=False, stop_after=None):
    nc = bass.Bass("TRN2", target_bir_lowering=False)

    def din(name, shape, dt=F32):
        return nc.dram_tensor(name, shape, dt, kind="ExternalInput").ap()

    def dscr(name, shape, dt):
        return nc.dram_tensor(name, shape, dt, kind="ExternalOutput" if debug else "Internal").ap()

    x_d = din("x", [L, D])
    ctx_d = din("ctx", [CTX, D])
    cc_d = din("cc", [128, 8, 2])
    wada_d = din("w_ada", [D, 6 * D])
    bada_d = din("b_ada_p", [128, 48])
    gmix_d = din("gmix_p", [128, 8])
    gffn_d = din("gffn_p", [128, 8])
    win_d = din("w_in", [D, 5120])
    lbf_d = din("lbf", [128, 4, 2])
    lbb_d = din("lbb", [128, 4, 2])
    hgn_d = din("hgn_p", [128, 4])
    whg_d = din("w_hg_out", [512, D])
    wft_d = din("w_ft_out", [512, D])
    wout_d = din("w_out", [D, D])
    wq_d = din("w_q", [D, 2048])
    skT_d = din("skT", [128, 16, 128])
    pu_d = din("peer_u", [16384, D])
    pv_d = din("peer_v", [16384, D])
    fng_d = din("fng", [D])
    csw_d = din("csw", [128, 256], BF16)
    cl_d = din("cl", [32, 128, 32, 128], BF16)
    sl_d = din("sl", [32, 128, 32, 128], BF16)
    identb_d = din("identb", [128, 128], BF16)
    identf_d = din("identf", [128, 128])
    mask_d = din("masks", [128, 2, 128])
    iota_d = din("iota16", [128, 16])
    rmask_d = din("rmask", [128, NT], BF16)
    out_d = nc.dram_tensor("out", [L, D], F32, kind="ExternalOutput").ap()

    rows_d = dscr("rows_s", [32, 128], F32)
    uT_dbg = dscr("uT_s", [128, 8, NT], BF16) if debug else None
    zq_s = dscr("zq_s", [4, 128, NT], BF16)
    zf_s = dscr("zf_s", [8, 128, NT], F32)
    zg_s = dscr("zg_s", [4, 128, NT], BF16)
    zft_s = dscr("zft_s", [4, 128, NT], BF16)
    zgh_s = dscr("zgh_s", [16, 128, NT], BF16)
    v_s = dscr("v_s", [NB, 128, 512], BF16)
    og_s = dscr("og_s", [4, 128, L], BF16)
    of_dbg = dscr("of_s", [4, 128, L], F32) if debug else None
    yT_s = dscr("yT_s", [4, 128, L], BF16)
    h1_s = dscr("h1_s", [L, D], F32)
    u2_s = dscr("u2_s", [L, D], F32)
    uvb_s = nc.dram_tensor("uvb_s", [16384, 2 * D], BF16, kind="Internal").ap()
    eidx_dbg = dscr("eidx_s", [128, 32, 128], I32) if debug else None
    gate_dbg = dscr("gate_s", [128, 32, 128], F32) if debug else None

    with ExitStack() as eg:
        S = Sched(nc, eg)

        def mm(out, lhsT, rhs, start=True, stop=True, r=(), w=(), sgc=False):
            S.op("pe", lambda e: e.matmul(out, lhsT=lhsT, rhs=rhs, start=start, stop=stop,
                                          skip_group_check=sgc), r, w)

        def tr(out, in_, ident, r=(), w=()):
            S.op("pe", lambda e: e.transpose(out, in_, ident), r, w)

        def act(out, in_, func, r=(), w=(), **kw):
            S.op("act", lambda e: e.activation(out=out, in_=in_, func=func, **kw), r, w)

        def tt(out, in0, in1, op, r=(), w=(), eng="dve"):
            S.op(eng, lambda e: e.tensor_tensor(out=out, in0=in0, in1=in1, op=op), r, w)

        def ts(out, in0, s1, op0, s2=None, op1=None, r=(), w=(), eng="dve"):
            if op1 is None:
                S.op(eng, lambda e: e.tensor_scalar(out=out, in0=in0, scalar1=s1, scalar2=None, op0=op0), r, w)
            else:
                S.op(eng, lambda e: e.tensor_scalar(out=out, in0=in0, scalar1=s1, scalar2=s2, op0=op0, op1=op1), r, w)

        def stt(out, in0, scalar, in1, op0, op1, r=(), w=(), accum_out=None):
            S.op("dve", lambda e: e.scalar_tensor_tensor(out=out, in0=in0, scalar=scalar, in1=in1, op0=op0,
                                                         op1=op1, accum_out=accum_out), r, w)

        def cp(out, in_, r=(), w=(), eng="dve"):
            S.op(eng, lambda e: e.tensor_copy(out=out, in_=in_), r, w)

        def dma(out, in_, r=(), w=(), q="sp", sem_res=None):
            S.dma(q, lambda e: e.dma_start(out=out, in_=in_), r, w, sem_res=sem_res)

        def gsb(name, shape, dt):
            return eg.enter_context(nc.sbuf_tensor("s_" + name, shape, dt)), S.res(name)

        PS = []
        for i in range(7):
            PS.append((eg.enter_context(nc.psum_tensor("ps%d" % i, [128, 512], F32)), S.res("ps%d" % i)))
        ptr, r_ptr = eg.enter_context(nc.psum_tensor("ptr", [128, 1024], BF16)), S.res("ptr")
        S.excl = set([p[1] for p in PS] + [r_ptr])

        identb, r_identb = gsb("identb", [128, 128], BF16)
        identf, r_identf = gsb("identf", [128, 128], F32)
        onesf, r_onesf = gsb("onesf", [128, 128], F32)
        modP, r_modP = gsb("modP", [128, 96], F32)
        vecP, r_vecP = gsb("vecP", [128, 64], F32)
        lbT, r_lbT = gsb("lbT", [128, 16], F32)
        hgn, r_hgn = gsb("hgn", [128, 4], F32)
        dma(identb[:], identb_d, w=[r_identb])
        dma(identf[:], identf_d, w=[r_identf])
        dma(hgn[:], hgn_d, w=[r_hgn])
        S.op("pool", lambda e: e.memset(onesf[:], 1.0), (), [r_onesf])
        onec, r_onec = gsb("onec", [128, 1], F32)
        S.op("pool", lambda e: e.memset(onec[:], 1.0), (), [r_onec])
        epsc, r_epsc = gsb("epsc", [128, 1], F32)
        S.op("pool", lambda e: e.memset(epsc[:], EPS), (), [r_epsc])

        with ExitStack() as es:
            def sb(name, shape, dt):
                return es.enter_context(nc.sbuf_tensor("s_" + name, shape, dt)), S.res(name)
            cc, r_cc = sb("cc", [128, 8, 2], F32)
            scc, r_scc = sb("scc", [128, 8, 2], F32)
            bada, r_bada = sb("bada", [128, 48], F32)
            gmix, r_gmix = sb("gmix", [128, 8], F32)
            gffn, r_gffn = sb("gffn", [128, 8], F32)
            lbin, r_lbin = sb("lbin", [128, 2, 4, 2], F32)
            lbd, r_lbd = sb("lbd", [128, 8], F32)
            tmp8, r_tmp8 = sb("tmp8", [128, 8], F32)
            rowsrc, r_rowsrc = sb("rowsrc", [32, 128], F32)
            slabs = [sb("slab%d" % i, [128, 8, 512], F32) for i in range(2)]
            dma(cc[:], cc_d, w=[r_cc])
            dma(bada[:], bada_d, w=[r_bada])
            dma(gmix[:], gmix_d, w=[r_gmix])
            dma(gffn[:], gffn_d, w=[r_gffn])
            dma(lbin[:, 0], lbf_d, w=[r_lbin])
            dma(lbin[:, 1], lbb_d, w=[r_lbin])
            act(scc[:], cc[:], AF.Silu, r=[r_cc], w=[r_scc])
            pmod, r_pmod = PS[0]
            wv = wada_d.rearrange("(kc p) n -> p kc n", p=128)
            for s in range(12):
                slab, r_slab = slabs[s % 2]
                dma(slab[:], wv[:, :, s * 512:(s + 1) * 512], w=[r_slab])
                for jj in range(4):
                    j = s * 4 + jj
                    for kc in range(8):
                        mm(pmod[:, 2 * j:2 * j + 2], slab[:, kc, jj * 128:(jj + 1) * 128], scc[:, kc, :],
                           start=(kc == 0), stop=(kc == 7), r=[r_slab, r_scc], w=[r_pmod])
            pm = pmod[:, 0:96].rearrange("p (j t) -> p j t", t=2)
            tt(modP[:, 0:48], pm[:, :, 0], bada[:], ALU.add, r=[r_pmod, r_bada], w=[r_modP])
            tt(modP[:, 48:96], pm[:, :, 1], bada[:], ALU.add, r=[r_pmod, r_bada], w=[r_modP])
            ts(tmp8[:], modP[:, 8:16], 1.0, ALU.add, r=[r_modP], w=[r_tmp8])
            tt(vecP[:, 0:8], tmp8[:], gmix[:], ALU.mult, r=[r_tmp8, r_gmix], w=[r_vecP])
            cp(vecP[:, 8:16], modP[:, 0:8], r=[r_modP], w=[r_vecP])
            ts(tmp8[:], modP[:, 56:64], 1.0, ALU.add, r=[r_modP], w=[r_tmp8])
            tt(vecP[:, 16:24], tmp8[:], gmix[:], ALU.mult, r=[r_tmp8, r_gmix], w=[r_vecP])
            cp(vecP[:, 24:32], modP[:, 48:56], r=[r_modP], w=[r_vecP])
            cp(vecP[:, 32:40], modP[:, 16:24], r=[r_modP], w=[r_vecP])
            ts(tmp8[:], modP[:, 32:40], 1.0, ALU.add, r=[r_modP], w=[r_tmp8])
            tt(vecP[:, 40:48], tmp8[:], gffn[:], ALU.mult, r=[r_tmp8, r_gffn], w=[r_vecP])
            cp(vecP[:, 48:56], modP[:, 24:32], r=[r_modP], w=[r_vecP])
            cp(vecP[:, 56:64], modP[:, 40:48], r=[r_modP], w=[r_vecP])
            pT, r_pT = PS[1]
            tr(pT[0:32, 0:128], vecP[:, 32:64], identf[:], r=[r_vecP, r_identf], w=[r_pT])
            cp(rowsrc[:], pT[0:32, 0:128], r=[r_pT], w=[r_rowsrc])
            r_rowsd = S.res("rows_d")
            dma(rows_d, rowsrc[:], r=[r_rowsrc], w=[r_rowsd])
            lv = lbin[:].rearrange("p a h t -> p (a h) t")
            tt(lbd[:], lv[:, :, 0], lv[:, :, 1], ALU.subtract, r=[r_lbin], w=[r_lbd])
            lbT4 = lbT[:].rearrange("p (a b h) -> p a b h", a=2, b=2)
            act(lbT4[:, :, 0, :], lbd[:].rearrange("p (a h) -> p a h", a=2), AF.Sigmoid, r=[r_lbd], w=[r_lbT])
            ts(lbT4[:, :, 1, :], lbT4[:, :, 0, :], -1.0, ALU.mult, 1.0, ALU.add, r=[r_lbT], w=[r_lbT])
            S.phase_end()
        if stop_after == "A":
            return nc, S

        r_tbg = [S.res("tbg%d" % i) for i in range(4)]

        def gen_T():
            TR = 1024
            for c in range(16384 // TR):
                rs_ = slice(c * TR, (c + 1) * TR)
                dma(uvb_s[rs_, 0:D], pu_d[rs_, :], q="pool", sem_res=r_tbg[c % 4])
                dma(uvb_s[rs_, D:2 * D], pv_d[rs_, :], q="pool", sem_res=r_tbg[c % 4])
                yield
        gT = gen_T()

        with ExitStack() as es:
            def sb(name, shape, dt):
                return es.enter_context(nc.sbuf_tensor("s_" + name, shape, dt)), S.res(name)
            uT, r_uT = sb("uT", [128, 8, NT], BF16)
            r_uTb = [S.res("uTb%d" % b) for b in range(NB)]
            xts = [sb("xt%d" % i, [128, D], F32) for i in range(2)]
            xns = [sb("xn%d" % i, [128, D], BF16) for i in range(2)]
            junk, r_junk = sb("junk", [128, D], BF16)
            st4, r_st4 = sb("st4", [128, 4], F32)
            tmpu, r_tmpu = sb("tmpu", [128, 8, 128], F32)
            for blk in range(NB):
                xt, r_xt = xts[blk % 2]
                xn, r_xn = xns[blk % 2]
                src = ctx_d[blk * 128:(blk + 1) * 128, :] if blk < 2 else x_d[(blk - 2) * 128:(blk - 1) * 128, :]
                dma(xt[:], src, w=[r_xt])
                act(junk[:], xt[:], AF.Square, r=[r_xt], w=[r_junk, r_st4], accum_out=st4[:, 0:1])
                act(st4[:, 1:2], st4[:, 0:1], AF.Sqrt, r=[r_st4], w=[r_st4], scale=1.0 / D, bias=EPS)
                S.op("dve", lambda e: e.reciprocal(out=st4[:, 2:3], in_=st4[:, 1:2]), [r_st4], [r_st4])
                act(xn[:], xt[:], AF.Copy, r=[r_xt, r_st4], w=[r_xn], scale=st4[:, 2:3])
                for kc in range(8):
                    tr(ptr[:, kc * 128:(kc + 1) * 128], xn[:, kc * 128:(kc + 1) * 128], identb[:],
                       r=[r_xn, r_identb], w=[r_ptr])
                vo = 16 if blk < 2 else 0
                pv = ptr[:].rearrange("p (k t) -> p k t", t=128)
                tt(tmpu[:], pv, vecP[:, vo:vo + 8].unsqueeze(2).broadcast_to([128, 8, 128]), ALU.mult,
                   r=[r_ptr, r_vecP], w=[r_tmpu])
                tt(uT[:, :, blk * 128:(blk + 1) * 128], tmpu[:],
                   vecP[:, vo + 8:vo + 16].unsqueeze(2).broadcast_to([128, 8, 128]), ALU.add,
                   r=[r_tmpu, r_vecP], w=[r_uTb[blk]])
            if debug:
                dma(uT_dbg, uT[:], r=r_uTb, w=[])
            wsts = [sb("wst%d" % i, [128, 8, 128], F32) for i in range(2)]
            wbfs = [sb("wbf%d" % i, [128, 8, 128], BF16) for i in range(2)]
            zf32 = [sb("zf32_%d" % i, [128, NT], F32) for i in range(2)]
            zb16 = [sb("zb16_%d" % i, [128, NT], BF16) for i in range(2)]
            wview = win_d.rearrange("(kc p) n -> p kc n", p=128)
            tiles = [(t * 512, 512) for t in range(8)] + [(4096, 256)]
            plan = []
            for h in range(4):
                plan.append((h, AF.Copy, 128.0 ** -0.5, zq_s[h], False))
            for j in range(8):
                plan.append((4 + j, AF.Sigmoid, 1.0, zf_s[j], True))
            for h in range(4):
                plan.append((16 + h, AF.Silu, 1.0, zg_s[h], False))
            for h in range(4):
                plan.append((20 + h, AF.Copy, 1.0, zft_s[h], False))
            for j in range(16):
                plan.append((24 + j, AF.Sigmoid, 1.0, zgh_s[j], False))
            plan.sort(key=lambda p: {AF.Copy: 0, AF.Sigmoid: 1, AF.Silu: 2}[p[1]])
            nf = nb = 0
            r_zscr = S.res("zscr")
            def load_w(ci):
                wst, r_wst = wsts[ci % 2]
                cb = plan[ci][0]
                dma(wst[:], wview[:, :, cb * 128:(cb + 1) * 128], w=[r_wst])
            load_w(0)
            for ci, (cb, func, scale, dst, isf) in enumerate(plan):
                wst, r_wst = wsts[ci % 2]
                wbf, r_wbf = wbfs[ci % 2]
                cp(wbf[:], wst[:], r=[r_wst], w=[r_wbf])
                if ci + 1 < len(plan):
                    load_w(ci + 1)
                if isf:
                    zst, r_zst = zf32[nf % 2]; nf += 1
                else:
                    zst, r_zst = zb16[nb % 2]; nb += 1
                for ti, (t0, tn) in enumerate(tiles):
                    pz, r_pz = PS[ti % 4]
                    for kc in range(8):
                        mm(pz[:, 0:tn], wbf[:, kc, :], uT[:, kc, t0:t0 + tn], start=(kc == 0), stop=(kc == 7),
                           r=[r_wbf] + r_uTb[t0 // 128:(t0 + tn) // 128], w=[r_pz])
                    act(zst[:, t0:t0 + tn], pz[:, 0:tn], func, r=[r_pz], w=[r_zst], scale=scale)
                    if 4 <= cb < 12:
                        jd, jh = (cb - 4) // 4, (cb - 4) % 4
                        ts(zst[:, t0:t0 + tn], zst[:, t0:t0 + tn], lbT[:, jd * 8 + 4 + jh:jd * 8 + 4 + jh + 1], ALU.mult,
                           lbT[:, jd * 8 + jh:jd * 8 + jh + 1], ALU.add, r=[r_zst, r_lbT], w=[r_zst])
                    if ti == 0 and ci % 6 == 2:
                        next(gT, None)
                dma(dst, zst[:], r=[r_zst], w=[], sem_res=r_zst)
            wV32, r_wV32 = sb("wV32", [128, 8, 512], F32)
            wV, r_wV = sb("wV", [128, 8, 512], BF16)
            Vsts = [sb("Vst%d" % i, [128, 512], BF16) for i in range(2)]
            dma(wV32[:], wview[:, :, 1536:2048], w=[r_wV32])
            cp(wV[:], wV32[:], r=[r_wV32], w=[r_wV], eng="pool")
            for blk in range(NB):
                pz, r_pz = PS[4 + blk % 3]
                for kc in range(8):
                    mm(pz[:, :], uT[:, kc, blk * 128:(blk + 1) * 128], wV[:, kc, :], start=(kc == 0), stop=(kc == 7),
                       r=[r_wV, r_uTb[blk]], w=[r_pz])
                Vst, r_Vst = Vsts[blk % 2]
                if blk % 2 == 0:
                    act(Vst[:], pz[:, :], AF.Copy, r=[r_pz], w=[r_Vst])
                else:
                    cp(Vst[:], pz[:, :], r=[r_pz], w=[r_Vst])
                dma(v_s[blk], Vst[:], r=[r_Vst], w=[], sem_res=r_Vst)
            S.phase_end()
        if stop_after == "P":
            return nc, S

        with ExitStack() as es:
            def sb(name, shape, dt):
                return es.enter_context(nc.sbuf_tensor("s_" + name, shape, dt)), S.res(name)
            rmask, r_rmask = sb("rmask", [128, NT], BF16)
            masks, r_masks = sb("masks", [128, 2, 128], F32)
            dma(rmask[:], rmask_d, w=[r_rmask])
            dma(masks[:], mask_d, w=[r_masks])
            Fb, r_F = sb("Fb", [128, NT], F32)
            Lb, r_L = sb("Lb", [128, NT], F32)
            Bb, r_B = sb("Bb", [128, NT], F32)
            Kb, r_K = sb("Kb", [128, NT], BF16)
            qb, r_q = sb("qb", [128, NT], BF16)
            QD, r_QD = sb("QD", [128, NT], BF16)
            QE, r_QE = sb("QE", [128, NT], BF16)
            KD, r_KD = sb("KD", [128, NT], BF16)
            KDt, r_KDt = sb("KDt", [128, NB, 128], BF16)
            Vb, r_V = sb("Vb", [128, NB, 128], BF16)
            sgg, r_sgg = sb("sgg", [128, NT], BF16)
            Sall, r_Sall = sb("Sall", [128, NCH, 128], BF16)
            Sf = [sb("Sf%d" % i, [128, 128], F32) for i in range(2)]
            oT, r_oT = sb("oT", [128, L], F32)
            edge, r_edge = sb("edge", [128, NCH], F32)
            dec, r_dec = sb("dec", [128, NCH], F32)
            Am_all, _ = sb("Am_all", [128, 32, 128], BF16)
            r_Amb = [S.res("Amb%d" % i) for i in range(32)]
            sq, r_sq = sb("sq", [128, 512], F32)
            rs, r_rs = sb("rs", [128, 512], F32)
            t1, r_t1 = sb("t1", [128, 512], F32)
            ogst, r_ogst = QE, r_QE
            r_ogscr = S.res("ogscr")
            r_pUs = [S.res("pU%d" % i) for i in range(8)]
            for h in range(4):
                dma(qb[:], zq_s[h], w=[r_q])
                dma(Vb[:], v_s[:, :, h * 128:(h + 1) * 128].rearrange("b p v -> p b v"), w=[r_V])
                dma(sgg[:], zg_s[h], w=[r_sgg])
                for dr in range(2):
                    lbc = lbT[:, dr * 8 + h:dr * 8 + h + 1]
                    omlc = lbT[:, dr * 8 + 4 + h:dr * 8 + 4 + h + 1]
                    dma(Fb[:], zf_s[dr * 4 + h], w=[r_F])
                    act(Lb[:], Fb[:], AF.Ln, r=[r_F], w=[r_L])
                    act(Kb[:], Fb[:], AF.Identity, r=[r_F], w=[r_K], scale=-1.0, bias=onec[:, 0:1])
                    S.op("dve", lambda e: e.tensor_tensor_scan(out=Bb[:], data0=rmask[:], data1=Lb[:], initial=0.0,
                                                               op0=ALU.mult, op1=ALU.add), [r_rmask, r_L], [r_B])
                    Bv = Bb[:].rearrange("p (c t) -> p c t", t=64)
                    cp(edge[:], Bv[:, :, 63], r=[r_B], w=[r_edge])
                    ebc = edge[:].unsqueeze(2).broadcast_to([128, NCH, 64])
                    if dr == 1:
                        tt(Lb[:], Lb[:], Bb[:], ALU.subtract, r=[r_L, r_B], w=[r_L])
                        tt(Bv, Lb[:].rearrange("p (c t) -> p c t", t=64), ebc, ALU.add, r=[r_L, r_edge], w=[r_B])
                    if dr == 0:
                        tt(Lb[:].rearrange("p (c t) -> p c t", t=64), Bv, ebc, ALU.subtract, r=[r_B, r_edge], w=[r_L])
                    act(Fb[:], Bb[:], AF.Exp, r=[r_B], w=[r_F])
                    tt(QD[:], Fb[:], qb[:], ALU.mult, r=[r_F, r_q], w=[r_QD])
                    act(Bb[:], Lb[:], AF.Exp, r=[r_L], w=[r_B])
                    tt(QE[:], Bb[:], qb[:], ALU.mult, r=[r_B, r_q], w=[r_QE])
                    act(Fb[:], Lb[:], AF.Exp, r=[r_L], w=[r_F], scale=-1.0)
                    tt(KD[:], Fb[:], Kb[:], ALU.mult, r=[r_F, r_K], w=[r_KD])
                    act(dec[:], edge[:], AF.Exp, r=[r_edge], w=[r_dec])
                    next(gT, None)
                    for b0 in range(0, NB, 8):
                        nb_ = min(8, NB - b0)
                        for bi in range(nb_):
                            blk = b0 + bi
                            tr(ptr[:, bi * 128:(bi + 1) * 128], KD[:, blk * 128:(blk + 1) * 128], identb[:],
                               r=[r_KD, r_identb], w=[r_ptr])
                        act(KDt[:, b0:b0 + nb_, :], ptr[:, 0:nb_ * 128].rearrange("p (b k) -> p b k", k=128), AF.Copy,
                            r=[r_ptr], w=[r_KDt])
                    order = list(range(NCH)) if dr == 0 else [3, 2, 1, 0] + list(range(NCH - 1, 3, -1))
                    S.op("pool", lambda e: e.memset(Sf[0][0][:], 0.0), (), [Sf[0][1]])
                    S.op("pool", lambda e, n0=order[0]: e.memset(Sall[:, n0, :], 0.0), (), [r_Sall])
                    def a_part(i):
                        blk = i + 2
                        pA, r_pA = PS[2 + i % 2]
                        cs = slice(blk * 128, (blk + 1) * 128)
                        mm(pA[:, 0:128], KD[:, cs], QE[:, cs], r=[r_KD, r_QE], w=[r_pA])
                        tt(Am_all[:, i, :], pA[:, 0:128], masks[:, dr, :], ALU.mult, r=[r_pA, r_masks], w=[r_Amb[i]])
                    n_a = 0
                    for idx in range(len(order) - 1):
                        n = order[idx]
                        blk, half = n // 2, n % 2
                        psl = slice(64 * half, 64 * half + 64)
                        pU, r_pU = PS[(0, 1, 4, 5)[idx % 4]]
                        pUs = pU[:, ((idx // 4) % 4) * 128:((idx // 4) % 4 + 1) * 128]
                        mm(pUs, KDt[psl, blk, :], Vb[psl, blk, :], r=[r_KDt, r_V], w=[r_pU])
                        cur, r_cur = Sf[idx % 2]
                        nxt, r_nxt = Sf[(idx + 1) % 2]
                        stt(nxt[:], cur[:], dec[:, n:n + 1], pUs, ALU.mult, ALU.add, r=[r_cur, r_dec, r_pU], w=[r_nxt])
                        act(Sall[:, order[idx + 1], :], nxt[:], AF.Copy, r=[r_nxt], w=[r_Sall])
                        if idx % 2 == 1 and n_a < 32:
                            a_part(n_a)
                            n_a += 1
                    while n_a < 32:
                        a_part(n_a)
                        n_a += 1
                    for blk in range(2, NB):
                        i = blk - 2
                        pO, r_pO = PS[4 + i % 2]
                        mm(pO[:, 0:128], Vb[:, blk, :], Am_all[:, i, :], start=True, stop=False, r=[r_V, r_Amb[i]], w=[r_pO],
                           sgc=True)
                        mm(pO[:, 0:64], Sall[:, 2 * blk, :], QD[:, blk * 128:blk * 128 + 64], start=False, stop=False,
                           r=[r_Sall, r_QD], w=[r_pO], sgc=True)
                        mm(pO[:, 64:128], Sall[:, 2 * blk + 1, :], QD[:, blk * 128 + 64:blk * 128 + 128], start=False,
                           stop=True, r=[r_Sall, r_QD], w=[r_pO], sgc=True)
                        if dr == 0:
                            act(oT[:, i * 128:(i + 1) * 128], pO[:, 0:128], AF.Copy, r=[r_pO], w=[r_oT])
                        else:
                            tt(oT[:, i * 128:(i + 1) * 128], oT[:, i * 128:(i + 1) * 128], pO[:, 0:128], ALU.add,
                               r=[r_pO, r_oT], w=[r_oT])
                    if debug and dr == 0:
                        r_ofd = S.res("ofd")
                        dma(of_dbg[h], oT[:], r=[r_oT], w=[r_ofd], sem_res=r_oT)
                for t in range(8):
                    tsl = slice(t * 512, (t + 1) * 512)
                    pN, r_pN = PS[6]
                    tt(sq[:], oT[:, tsl], oT[:, tsl], ALU.mult, r=[r_oT], w=[r_sq])
                    mm(pN[:, :], onesf[:], sq[:], r=[r_onesf, r_sq], w=[r_pN])
                    act(rs[:], pN[:, :], AF.Ln, r=[r_pN], w=[r_rs], scale=1.0 / 128, bias=epsc[:, 0:1])
                    act(rs[:], rs[:], AF.Exp, r=[r_rs], w=[r_rs], scale=-0.5)
                    tt(t1[:], oT[:, tsl], rs[:], ALU.mult, r=[r_oT, r_rs], w=[r_t1])
                    stt(ogst[:, tsl], t1[:], hgn[:, h:h + 1], sgg[:, CTX + t * 512:CTX + (t + 1) * 512], ALU.mult, ALU.mult,
                        r=[r_t1, r_hgn, r_sgg], w=[r_ogst])
                dma(og_s[h], ogst[:, 0:L], r=[r_ogst], w=[], sem_res=r_ogst)
            S.phase_end()
        if stop_after == "H":
            return nc, S

        with ExitStack() as es:
            def sb(name, shape, dt):
                return es.enter_context(nc.sbuf_tensor("s_" + name, shape, dt)), S.res(name)
            ftT, r_ftT = sb("ftT", [128, 4, L], BF16)
            csw, r_csw = sb("csw", [128, 256], BF16)
            Pc, r_Pc = sb("Pc", [128, 32, 512], BF16)
            Psn, r_Psn = sb("Psn", [128, 32, 512], BF16)
            CLs = [sb("CLs%d" % i, [128, 32, 128], BF16) for i in range(2)]
            SLs = [sb("SLs%d" % i, [128, 32, 128], BF16) for i in range(2)]
            Ytm = [sb("Ytm%d" % i, [128, 512], BF16) for i in range(2)]
            YT, r_YT = sb("YT", [128, 4, L], BF16)
            dma(csw[:], csw_d, w=[r_csw])
            for g in range(4):
                dma(ftT[:, g, :], zft_s[g][:, CTX:NT], w=[r_ftT])
            for lb in range(32):
                pa, r_pa = PS[(lb % 2) * 2]
                pb, r_pb = PS[(lb % 2) * 2 + 1]
                for g in range(4):
                    p_, r_p = (pa, r_pa) if g < 2 else (pb, r_pb)
                    mm(p_[:, (g % 2) * 256:(g % 2 + 1) * 256], ftT[:, g, lb * 128:(lb + 1) * 128], csw[:],
                       r=[r_ftT, r_csw], w=[r_p])
                for (p_, r_p, g0) in ((pa, r_pa, 0), (pb, r_pb, 2)):
                    pv = p_[:, :].rearrange("p (g t m) -> p g t m", g=2, t=2)
                    act(Pc[:, lb, g0 * 128:(g0 + 2) * 128].rearrange("p (g m) -> p g m", g=2), pv[:, :, 0, :], AF.Copy,
                        r=[r_p], w=[r_Pc])
                    ts(Psn[:, lb, g0 * 128:(g0 + 2) * 128].rearrange("p (g m) -> p g m", g=2), pv[:, :, 1, :], -1.0,
                       ALU.mult, r=[r_p], w=[r_Psn])
            r_ytd = S.res("ytd")
            for kb in range(32):
                cl, r_cl = CLs[kb % 2]
                sl, r_sl = SLs[kb % 2]
                dma(cl[:], cl_d[kb], w=[r_cl])
                dma(sl[:], sl_d[kb], w=[r_sl])
                pY, r_pY = PS[4 + kb % 2]
                for lb in range(32):
                    mm(pY[:, :], cl[:, lb, :], Pc[:, lb, :], start=(lb == 0), stop=False, r=[r_cl, r_Pc], w=[r_pY])
                for lb in range(32):
                    mm(pY[:, :], sl[:, lb, :], Psn[:, lb, :], start=False, stop=(lb == 31), r=[r_sl, r_Psn], w=[r_pY])
                ytm, r_ytm = Ytm[kb % 2]
                act(ytm[:], pY[:, :], AF.Copy, r=[r_pY], w=[r_ytm], scale=float((4096.0 * 128.0) ** -0.5))
                for g in range(4):
                    tr(ptr[:, g * 128:(g + 1) * 128], ytm[:, g * 128:(g + 1) * 128], identb[:], r=[r_ytm, r_identb],
                       w=[r_ptr])
                cp(YT[:, :, kb * 128:(kb + 1) * 128], ptr[:, 0:512].rearrange("p (g t) -> p g t", g=4), r=[r_ptr],
                   w=[r_YT])
            for g in range(4):
                dma(yT_s[g], YT[:, g, :], r=[r_YT], w=[], sem_res=r_YT)
            for _ in gT:
                pass
            for r_ in r_tbg:
                r_.bg = False
            S.phase_end()
        if stop_after == "F":
            return nc, S

        with ExitStack() as es:
            def sb(name, shape, dt):
                return es.enter_context(nc.sbuf_tensor("s_" + name, shape, dt)), S.res(name)
            rows, r_rows = sb("rowsM", [128, 4, D], F32)
            dma(rows[:].rearrange("p a d -> p (a d)"), rows_d.rearrange("a b -> (a b)").partition_broadcast(128), w=[r_rows])
            stg, r_stg = sb("stg", [128, 8, D], F32)
            whg, r_whg = sb("whg", [128, 4, D], BF16)
            wft, r_wft = sb("wft", [128, 4, D], BF16)
            wo, r_wo = sb("wo", [128, 8, D], BF16)
            dma(stg[:, 0:4, :], whg_d.rearrange("(h p) n -> p h n", p=128), w=[r_stg])
            cp(whg[:], stg[:, 0:4, :], r=[r_stg], w=[r_whg], eng="pool")
            dma(stg[:, 0:4, :], wft_d.rearrange("(h p) n -> p h n", p=128), w=[r_stg])
            cp(wft[:], stg[:, 0:4, :], r=[r_stg], w=[r_wft], eng="pool")
            dma(stg[:], wout_d.rearrange("(h p) n -> p h n", p=128), w=[r_stg])
            cp(wo[:], stg[:], r=[r_stg], w=[r_wo], eng="pool")
            ogs = [sb("og%d" % i, [128, 4, 512], BF16) for i in range(2)]
            yts = [sb("yt%d" % i, [128, 4, 512], BF16) for i in range(2)]
            sghs = [sb("sgh%d" % i, [128, 8, 512], BF16) for i in range(2)]
            sgfs = [sb("sgf%d" % i, [128, 8, 512], BF16) for i in range(2)]
            yT2, r_yT2 = sb("yT2", [128, 8, 512], BF16)
            ta, r_ta = sb("ta", [128, 512], F32)
            tb, r_tb = sb("tb", [128, 512], F32)
            xms = [sb("xm%d" % i, [128, D], F32) for i in range(2)]
            h1s = [sb("h1_%d" % i, [128, D], F32) for i in range(2)]
            r_h1d = S.res("h1d")
            def load_m(t):
                dma(ogs[t % 2][0][:], og_s.rearrange("h p t -> p h t")[:, :, t * 512:(t + 1) * 512], w=[ogs[t % 2][1]])
                dma(yts[t % 2][0][:], yT_s.rearrange("h p t -> p h t")[:, :, t * 512:(t + 1) * 512], w=[yts[t % 2][1]])
                dma(sghs[t % 2][0][:], zgh_s[0:8].rearrange("h p t -> p h t")[:, :, CTX + t * 512:CTX + (t + 1) * 512],
                    w=[sghs[t % 2][1]])
                dma(sgfs[t % 2][0][:], zgh_s[8:16].rearrange("h p t -> p h t")[:, :, CTX + t * 512:CTX + (t + 1) * 512],
                    w=[sgfs[t % 2][1]])
            load_m(0)
            for t in range(8):
                og, r_og = ogs[t % 2]
                yt, r_yt = yts[t % 2]
                sgh, r_sgh = sghs[t % 2]
                sgf, r_sgf = sgfs[t % 2]
                if t + 1 < 8:
                    load_m(t + 1)
                for db in range(8):
                    pH, r_pH = PS[db % 2]
                    pF, r_pF = PS[2 + db % 2]
                    for h in range(4):
                        mm(pH[:, :], whg[:, h, db * 128:(db + 1) * 128], og[:, h, :], start=(h == 0), stop=(h == 3),
                           r=[r_whg, r_og], w=[r_pH])
                    for g in range(4):
                        mm(pF[:, :], wft[:, g, db * 128:(db + 1) * 128], yt[:, g, :], start=(g == 0), stop=(g == 3),
                           r=[r_wft, r_yt], w=[r_pF])
                    tt(ta[:], pH[:, :], sgh[:, db, :], ALU.mult, r=[r_pH, r_sgh], w=[r_ta])
                    tt(tb[:], pF[:, :], sgf[:, db, :], ALU.mult, r=[r_pF, r_sgf], w=[r_tb])
                    tt(yT2[:, db, :], ta[:], tb[:], ALU.add, r=[r_ta, r_tb], w=[r_yT2], eng="pool")
                for sub in range(4):
                    tb_ = t * 4 + sub
                    xm, r_xm = xms[tb_ % 2]
                    h1, r_h1 = h1s[tb_ % 2]
                    if tb_ == 0:
                        dma(xm[:], x_d[0:128, :], w=[r_xm])
                    if tb_ + 1 < 32:
                        dma(xms[(tb_ + 1) % 2][0][:], x_d[(tb_ + 1) * 128:(tb_ + 2) * 128, :], w=[xms[(tb_ + 1) % 2][1]])
                    for half in range(2):
                        pM, r_pM = PS[4 + half]
                        for db in range(8):
                            mm(pM[:, :], yT2[:, db, sub * 128:(sub + 1) * 128], wo[:, db, half * 512:(half + 1) * 512],
                               start=(db == 0), stop=(db == 7), r=[r_yT2, r_wo], w=[r_pM])
                        hs = slice(half * 512, (half + 1) * 512)
                        tt(ta[:], pM[:, :], rows[:, 0, hs], ALU.mult, r=[r_pM, r_rows], w=[r_ta])
                        tt(h1[:, hs], ta[:], xm[:, hs], ALU.add, r=[r_ta, r_xm], w=[r_h1])
                    dma(h1_s[tb_ * 128:(tb_ + 1) * 128, :], h1[:], r=[r_h1], w=[], sem_res=r_h1)
            S.phase_end()
        if stop_after == "M":
            return nc, S

        eq = eg.enter_context(ExitStack())
        wq, r_wq = eq.enter_context(nc.sbuf_tensor("s_wq", [128, 8, 2048], BF16)), S.res("wq")
        skb, r_skb = eq.enter_context(nc.sbuf_tensor("s_skb", [128, 16, 128], BF16)), S.res("skb")
        with ExitStack() as es:
            def sb(name, shape, dt):
                return es.enter_context(nc.sbuf_tensor("s_" + name, shape, dt)), S.res(name)
            stgs = [sb("stgq%d" % i, [128, 8, 512], F32) for i in range(2)]
            wqv = wq_d.rearrange("(kc p) n -> p kc n", p=128)
            for c4 in range(4):
                stg, r_stg = stgs[c4 % 2]
                dma(stg[:], wqv[:, :, c4 * 512:(c4 + 1) * 512], w=[r_stg])
                cp(wq[:, :, c4 * 512:(c4 + 1) * 512], stg[:], r=[r_stg], w=[r_wq], eng=("pool" if c4 % 2 else "dve"))
            stg, r_stg = stgs[0]
            dma(stg[:, 0:4, :].rearrange("p a (b n) -> p (a b) n", n=128), skT_d, w=[r_stg])
            cp(skb[:], stg[:, 0:4, :].rearrange("p a (b n) -> p (a b) n", n=128), r=[r_stg], w=[r_skb])
            S.phase_end()

        with ExitStack() as es:
            def sb(name, shape, dt):
                return es.enter_context(nc.sbuf_tensor("s_" + name, shape, dt)), S.res(name)
            rows, r_rows = sb("rowsQ", [128, 4, D], F32)
            dma(rows[:].rearrange("p a d -> p (a d)"), rows_d.rearrange("a b -> (a b)").partition_broadcast(128), w=[r_rows])
            fngr, r_fngr = sb("fngr", [128, D], F32)
            dma(fngr[:], fng_d.partition_broadcast(128), w=[r_fngr])
            io16, r_io16 = sb("io16", [128, 16], F32)
            dma(io16[:], iota_d, w=[r_io16])
            hbs = [sb("hb%d" % i, [128, D], F32) for i in range(2)]
            u2f, r_u2f = sb("u2f", [128, D], F32)
            u2bs = [sb("u2b%d" % i, [128, D], BF16) for i in range(2)]
            eidxs = [sb("eidx%d" % i, [128, 128], I32) for i in range(2)]
            gates = [sb("gate%d" % i, [128, 128], F32) for i in range(2)]
            junkq, r_junkq = sb("junkq", [128, D], BF16)
            st4q, r_st4q = sb("st4q", [128, 4], F32)
            u2T, r_u2T = sb("u2T", [128, 8, 128], BF16)
            qT, r_qT = sb("qT", [128, 16, 128], BF16)
            ssb, r_ssb = sb("ssb", [128, 16, 128], F32)
            swk, _ = sb("swk", [128, 8, 128], F32)
            sv, _ = sb("sv", [128, 16, 16], F32)
            si, _ = sb("si", [128, 16, 16], U32)
            sif, r_sif = sb("sif", [128, 16, 16], F32)
            cand, _ = sb("cand", [128, 8, 256], F32)
            cwk, _ = sb("cwk", [128, 4, 256], F32)
            cv, _ = sb("cv", [128, 8, 16], F32)
            ci, _ = sb("ci", [128, 8, 16], U32)
            cih, r_cih = sb("cih", [128, 8, 16], U32)
            cil, r_cil = sb("cil", [128, 8, 16], U32)
            cif, r_cif = sb("cif", [128, 2, 128], F32)
            oh, r_oh = sb("oh", [128, 64, 16], F32)
            ef, r_ef = sb("ef", [128, 2, 128], F32)
            ex, r_ex = sb("ex", [128, 8, 16], F32)
            sm, r_sm = sb("sm", [128, 8], F32)
            r_svh = [S.res("svh%d" % i) for i in range(16)]
            r_sih = [S.res("sih%d" % i) for i in range(16)]
            r_swkh = [S.res("swkh%d" % i) for i in range(8)]
            r_candh = [S.res("candh%d" % i) for i in range(8)]
            r_cvh = [S.res("cvh%d" % i) for i in range(8)]
            r_cih_ = [S.res("cih_%d" % i) for i in range(8)]
            r_cwkh = [S.res("cwkh%d" % i) for i in range(4)]
            NR = 18
            GS = 4
            NG = 128 // GS
            uvg = [sb("uvg%d" % i, [128, 2 * D], BF16) for i in range(NR)]
            diag = [sb("diag%d" % i, [128, GS, 128], BF16) for i in range(2)]
            junkg, _ = sb("junkg", [128, D], BF16)
            apre = [sb("apre%d" % i, [128, 128], F32) for i in range(2)]
            wts = [sb("wts%d" % i, [128, 128], F32) for i in range(2)]
            r_aps = [[S.res("aps%d_%d" % (i, j)) for j in range(128)] for i in range(2)]
            r_wtg = [[S.res("wtg%d_%d" % (i, g)) for g in range(NG)] for i in range(2)]
            h2, r_h2 = sb("h2", [128, D], F32)
            st4, r_st4 = sb("st4g", [128, 4], F32)
            outs = [sb("outs%d" % i, [128, D], F32) for i in range(1)]
            gstate = {"gcount": 0}

            def vop(fn, r, w):
                S.op("dve", fn, r, w)

            def gen_Q(tb_):
                hb, r_hb = hbs[tb_ % 2]
                u2b, r_u2b = u2bs[tb_ % 2]
                eidx, r_eidx = eidxs[tb_ % 2]
                gate, r_gate = gates[tb_ % 2]
                dma(hb[:], h1_s[tb_ * 128:(tb_ + 1) * 128, :], w=[r_hb])
                yield
                act(junkq[:], hb[:], AF.Square, r=[r_hb], w=[r_st4q], accum_out=st4q[:, 0:1])
                act(st4q[:, 1:2], st4q[:, 0:1], AF.Sqrt, r=[r_st4q], w=[r_st4q], scale=1.0 / D, bias=EPS)
                yield
                vop(lambda e: e.reciprocal(out=st4q[:, 2:3], in_=st4q[:, 1:2]), [r_st4q], [r_st4q])
                stt(u2f[:], hb[:], st4q[:, 2:3], rows[:, 1, :], ALU.mult, ALU.mult, r=[r_hb, r_st4q, r_rows], w=[r_u2f])
                tt(u2b[:], u2f[:], rows[:, 2, :], ALU.add, r=[r_u2f, r_rows], w=[r_u2b])
                yield
                for kc in range(8):
                    tr(ptr[:, kc * 128:(kc + 1) * 128], u2b[:, kc * 128:(kc + 1) * 128], identb[:],
                       r=[r_u2b, r_identb], w=[r_ptr])
                yield
                act(u2T[:], ptr[:].rearrange("p (k t) -> p k t", t=128), AF.Copy, r=[r_ptr], w=[r_u2T])
                yield

                def qmm(q4):
                    pQ, r_pQ = PS[2 + q4 % 2]
                    for k4 in range(4):
                        hp = q4 * 4 + k4
                        for kc in range(8):
                            mm(pQ[:, k4 * 128:(k4 + 1) * 128], wq[:, kc, hp * 128:(hp + 1) * 128], u2T[:, kc, :],
                               start=(kc == 0), stop=(kc == 7), r=[r_wq, r_u2T], w=[r_pQ])

                def qev(q4):
                    pQ, r_pQ = PS[2 + q4 % 2]
                    act(qT[:, q4 * 4:(q4 + 1) * 4, :], pQ[:, :].rearrange("p (a t) -> p a t", t=128), AF.Copy,
                        r=[r_pQ], w=[r_qT])

                def smm(b4):
                    pS, r_pS = PS[2 + b4 % 2]
                    for k4 in range(4):
                        hp = b4 * 4 + k4
                        mm(pS[:, k4 * 128:(k4 + 1) * 128], qT[:, hp, :], skb[:, hp, :], r=[r_qT, r_skb], w=[r_pS])

                def sev(b4):
                    pS, r_pS = PS[2 + b4 % 2]
                    act(ssb[:, b4 * 4:(b4 + 1) * 4, :], pS[:, :].rearrange("p (a n) -> p a n", n=128), AF.Copy,
                        r=[r_pS], w=[r_ssb])

                qmm(0)
                yield
                qmm(1)
                qev(0)
                yield
                qmm(2)
                qev(1)
                yield
                qmm(3)
                qev(2)
                yield
                qev(3)
                yield
                smm(0)
                yield
                smm(1)
                sev(0)
                yield
                smm(2)
                sev(1)
                yield
                smm(3)
                sev(2)
                yield
                sev(3)
                yield
                for hh in range(2):
                    L8 = list(range(hh * 8, hh * 8 + 8))
                    for hp in L8:
                        vop(lambda e, hp=hp: e.max(out=sv[:, hp, 0:8], in_=ssb[:, hp, :]), [r_ssb], [r_svh[hp]])
                    for hp in L8:
                        vop(lambda e, hp=hp: e.max_index(out=si[:, hp, 0:8], in_max=sv[:, hp, 0:8], in_values=ssb[:, hp, :]),
                            [r_ssb, r_svh[hp]], [r_sih[hp]])
                    for hp in L8:
                        vop(lambda e, hp=hp: e.match_replace(out=swk[:, hp % 8, :], in_to_replace=sv[:, hp, 0:8],
                                                             in_values=ssb[:, hp, :], imm_value=-1e30),
                            [r_ssb, r_svh[hp]], [r_swkh[hp % 8]])
                    yield
                    for hp in L8:
                        vop(lambda e, hp=hp: e.max(out=sv[:, hp, 8:16], in_=swk[:, hp % 8, :]), [r_swkh[hp % 8]], [r_svh[hp]])
                    for hp in L8:
                        vop(lambda e, hp=hp: e.max_index(out=si[:, hp, 8:16], in_max=sv[:, hp, 8:16], in_values=swk[:, hp % 8, :]),
                            [r_swkh[hp % 8], r_svh[hp]], [r_sih[hp]])
                    yield
                cp(sif[:], si[:], r=r_sih, w=[r_sif])
                sv4 = sv[:].rearrange("p (h a) k -> p h a k", a=2)
                for h in range(8):
                    tt(cand[:, h, :].rearrange("p (i j) -> p i j", j=16),
                       sv4[:, h, 0, :].unsqueeze(2).broadcast_to([128, 16, 16]),
                       sv4[:, h, 1, :].unsqueeze(1).broadcast_to([128, 16, 16]), ALU.add,
                       r=[r_svh[2 * h], r_svh[2 * h + 1]], w=[r_candh[h]])
                yield
                for hh in range(2):
                    H4 = list(range(hh * 4, hh * 4 + 4))
                    for h in H4:
                        vop(lambda e, h=h: e.max(out=cv[:, h, 0:8], in_=cand[:, h, :]), [r_candh[h]], [r_cvh[h]])
                    for h in H4:
                        vop(lambda e, h=h: e.max_index(out=ci[:, h, 0:8], in_max=cv[:, h, 0:8], in_values=cand[:, h, :]),
                            [r_candh[h], r_cvh[h]], [r_cih_[h]])
                    for h in H4:
                        vop(lambda e, h=h: e.match_replace(out=cwk[:, h % 4, :], in_to_replace=cv[:, h, 0:8],
                                                           in_values=cand[:, h, :], imm_value=-1e30),
                            [r_candh[h], r_cvh[h]], [r_cwkh[h % 4]])
                    for h in H4:
                        vop(lambda e, h=h: e.max(out=cv[:, h, 8:16], in_=cwk[:, h % 4, :]), [r_cwkh[h % 4]], [r_cvh[h]])
                    for h in H4:
                        vop(lambda e, h=h: e.max_index(out=ci[:, h, 8:16], in_max=cv[:, h, 8:16], in_values=cwk[:, h % 4, :]),
                            [r_cwkh[h % 4], r_cvh[h]], [r_cih_[h]])
                    yield
                vop(lambda e: e.tensor_single_scalar(out=cih[:], in_=ci[:], scalar=4, op=ALU.logical_shift_right),
                    r_cih_, [r_cih])
                vop(lambda e: e.tensor_single_scalar(out=cil[:], in_=ci[:], scalar=15, op=ALU.bitwise_and),
                    r_cih_, [r_cil])
                cp(cif[:, 0, :], cih[:].rearrange("p h k -> p (h k)"), r=[r_cih], w=[r_cif])
                cp(cif[:, 1, :], cil[:].rearrange("p h k -> p (h k)"), r=[r_cil], w=[r_cif])
                tt(ex[:], cv[:], cv[:, :, 0:1].broadcast_to([128, 8, 16]), ALU.subtract, r=r_cvh, w=[r_ex])
                act(ex[:], ex[:], AF.Exp, r=[r_ex], w=[r_ex])
                yield
                sif4 = sif[:].rearrange("p (h a) k -> p h a k", a=2)
                for a in range(2):
                    for hh in range(2):
                        hs_ = slice(hh * 64, hh * 64 + 64)
                        tt(oh[:], cif[:, a, hs_].unsqueeze(2).broadcast_to([128, 64, 16]),
                           io16[:].unsqueeze(1).broadcast_to([128, 64, 16]), ALU.is_equal,
                           r=[r_cif, r_io16], w=[r_oh])
                        tt(oh[:].rearrange("p (h k) i -> p h k i", h=4), oh[:].rearrange("p (h k) i -> p h k i", h=4),
                           sif4[:, hh * 4:hh * 4 + 4, a, :].unsqueeze(2).broadcast_to([128, 4, 16, 16]), ALU.mult,
                           r=[r_oh, r_sif], w=[r_oh])
                        vop(lambda e, a=a, hs_=hs_: e.tensor_reduce(out=ef[:, a, hs_], in_=oh[:], axis=AX.X, op=ALU.add),
                            [r_oh], [r_ef])
                        yield
                stt(ef[:, 0, :], ef[:, 0, :], 128.0, ef[:, 1, :], ALU.mult, ALU.add, r=[r_ef], w=[r_ef])
                cp(eidx[:], ef[:, 0, :], r=[r_ef], w=[r_eidx])
                vop(lambda e: e.tensor_reduce(out=sm[:], in_=ex[:], axis=AX.X, op=ALU.add), [r_ex], [r_sm])
                vop(lambda e: e.reciprocal(out=sm[:], in_=sm[:]), [r_sm], [r_sm])
                tt(gate[:].rearrange("p (h k) -> p h k", h=8), ex[:],
                   sm[:].unsqueeze(2).broadcast_to([128, 8, 16]), ALU.mult, r=[r_ex, r_sm], w=[r_gate])
                yield

            def gen_G(tb_):
                hb, r_hb = hbs[tb_ % 2]
                u2b, r_u2b = u2bs[tb_ % 2]
                eidx, r_eidx = eidxs[tb_ % 2]
                gate, r_gate = gates[tb_ % 2]
                ap_ = apre[tb_ % 2][0]
                wt = wts[tb_ % 2][0]
                pa, r_pa = PS[0 if tb_ % 2 == 0 else 4]
                pb, r_pb = PS[1 if tb_ % 2 == 0 else 5]
                def stage_a(g):
                    bufs = []
                    for jj in range(GS):
                        j = g * GS + jj
                        gc = gstate["gcount"]
                        gstate["gcount"] += 1
                        ug, r_ug = uvg[gc % NR]
                        bufs.append((ug, r_ug))
                        S.dma("pool", lambda e, ug=ug, eidx=eidx, j=j: e.indirect_dma_start(
                            out=ug[:], out_offset=None, in_=uvb_s,
                            in_offset=bass.IndirectOffsetOnAxis(ap=eidx[:, j:j + 1], axis=0)),
                            [r_eidx], [r_ug])
                        stt(junkg[:], ug[:, 0:D], 1.0, u2b[:], ALU.mult, ALU.mult, r=[r_ug, r_u2b],
                            w=[r_aps[tb_ % 2][j]], accum_out=ap_[:, j:j + 1])
                    return bufs

                def stage_gelu(g):
                    gsl = slice(g * GS, (g + 1) * GS)
                    act(wt[:, gsl], ap_[:, gsl], AF.Gelu, r=r_aps[tb_ % 2][g * GS:(g + 1) * GS], w=[r_wtg[tb_ % 2][g]])

                def stage_mult(g):
                    r_wt = r_wtg[tb_ % 2][g]
                    for jj in range(GS):
                        j = g * GS + jj
                        act(wt[:, j:j + 1], wt[:, j:j + 1], AF.Copy, r=[r_wt, r_gate], w=[r_wt], scale=gate[:, j:j + 1])

                def stage_c(g, bufs):
                    r_wt = r_wtg[tb_ % 2][g]
                    dg, r_dg = diag[g % 2]
                    for jj in range(GS):
                        j = g * GS + jj
                        act(dg[:, jj, :], identb[:], AF.Copy, r=[r_identb, r_wt], w=[r_dg], scale=wt[:, j:j + 1])
                    for jj in range(GS):
                        j = g * GS + jj
                        ug, r_ug = bufs[jj]
                        mm(pa[:, :], dg[:, jj, :], ug[:, D:D + 512], start=(j == 0), stop=(j == 127), r=[r_dg, r_ug], w=[r_pa])
                        mm(pb[:, :], dg[:, jj, :], ug[:, D + 512:2 * D], start=(j == 0), stop=(j == 127), r=[r_dg, r_ug], w=[r_pb])

                allb = {}
                prev_tb = pend["tb"]
                for g in range(NG + 1):
                    if prev_tb is not None and g == 2:
                        block_end_1(prev_tb)
                    if prev_tb is not None and g == 4:
                        block_end_2(prev_tb)
                    if g < NG:
                        allb[g] = stage_a(g)
                    if g >= 1:
                        stage_mult(g - 1)
                        stage_c(g - 1, allb.pop(g - 1))
                    if g < NG:
                        stage_gelu(g)
                    yield
                pend["tb"] = tb_

            def block_end_1(tb_):
                hb, r_hb = hbs[tb_ % 2]
                pa, r_pa = PS[0 if tb_ % 2 == 0 else 4]
                pb, r_pb = PS[1 if tb_ % 2 == 0 else 5]
                tt(h2[:, 0:512], pa[:, :], rows[:, 3, 0:512], ALU.mult, r=[r_pa, r_rows], w=[r_h2])
                tt(h2[:, 512:1024], pb[:, :], rows[:, 3, 512:1024], ALU.mult, r=[r_pb, r_rows], w=[r_h2])
                tt(h2[:], h2[:], hb[:], ALU.add, r=[r_h2, r_hb], w=[r_h2])
                act(junkq[:], h2[:], AF.Square, r=[r_h2], w=[r_st4], accum_out=st4[:, 0:1])
                act(st4[:, 1:2], st4[:, 0:1], AF.Sqrt, r=[r_st4], w=[r_st4], scale=1.0 / D, bias=EPS)

            def block_end_2(tb_):
                vop(lambda e: e.reciprocal(out=st4[:, 2:3], in_=st4[:, 1:2]), [r_st4], [r_st4])
                ot, r_ot = outs[0]
                stt(ot[:], h2[:], st4[:, 2:3], fngr[:], ALU.mult, ALU.mult, r=[r_h2, r_st4, r_fngr], w=[r_ot])
                dma(out_d[tb_ * 128:(tb_ + 1) * 128, :], ot[:], r=[r_ot], w=[], sem_res=r_ot)

            pend = {"tb": None}
            for _ in gen_Q(0):
                pass
            for tb_ in range(32):
                gq = gen_Q(tb_ + 1) if tb_ < 31 else None
                for gi, _ in enumerate(gen_G(tb_)):
                    if gq is not None and gi >= 2:
                        next(gq, None)
                        if gi < 5:
                            next(gq, None)
                if gq is not None:
                    for _ in gq:
                        pass
            block_end_1(pend["tb"])
            block_end_2(pend["tb"])
            S.phase_end()
    return nc, S


_CONST = {}


def _consts():
    if _CONST:
        return _CONST
    bf = ml_dtypes.bfloat16
    n = np.arange(128)
    ang = 2.0 * np.pi * ((n[:, None] * n[None, :]) % 128) / 128.0
    _CONST["csw"] = np.concatenate([np.cos(ang), np.sin(ang)], axis=1).astype(bf)
    l = np.arange(4096, dtype=np.int64)
    angL = 2.0 * np.pi * ((l[:, None] * l[None, :]) % 4096) / 4096.0
    cl = np.cos(angL).astype(np.float32)
    sl = np.sin(angL).astype(np.float32)
    lay = lambda m: np.ascontiguousarray(m.reshape(32, 128, 32, 128).transpose(2, 1, 0, 3)).astype(bf)
    _CONST["cl"] = lay(cl)
    _CONST["sl"] = lay(sl)
    _CONST["identb"] = np.eye(128, dtype=np.float32).astype(bf)
    _CONST["identf"] = np.eye(128, dtype=np.float32)
    j = n[:, None]
    i = n[None, :]
    same = (j // 64) == (i // 64)
    mf = (same & (j <= i)).astype(np.float32)
    mb = (same & (j >= i)).astype(np.float32)
    _CONST["masks"] = np.ascontiguousarray(np.stack([mf, mb], axis=1))
    _CONST["iota16"] = np.tile(np.arange(16, dtype=np.float32)[None, :], (128, 1))
    rm = np.ones((128, NT), np.float32)
    rm[:, 0::64] = 0.0
    _CONST["rmask"] = rm.astype(bf)
    return _CONST


def make_in_maps(x, c, ctx, c_ctx, w_ada, b_ada, norm_mix_g, norm_ffn_g, w_in, hg_lb_f, hg_lb_b, hg_norm_g,
                 w_hg_out, w_ft_out, w_out, peer_w_q, peer_sub_keys, peer_u, peer_v, final_norm_g):
    f = lambda a: np.ascontiguousarray(np.asarray(a, dtype=np.float32))
    C = _consts()
    pl = lambda v, k: f(np.asarray(v).reshape(k, 128).T)
    shared = {
        "w_ada": f(w_ada[0]), "b_ada_p": pl(b_ada[0], 48), "gmix_p": pl(norm_mix_g[0], 8),
        "gffn_p": pl(norm_ffn_g[0], 8), "w_in": f(w_in[0]),
        "lbf": f(np.asarray(hg_lb_f).reshape(2, 4, 128).transpose(2, 1, 0)),
        "lbb": f(np.asarray(hg_lb_b).reshape(2, 4, 128).transpose(2, 1, 0)),
        "hgn_p": f(np.asarray(hg_norm_g[0]).T), "w_hg_out": f(w_hg_out[0]), "w_ft_out": f(w_ft_out[0]),
        "w_out": f(w_out[0]), "w_q": f(peer_w_q[0]),
        "skT": f(np.asarray(peer_sub_keys[0]).reshape(16, 128, 128).transpose(2, 0, 1)),
        "peer_u": f(peer_u[0]), "peer_v": f(peer_v[0]), "fng": f(final_norm_g),
    }
    shared.update(C)
    maps = []
    c = np.asarray(c)
    c_ctx = np.asarray(c_ctx)
    for b in range(8):
        m = dict(shared)
        m["x"] = f(x[b])
        m["ctx"] = f(ctx[b])
        m["cc"] = f(np.stack([c[b].reshape(8, 128).T, c_ctx.reshape(8, 128).T], axis=2))
        maps.append(m)
    return maps


_NC = None


def kernel(**inputs):
    global _NC
    if _NC is None:
        _NC = build()[0]
    maps = make_in_maps(**inputs)
    res = run_bass_kernel_spmd(_NC, maps, core_ids=list(range(8)))
    return np.stack([np.asarray(r["out"], dtype=np.float32) for r in res.results], axis=0)
```
